# Optimizing a Trainium2 kernel written in Bass

```python
import jax, jax.numpy as jnp
from jax import lax
import numpy as np

D_MODEL = 2048
BATCH = 2
SEQ = 4096
DEPTH = 4
DEC_BATCH = 2
DEC_SEQ = 8192
PAST_LEN = 128

N_META = 16
GRID_W = 64
CHUNK = 64
CONV_K = 5
EPS = 1e-6
NEG_GATE = -1e30

D_A = D_MODEL
NH_A = 8
DV_A = D_A // NH_A
DK_A = DV_A // 2
D_B = D_MODEL
HD_B = 64
NH_B = D_B // HD_B
NG_B = 8
DS_B = 128
D_C = D_MODEL
HD_C = 128
NH_C = D_C // HD_C
WIN_H = 8
WIN_W = 16
D_D = D_MODEL
HD_D = 128
NH_D = D_D // HD_D

N_EVEN = (DEPTH + 1) // 2
N_ODD = DEPTH // 2
EVEN_SIZES = (NH_A * DK_A, NH_A * DK_A, D_A, D_A, D_A, 4 * NH_A, D_B, D_B, NG_B * DS_B, NG_B * DS_B, 2 * NH_B)
ODD_SIZES = (D_C, D_C, D_C, D_C, D_D, D_D, D_D, D_D, 4 * NH_D)
E_IN = sum(EVEN_SIZES)
O_IN = sum(ODD_SIZES)

kernel_name = 'hybrid_bidir_encoder_mlstm_ssd_na_gdn'


def _split(t, sizes):
    idx, acc = [], 0
    for s in sizes[:-1]:
        acc += s
        idx.append(acc)
    return jnp.split(t, idx, axis=-1)


def _rms_norm(x, w):
    xf = x.astype(jnp.float32)
    y = xf * lax.rsqrt(jnp.mean(xf * xf, axis=-1, keepdims=True) + EPS)
    return (y * w.astype(jnp.float32)).astype(x.dtype)


def _l2n(x):
    return x * lax.rsqrt(jnp.sum(x * x, axis=-1, keepdims=True) + EPS)


def _conv_centred(x, w, b):
    pad = CONV_K // 2
    y = lax.conv_general_dilated(x, w[:, None, :].astype(x.dtype), window_strides=(1,), padding=[(pad, pad)],
                                 dimension_numbers=('NWC', 'WIO', 'NWC'), feature_group_count=x.shape[-1])
    return y + b.astype(x.dtype)


def _pad_front(t, n):
    return jnp.pad(t, [(0, 0), (n, 0)] + [(0, 0)] * (t.ndim - 2))


def _chunk_scan(step, init, seqs):
    B, L = seqs[0].shape[:2]
    nc = L // CHUNK
    xs = tuple(jnp.moveaxis(s.reshape((B, nc, CHUNK) + s.shape[2:]), 1, 0) for s in seqs)
    _, ys = lax.scan(step, init, xs)
    return jnp.moveaxis(ys, 0, 1).reshape((B, L) + ys.shape[3:])


def _bidir(scan_fn, fwd, bwd, n_pad):
    y_f = scan_fn(*fwd)
    y_b = jnp.flip(scan_fn(*(jnp.flip(t, 1) for t in bwd)), 1)
    return (y_f + y_b)[:, n_pad:]


def _mlstm_scan(q, k, v, log_i, log_f):
    B, L, H, DK = q.shape
    DV = v.shape[-1]
    causal = jnp.tril(jnp.ones((CHUNK, CHUNK), dtype=bool))

    def step(carry, inp):
        C, n, m = carry
        qb, kb, vb, ib, fb = inp
        b = jnp.swapaxes(jnp.cumsum(fb, axis=1), 1, 2)
        ih = jnp.swapaxes(ib, 1, 2)
        d = jnp.where(causal, b[..., :, None] - b[..., None, :] + ih[..., None, :], -jnp.inf)
        inter = b + m[..., None]
        m_t = jnp.maximum(inter, jnp.max(d, axis=-1))
        w_st = jnp.exp(inter - m_t)
        s = jnp.einsum('bthd,bshd->bhts', qb, kb) * jnp.exp(d - m_t[..., None])
        num = jnp.einsum('bhts,bshv->bhtv', s, vb) + w_st[..., None] * jnp.einsum('bthd,bhdv->bhtv', qb, C)
        den = jnp.sum(s, axis=-1) + w_st * jnp.einsum('bthd,bhd->bht', qb, n)
        h = num / jnp.maximum(jnp.abs(den), jnp.exp(-m_t))[..., None]
        b_end = b[..., -1]
        d_end = b_end[..., None] - b + ih
        m_new = jnp.maximum(b_end + m, jnp.max(d_end, axis=-1))
        w_s = jnp.exp(d_end - m_new[..., None])
        dec = jnp.exp(b_end + m - m_new)
        C = dec[..., None, None] * C + jnp.einsum('bhs,bshd,bshv->bhdv', w_s, kb, vb)
        n = dec[..., None] * n + jnp.einsum('bhs,bshd->bhd', w_s, kb)
        return (C, n, m_new), jnp.swapaxes(h, 1, 2)

    init = (jnp.zeros((B, H, DK, DV), jnp.float32), jnp.zeros((B, H, DK), jnp.float32), jnp.zeros((B, H), jnp.float32))
    return _chunk_scan(step, init, (q, k, v, log_i, log_f))


def _ssd_scan(x, dt, a, bm, cm):
    B, L, G, R, P = x.shape
    N = bm.shape[-1]
    causal = jnp.tril(jnp.ones((CHUNK, CHUNK), dtype=bool))

    def step(S, inp):
        xb, dtb, ab, bb, cc = inp
        acum = jnp.moveaxis(jnp.cumsum(ab, axis=1), 1, -1)
        decay = jnp.exp(jnp.where(causal, acum[..., :, None] - acum[..., None, :], -jnp.inf))
        xdt = xb * dtb[..., None]
        cb = jnp.einsum('btgn,bsgn->bgts', cc, bb)
        y = jnp.einsum('bgts,bgrts,bsgrp->btgrp', cb, decay, xdt)
        y = y + jnp.einsum('btgn,bgrpn->btgrp', cc, S) * jnp.moveaxis(jnp.exp(acum), -1, 1)[..., None]
        a_end = acum[..., -1]
        w_s = jnp.exp(a_end[..., None] - acum)
        S = jnp.exp(a_end)[..., None, None] * S + jnp.einsum('bgrs,bsgrp,bsgn->bgrpn', w_s, xdt, bb)
        return S, y

    return _chunk_scan(step, jnp.zeros((B, G, R, P, N), jnp.float32), (x, dt, a, bm, cm))


def _gdn_scan(q, k, v, beta, g):
    B, L, H, DK = q.shape
    DV = v.shape[-1]
    incl = jnp.tril(jnp.ones((CHUNK, CHUNK), dtype=bool))
    strict = jnp.tril(jnp.ones((CHUNK, CHUNK), dtype=bool), -1)
    eye = jnp.eye(CHUNK, dtype=jnp.float32)

    def step(S, inp):
        qb, kb, vb, bb, gb = inp
        G = jnp.swapaxes(jnp.cumsum(gb, axis=1), 1, 2)
        bh = jnp.swapaxes(bb, 1, 2)
        dec = jnp.exp(jnp.where(incl, G[..., :, None] - G[..., None, :], -jnp.inf))
        kk = jnp.einsum('bthd,bshd->bhts', kb, kb)
        lmat = jnp.where(strict, bh[..., :, None] * dec * kk, 0.0)
        vh = jnp.swapaxes(vb, 1, 2)
        kh = jnp.swapaxes(kb, 1, 2)
        rhs = jnp.concatenate([bh[..., None] * vh, (bh * jnp.exp(G))[..., None] * kh], axis=-1)
        sol = lax.linalg.triangular_solve(eye + lmat, rhs, left_side=True, lower=True, unit_diagonal=True)
        u = sol[..., :DV] - jnp.einsum('bhtd,bhdv->bhtv', sol[..., DV:], S)
        qk = jnp.einsum('bthd,bshd->bhts', qb, kb) * dec
        o = jnp.einsum('bhts,bhsv->bhtv', qk, u) + jnp.exp(G)[..., None] * jnp.einsum('bthd,bhdv->bhtv', qb, S)
        g_end = G[..., -1]
        S = jnp.exp(g_end)[..., None, None] * S + jnp.einsum('bhs,bhsd,bhsv->bhdv', jnp.exp(g_end[..., None] - G), kh, u)
        return S, jnp.swapaxes(o, 1, 2)

    return _chunk_scan(step, jnp.zeros((B, H, DK, DV), jnp.float32), (q, k, v, beta, g))


def _neighbourhood_attention(q, k, v, rpb):
    B, L, H, Dh = q.shape
    T = L - N_META
    rows = T // GRID_W
    kh = min(WIN_H, rows)
    scale = Dh ** -0.5
    qm, km, vm = q[:, :N_META], k[:, :N_META], v[:, :N_META]
    qg = q[:, N_META:].reshape(B, rows, GRID_W, H, Dh)
    kg = k[:, N_META:].reshape(B, rows, GRID_W, H, Dh)
    vg = v[:, N_META:].reshape(B, rows, GRID_W, H, Dh)
    s_mm = jnp.einsum('bqhd,bkhd->bhqk', qm, km).astype(jnp.float32) * scale
    o_meta = jnp.einsum('bhqk,bkhd->bqhd', jax.nn.softmax(s_mm, axis=-1).astype(v.dtype), vm)
    cols = jnp.arange(GRID_W)
    c0 = jnp.clip(cols - WIN_W // 2, 0, GRID_W - WIN_W)
    col_ok = (cols[None, :] >= c0[:, None]) & (cols[None, :] < c0[:, None] + WIN_W)
    dc_idx = jnp.clip(cols[None, :] - cols[:, None] + WIN_W - 1, 0, 2 * WIN_W - 2)
    col_bias = rpb[:, :, dc_idx]
    nk = kh * GRID_W

    def row_block(r):
        r0 = jnp.clip(r - kh // 2, 0, rows - kh)
        q_r = lax.dynamic_index_in_dim(qg, r, axis=1, keepdims=False)
        k_r = lax.dynamic_slice_in_dim(kg, r0, kh, axis=1)
        v_r = lax.dynamic_slice_in_dim(vg, r0, kh, axis=1)
        dr_idx = r0 + jnp.arange(kh) - r + WIN_H - 1
        bias = jnp.transpose(col_bias[:, dr_idx], (0, 2, 1, 3))
        s = jnp.einsum('bchd,bjxhd->bhcjx', q_r, k_r).astype(jnp.float32) * scale + bias
        s = jnp.where(col_ok[:, None, :], s, -jnp.inf).reshape(B, H, GRID_W, nk)
        s_meta = jnp.einsum('bchd,bkhd->bhck', q_r, km).astype(jnp.float32) * scale
        p = jax.nn.softmax(jnp.concatenate([s, s_meta], axis=-1), axis=-1).astype(v.dtype)
        o = jnp.einsum('bhcn,bnhd->bchd', p[..., :nk], v_r.reshape(B, nk, H, Dh))
        return o + jnp.einsum('bhck,bkhd->bchd', p[..., nk:], vm)

    o_grid = lax.map(row_block, jnp.arange(rows))
    o_grid = jnp.moveaxis(o_grid, 0, 1).reshape(B, T, H, Dh)
    return jnp.concatenate([o_meta, o_grid], axis=1)


def _even_mixer(hn, w_in, conv_a_w, conv_a_b, ig_b, fg_b, hnorm_a, conv_b_w, conv_b_b, dt_bias, a_log, d_skip, gnorm_b, w_out):
    B, L, _ = hn.shape
    f32 = jnp.float32
    n_pad = (-L) % CHUNK
    v4 = (jnp.arange(L + n_pad) >= n_pad)[None, :, None, None]
    q_a, k_a, v_a, o_a, z_a, g_a, z_b, x_b, b_b, c_b, dt_raw = _split(hn @ w_in, EVEN_SIZES)
    qk = jax.nn.silu(_conv_centred(jnp.concatenate([q_a, k_a], axis=-1), conv_a_w, conv_a_b))
    q_a, k_a = jnp.split(qk, 2, axis=-1)
    q = _pad_front(q_a.reshape(B, L, NH_A, DK_A).astype(f32), n_pad)
    k = _pad_front(k_a.reshape(B, L, NH_A, DK_A).astype(f32) * DK_A ** -0.5, n_pad)
    v = _pad_front(v_a.reshape(B, L, NH_A, DV_A).astype(f32), n_pad)
    g = _pad_front(g_a.astype(f32).reshape(B, L, 2, 2, NH_A), n_pad)
    log_i = jnp.where(v4, g[:, :, 0] + ig_b.astype(f32), NEG_GATE)
    log_f = jnp.where(v4, jax.nn.log_sigmoid(g[:, :, 1] + fg_b.astype(f32)), 0.0)
    h_a = _bidir(_mlstm_scan, (q, k, v, log_i[:, :, 0], log_f[:, :, 0]), (q, k, v, log_i[:, :, 1], log_f[:, :, 1]), n_pad)
    h_a = _rms_norm(h_a, hnorm_a) * jax.nn.sigmoid(o_a.astype(f32).reshape(B, L, NH_A, DV_A))
    branch_a = h_a.reshape(B, L, D_A) * jax.nn.silu(z_a.astype(f32))
    xbc = jax.nn.silu(_conv_centred(jnp.concatenate([x_b, b_b, c_b], axis=-1), conv_b_w, conv_b_b))
    x_s, b_s, c_s = _split(xbc, (D_B, NG_B * DS_B, NG_B * DS_B))
    R = NH_B // NG_B
    x_h = x_s.astype(f32).reshape(B, L, NG_B, R, HD_B)
    dt = _pad_front(jax.nn.softplus(dt_raw.astype(f32).reshape(B, L, 2, NH_B) + dt_bias.astype(f32)), n_pad) * v4
    a = (dt * -jnp.exp(a_log.astype(f32))).reshape(B, L + n_pad, 2, NG_B, R)
    dt = dt.reshape(B, L + n_pad, 2, NG_B, R)
    xp = _pad_front(x_h, n_pad)
    bp = _pad_front(b_s.astype(f32).reshape(B, L, NG_B, DS_B), n_pad)
    cp = _pad_front(c_s.astype(f32).reshape(B, L, NG_B, DS_B), n_pad)
    y = _bidir(_ssd_scan, (xp, dt[:, :, 0], a[:, :, 0], bp, cp), (xp, dt[:, :, 1], a[:, :, 1], bp, cp), n_pad)
    y = y + d_skip.astype(f32).reshape(NG_B, R, 1) * x_h
    y = y.reshape(B, L, D_B) * jax.nn.silu(z_b.astype(f32))
    branch_b = _rms_norm(y.reshape(B, L, NG_B, D_B // NG_B), gnorm_b.reshape(NG_B, D_B // NG_B)).reshape(B, L, D_B)
    mixed = jnp.concatenate([branch_a.astype(hn.dtype), branch_b.astype(hn.dtype)], axis=-1)
    return mixed @ w_out


def _odd_mixer(hn, w_in, qn_w, kn_w, rpb, conv_d_w, conv_d_b, dt_bias, a_log, gnorm_d, w_out):
    B, L, _ = hn.shape
    f32 = jnp.float32
    n_pad = (-L) % CHUNK
    v4 = (jnp.arange(L + n_pad) >= n_pad)[None, :, None, None]
    q_c, k_c, v_c, z_c, q_d, k_d, v_d, z_d, g_d = _split(hn @ w_in, ODD_SIZES)
    qc = _rms_norm(q_c.reshape(B, L, NH_C, HD_C), qn_w)
    kc = _rms_norm(k_c.reshape(B, L, NH_C, HD_C), kn_w)
    o_c = _neighbourhood_attention(qc, kc, v_c.reshape(B, L, NH_C, HD_C), rpb)
    branch_c = o_c.reshape(B, L, D_C).astype(f32) * jax.nn.silu(z_c.astype(f32))
    qkv = jax.nn.silu(_conv_centred(jnp.concatenate([q_d, k_d, v_d], axis=-1), conv_d_w, conv_d_b))
    q_d, k_d, v_d = jnp.split(qkv, 3, axis=-1)
    q = _pad_front(_l2n(q_d.astype(f32).reshape(B, L, NH_D, HD_D)) * HD_D ** -0.5, n_pad)
    k = _pad_front(_l2n(k_d.astype(f32).reshape(B, L, NH_D, HD_D)), n_pad)
    v = _pad_front(v_d.astype(f32).reshape(B, L, NH_D, HD_D), n_pad)
    g = _pad_front(g_d.astype(f32).reshape(B, L, 2, 2, NH_D), n_pad)
    beta = jnp.where(v4, jax.nn.sigmoid(g[:, :, 0]), 0.0)
    decay = jnp.where(v4, -jnp.exp(a_log.astype(f32)) * jax.nn.softplus(g[:, :, 1] + dt_bias.astype(f32)), 0.0)
    o_d = _bidir(_gdn_scan, (q, k, v, beta[:, :, 0], decay[:, :, 0]), (q, k, v, beta[:, :, 1], decay[:, :, 1]), n_pad)
    branch_d = _rms_norm(o_d, gnorm_d).reshape(B, L, D_D) * jax.nn.silu(z_d.astype(f32))
    mixed = jnp.concatenate([branch_c.astype(hn.dtype), branch_d.astype(hn.dtype)], axis=-1)
    return mixed @ w_out


def _trunk(x, meta, norm_w, ev, od):
    B, T, D = x.shape
    h = jnp.concatenate([jnp.broadcast_to(meta.astype(x.dtype)[None], (B, N_META, D)), x], axis=1)
    for layer in range(DEPTH):
        i = layer // 2
        hn = _rms_norm(h, norm_w[layer])
        if layer % 2 == 0:
            h = h + _even_mixer(hn, *(p[i] for p in ev))
        else:
            h = h + _odd_mixer(hn, *(p[i] for p in od))
    return h[:, N_META:]


def setup_inputs(seed: int = 0) -> dict:
    key = jax.random.key(seed)
    ks = jax.random.split(key, 32)
    f32 = jnp.float32

    def nrm(k, shape, s):
        return jax.random.normal(k, shape, f32) * s

    def gain(k, shape):
        return 1.0 + 0.02 * jax.random.normal(k, shape, f32)

    def dt_bias_init(k, shape):
        dt0 = jnp.exp(jax.random.uniform(k, shape, f32, np.log(1e-3), np.log(1e-1)))
        return dt0 + jnp.log(-jnp.expm1(-dt0))

    def a_log_init(k, shape):
        return jnp.log(jax.random.uniform(k, shape, f32, 1.0, 16.0))

    n_qk_a = 2 * NH_A * DK_A
    n_xbc = D_B + 2 * NG_B * DS_B
    return {
        'x_prompt': nrm(ks[0], (BATCH, SEQ, D_MODEL), 1.0),
        'x_sample': nrm(ks[1], (DEC_BATCH, DEC_SEQ, D_MODEL), 1.0),
        'meta': nrm(ks[2], (N_META, D_MODEL), 1.0),
        'norm_w': gain(ks[3], (DEPTH, D_MODEL)),
        'ev_w_in': nrm(ks[4], (N_EVEN, D_MODEL, E_IN), D_MODEL ** -0.5),
        'ev_conv_a_w': nrm(ks[5], (N_EVEN, CONV_K, n_qk_a), CONV_K ** -0.5),
        'ev_conv_a_b': nrm(ks[6], (N_EVEN, n_qk_a), 0.02),
        'ev_ig_b': nrm(ks[7], (N_EVEN, 2, NH_A), 0.1),
        'ev_fg_b': jnp.broadcast_to(jnp.linspace(3.0, 6.0, NH_A, dtype=f32), (N_EVEN, 2, NH_A)) + nrm(ks[8], (N_EVEN, 2, NH_A), 0.1),
        'ev_hnorm_a': gain(ks[9], (N_EVEN, NH_A, DV_A)),
        'ev_conv_b_w': nrm(ks[10], (N_EVEN, CONV_K, n_xbc), CONV_K ** -0.5),
        'ev_conv_b_b': nrm(ks[11], (N_EVEN, n_xbc), 0.02),
        'ev_dt_bias': dt_bias_init(ks[12], (N_EVEN, 2, NH_B)),
        'ev_a_log': a_log_init(ks[13], (N_EVEN, 2, NH_B)),
        'ev_d_skip': gain(ks[14], (N_EVEN, NH_B)),
        'ev_gnorm_b': gain(ks[15], (N_EVEN, D_B)),
        'ev_w_out': nrm(ks[16], (N_EVEN, D_A + D_B, D_MODEL), (D_A + D_B) ** -0.5),
        'od_w_in': nrm(ks[17], (N_ODD, D_MODEL, O_IN), D_MODEL ** -0.5),
        'od_qn_w': gain(ks[18], (N_ODD, HD_C)),
        'od_kn_w': gain(ks[19], (N_ODD, HD_C)),
        'od_rpb': nrm(ks[20], (N_ODD, NH_C, 2 * WIN_H - 1, 2 * WIN_W - 1), 0.1),
        'od_conv_d_w': nrm(ks[21], (N_ODD, CONV_K, 3 * D_D), CONV_K ** -0.5),
        'od_conv_d_b': nrm(ks[22], (N_ODD, 3 * D_D), 0.02),
        'od_dt_bias': dt_bias_init(ks[23], (N_ODD, 2, NH_D)),
        'od_a_log': a_log_init(ks[24], (N_ODD, 2, NH_D)),
        'od_gnorm_d': gain(ks[25], (N_ODD, HD_D)),
        'od_w_out': nrm(ks[26], (N_ODD, D_C + D_D, D_MODEL), (D_C + D_D) ** -0.5),
    }


def reference(x_prompt, x_sample, meta, norm_w, ev_w_in, ev_conv_a_w, ev_conv_a_b, ev_ig_b, ev_fg_b, ev_hnorm_a,
              ev_conv_b_w, ev_conv_b_b, ev_dt_bias, ev_a_log, ev_d_skip, ev_gnorm_b, ev_w_out,
              od_w_in, od_qn_w, od_kn_w, od_rpb, od_conv_d_w, od_conv_d_b, od_dt_bias, od_a_log, od_gnorm_d, od_w_out):
    ev = (ev_w_in, ev_conv_a_w, ev_conv_a_b, ev_ig_b, ev_fg_b, ev_hnorm_a, ev_conv_b_w, ev_conv_b_b,
          ev_dt_bias, ev_a_log, ev_d_skip, ev_gnorm_b, ev_w_out)
    od = (od_w_in, od_qn_w, od_kn_w, od_rpb, od_conv_d_w, od_conv_d_b, od_dt_bias, od_a_log, od_gnorm_d, od_w_out)
    y_prompt = _trunk(x_prompt, meta, norm_w, ev, od)
    y_sample = _trunk(x_sample, meta, norm_w, ev, od)
    return (y_prompt, y_sample)
```

```python
import numpy as np
from contextlib import ExitStack
import concourse.bass as bass
import concourse.mybir as mybir
from concourse.bass_utils import run_bass_kernel_spmd

F32 = mybir.dt.float32
BF16 = mybir.dt.bfloat16
AF = mybir.ActivationFunctionType
ALU = mybir.AluOpType

D = 2048
N_META = 16
EPS = 1e-6
NEG = -30000.0
EVEN_SIZES = (1024, 1024, 2048, 2048, 2048, 32, 2048, 2048, 1024, 1024, 64)
ODD_SIZES = (2048, 2048, 2048, 2048, 2048, 2048, 2048, 2048, 64)
E_IN = sum(EVEN_SIZES)
O_IN = sum(ODD_SIZES)


def _offs(sizes):
    o, acc = [], 0
    for s in sizes:
        o.append(acc)
        acc += s
    return o


EOFF = _offs(EVEN_SIZES)
OOFF = _offs(ODD_SIZES)


class Track:
    __slots__ = ("writers", "readers")

    def __init__(self):
        self.writers = {}
        self.readers = {}


class View:
    __slots__ = ("ap", "tr")

    def __init__(self, ap, tr):
        self.ap = ap
        self.tr = tr

    def __getitem__(self, idx):
        return View(self.ap[idx], self.tr)

    def bc(self, shape):
        return View(self.ap.to_broadcast(shape), self.tr)

    def rr(self, s, **kw):
        return View(self.ap.rearrange(s, **kw), self.tr)


class Tile(View):
    def __init__(self, ap):
        View.__init__(self, ap, Track())


class Sched:
    SEM_ROT = 30000

    def __init__(self, nc, n_dma_sems=12):
        self.nc = nc
        self.eng = {"pe": nc.tensor, "act": nc.scalar, "dve": nc.vector,
                    "pool": nc.gpsimd, "sp": nc.sync}
        self.sem = {}
        self.cnt = {}
        self.semid = 0
        for e in self.eng:
            self._new_sem(e)
        self.known = {e: {} for e in self.eng}
        self.dma_sems = {}
        for q in ("sp", "act", "pool"):
            lst = []
            for i in range(n_dma_sems):
                s = nc.alloc_semaphore(name=f"dq_{q}_{i}")
                lst.append([s, 0, f"dq_{q}_{i}"])
            self.dma_sems[q] = lst
        self.dma_rr = {q: 0 for q in self.dma_sems}
        self.ninst = 0

    def _new_sem(self, e):
        self.semid += 1
        key = f"s_{e}_{self.semid}"
        self.sem[e] = (self.nc.alloc_semaphore(name=key), key)
        self.cnt[e] = 0

    def _wait(self, e, key, sem, val):
        k = self.known[e]
        if k.get(key, 0) >= val:
            return
        self.eng[e].wait_ge(sem, val)
        k[key] = val
        self.ninst += 1

    def _deps(self, e, reads, writes):
        mykey = self.sem[e][1]
        for r in reads:
            for key, (sem, val) in r.tr.writers.items():
                if key == mykey and e == "pe":
                    continue
                self._wait(e, key, sem, val)
        for w in writes:
            for key, (sem, val) in w.tr.writers.items():
                if key == mykey:
                    continue
                self._wait(e, key, sem, val)
            for key, (sem, val) in w.tr.readers.items():
                if key == mykey:
                    continue
                self._wait(e, key, sem, val)

    def _record(self, key, sem, val, reads, writes):
        for r in reads:
            r.tr.readers[key] = (sem, val)
        for w in writes:
            w.tr.writers = {key: (sem, val)}
            w.tr.readers = {}

    def op(self, e, fn, reads, writes):
        if self.cnt[e] >= self.SEM_ROT:
            self._new_sem(e)
        self._deps(e, reads, writes)
        inst = fn(self.eng[e])
        sem, key = self.sem[e]
        self.cnt[e] += 1
        inst.then_inc(sem, 1)
        self._record(key, sem, self.cnt[e], reads, writes)
        self.ninst += 1
        return inst

    def dma(self, q, out, in_, **kw):
        lst = self.dma_sems[q]
        i = self.dma_rr[q]
        self.dma_rr[q] = (i + 1) % len(lst)
        ent = lst[i]
        sem, val, key = ent
        if val > 0:
            self._wait(q, key, sem, val)
        self._deps(q, [in_], [out])
        inst = self.eng[q].dma_start(out=out.ap, in_=in_.ap, **kw)
        ent[1] = val + 16
        inst.then_inc(sem, 16)
        self._record(key, sem, ent[1], [in_], [out])
        self.ninst += 1
        return inst

    def drain(self, e, views):
        self._deps(e, views, views)

    def barrier(self, views):
        for e in self.eng:
            self._deps(e, views, views)

    def mm(self, out, lhsT, rhs, start=True, stop=True):
        rd = [lhsT, rhs] + ([] if start else [out])
        return self.op("pe", lambda g: g.matmul(out.ap, lhsT.ap, rhs.ap, start=start, stop=stop), rd, [out])

    def tr(self, out, in_, ident):
        return self.op("pe", lambda g: g.transpose(out.ap, in_.ap, ident.ap), [in_, ident], [out])

    def act(self, out, in_, func, bias=None, scale=None, accum=None):
        rd = [in_]
        kw = {}
        if bias is not None:
            if isinstance(bias, View):
                rd.append(bias)
                kw["bias"] = bias.ap
            else:
                kw["bias"] = bias
        if scale is not None:
            if isinstance(scale, View):
                rd.append(scale)
                kw["scale"] = scale.ap
            else:
                kw["scale"] = scale
        wr = [out]
        if accum is not None:
            wr.append(accum)
            kw["accum_out"] = accum.ap
        return self.op("act", lambda g: g.activation(out.ap, in_.ap, func, **kw), rd, wr)

    def ts(self, e, out, in0, s1, s2, op0, op1=None):
        rd = [in0]
        a1, a2 = s1, s2
        if isinstance(s1, View):
            rd.append(s1)
            a1 = s1.ap
        if isinstance(s2, View):
            rd.append(s2)
            a2 = s2.ap
        if op1 is None:
            return self.op(e, lambda g: g.tensor_scalar(out.ap, in0.ap, a1, a2, op0), rd, [out])
        return self.op(e, lambda g: g.tensor_scalar(out.ap, in0.ap, a1, a2, op0, op1), rd, [out])

    def stt(self, e, out, in0, s, in1, op0, op1):
        e = "dve"
        rd = [in0, in1]
        a = s
        if isinstance(s, View):
            rd.append(s)
            a = s.ap
        return self.op(e, lambda g: g.scalar_tensor_tensor(out.ap, in0.ap, a, in1.ap, op0, op1), rd, [out])

    def tt(self, e, out, in0, in1, op):
        return self.op(e, lambda g: g.tensor_tensor(out.ap, in0.ap, in1.ap, op), [in0, in1], [out])

    def copy(self, e, out, in_):
        if e == "act":
            return self.op(e, lambda g: g.copy(out.ap, in_.ap), [in_], [out])
        return self.op(e, lambda g: g.tensor_copy(out.ap, in_.ap), [in_], [out])

    def memset(self, e, out, val):
        return self.op(e, lambda g: g.memset(out.ap, val), [], [out])

    def recip(self, out, in_):
        return self.op("dve", lambda g: g.reciprocal(out.ap, in_.ap), [in_], [out])


class Pool:
    def __init__(self, tiles):
        self.tiles = tiles
        self.i = 0

    def get(self):
        t = self.tiles[self.i]
        self.i = (self.i + 1) % len(self.tiles)
        return t


C_ID, C_UFW, C_UBW, C_SUFW, C_SUBW, C_NEGFW, C_NEGBW, C_ONES, C_SFW, C_SBW = range(10)
N_CONST = 10


def make_consts():
    p = np.arange(128)[:, None]
    f = np.arange(128)[None, :]
    c = np.zeros((N_CONST, 128, 128), np.float32)
    c[C_ID] = (p == f)
    c[C_UFW] = (p <= f)
    c[C_UBW] = (p >= f)
    c[C_SUFW] = (p > f)
    c[C_SUBW] = (p < f)
    c[C_NEGFW] = np.where(p <= f, 0.0, NEG)
    c[C_NEGBW] = np.where(p >= f, 0.0, NEG)
    c[C_ONES] = 1.0
    c[C_SFW] = (p < f)
    c[C_SBW] = (p > f)
    return c


def build_program(NT, DEPTH, dbg=None):
    Lp = NT * 128
    nc = bass.Bass("TRN2", target_bir_lowering=False)
    S = Sched(nc)
    n_even = (DEPTH + 1) // 2
    n_odd = DEPTH // 2

    def din(name, shape, dt=F32):
        return Tile(nc.dram_tensor(name, list(shape), dt, kind="ExternalInput").ap())

    def dscratch(name, shape, dt=F32):
        return Tile(nc.dram_tensor(name, list(shape), dt).ap())

    H0 = din("h0", [Lp, D])
    TMASK = din("tmask", [128, NT])
    NEGM = din("negm", [128, NT])
    CONSTS = din("consts", [N_CONST, 128, 128])
    NORMW = din("normw", [DEPTH, 128, 16])
    EV = {}
    OD = {}
    if n_even:
        EV["w_in"] = din("ev_w_in", [n_even, D, E_IN])
        EV["w_out"] = din("ev_w_out", [n_even, 4096, D])
        EV["cab"] = din("ev_cab", [n_even, 16, 128, 6])
        EV["cbb"] = din("ev_cbb", [n_even, 32, 128, 6])
        EV["gbias"] = din("ev_gbias", [n_even, 128, 32])
        EV["hnorm"] = din("ev_hnorm", [n_even, 128, 2048])
        EV["dtb"] = din("ev_dtb", [n_even, 128, 64])
        EV["alog"] = din("ev_alog", [n_even, 128, 64])
        EV["dskip"] = din("ev_dskip", [n_even, 128, 32])
        EV["gnorm"] = din("ev_gnorm", [n_even, 128, 2048])
    if n_odd:
        OD["w_in"] = din("od_w_in", [n_odd, D, O_IN])
        OD["w_out"] = din("od_w_out", [n_odd, 4096, D])
        OD["qkn"] = din("od_qkn", [n_odd, 2, 128, 1])
        OD["bias"] = din("od_bias", [n_odd, 16, 128, 7 * 128])
        OD["mask"] = din("od_mask", [NT, 128, 7 * 128])
        OD["mbias"] = din("od_mbias", [32, 1])
        OD["cdb"] = din("od_cdb", [n_odd, 48, 128, 6])
        OD["dtb"] = din("od_dtb", [n_odd, 128, 32])
        OD["alog"] = din("od_alog", [n_odd, 128, 32])
        OD["gnorm"] = din("od_gnorm", [n_odd, 128, 2048])
    Y = Tile(nc.dram_tensor("y", [Lp, D], F32, kind="ExternalOutput").ap())
    DBG = {}
    if dbg:
        for name, shape, dt in dbg:
            DBG[name] = Tile(nc.dram_tensor("dbg_" + name, list(shape), dt, kind="ExternalOutput").ap())

    HB = [dscratch("hb0", [Lp, D]), dscratch("hb1", [Lp, D])]
    class RowSplit:
        def __init__(self, name, nblk, dt):
            self.blk = [dscratch(f"{name}{b}", [2048, Lp], dt) for b in range(nblk)]

        def __getitem__(self, idx):
            r, c = idx
            b = r.start // 2048
            assert (r.stop - 1) // 2048 == b
            return self.blk[b][r.start - b * 2048:r.stop - b * 2048, c]

    class ColSplit:
        def __init__(self, name, cut, width, dt):
            self.cut = cut
            self.a = dscratch(name + "a", [Lp, cut], dt)
            self.b = dscratch(name + "b", [Lp, width - cut], dt)

        def __getitem__(self, idx):
            r, c = idx
            if c.start >= self.cut:
                return self.b[r, c.start - self.cut:c.stop - self.cut]
            assert c.stop <= self.cut
            return self.a[r, c]

    ZF = RowSplit("zf", 5, BF16)
    ZB = RowSplit("zb", 5, BF16)
    ZT = ColSplit("zt", 6144, 8320, F32)
    YF = dscratch("yf", [Lp, 4096])
    MT = dscratch("mt", [4096, Lp], BF16)

    es = ExitStack()
    uid = [0]

    def sb(name, shape, dt=F32):
        return Tile(es.enter_context(nc.sbuf_tensor("g_" + name, list(shape), dt)).ap())

    cf = sb("cf", [128, N_CONST, 128], F32)
    cb = sb("cb", [128, N_CONST, 128], BF16)
    tmask = sb("tmask", [128, NT], F32)
    negm = sb("negm", [128, NT], F32)
    S.dma("sp", cf, CONSTS.rr("c p f -> p c f"))
    S.dma("sp", tmask, TMASK)
    S.dma("sp", negm, NEGM)
    S.copy("dve", cb, cf)

    def CF(i):
        return cf[:, i, :]

    def CB(i):
        return cb[:, i, :]

    psA = [Tile(nc.alloc_psum_tensor(f"psA{i}", [128, 512], F32).ap()) for i in range(5)]
    psB = [Tile(nc.alloc_psum_tensor(f"psB{i}", [128, 1024], BF16).ap()) for i in range(2)]
    psC = Tile(nc.alloc_psum_tensor("psC", [128, 512], F32).ap())
    PA = Pool(psA)
    PB = Pool(psB)

    evac_rr = [0]

    def evac(out, in_):
        evac_rr[0] ^= 1
        S.copy("act" if evac_rr[0] else "dve", out, in_)

    def phaseA(layer, Hin, Win, fm_groups, tm_groups):
        with ExitStack() as st:
            ltiles = []

            def lsb(name, shape, dt=F32):
                uid[0] += 1
                t_ = Tile(st.enter_context(nc.sbuf_tensor(f"{name}_{uid[0]}", list(shape), dt)).ap())
                ltiles.append(t_)
                return t_
            st.callback(lambda: S.barrier(ltiles + psA + psB + [psC]))
            SBT = 16
            hnT = lsb("hnT", [128, 16, SBT * 128], BF16)
            ht = Pool([lsb(f"ht{i}", [128, D]) for i in range(2)])
            hnb = Pool([lsb(f"hnb{i}", [128, D], BF16) for i in range(2)])
            junk = lsb("junk", [128, D], BF16)
            st4 = Pool([lsb(f"st4_{i}", [128, 4]) for i in range(2)])
            wst = Pool([lsb(f"wst{i}", [128, 16, 256]) for i in range(2)])
            wbf = Pool([lsb(f"wbf{i}", [128, 16, 256], BF16) for i in range(2)])
            stf = Pool([lsb(f"stf{i}", [128, SBT * 128], BF16) for i in range(2)])
            stt_ = Pool([lsb(f"stt{i}", [128, 256]) for i in range(4)])
            for t_ in stt_.tiles:
                S.memset("pool", t_, 0.0)
            nw = lsb("nw", [128, 16])
            S.dma("sp", nw, NORMW[layer])
            Wv = Win.rr("(kc p) n -> p kc n", p=128)
            for sb0 in range(0, NT, SBT):
                tiles = list(range(sb0, min(sb0 + SBT, NT)))
                ntok = len(tiles) * 128
                for j in tiles:
                    h = ht.get()
                    S.dma("sp", h, Hin[j * 128:(j + 1) * 128, :])
                    s4 = st4.get()
                    S.act(junk, h, AF.Square, accum=s4[:, 0:1])
                    S.act(s4[:, 1:2], s4[:, 0:1], AF.Sqrt, bias=EPS, scale=1.0 / D)
                    S.recip(s4[:, 2:3], s4[:, 1:2])
                    S.tt("dve", s4[:, 3:4], s4[:, 2:3], tmask[:, j:j + 1], ALU.mult)
                    hb = hnb.get()
                    S.act(hb, h, AF.Copy, scale=s4[:, 3:4])
                    jj = j - sb0
                    for half in range(2):
                        pb = PB.get()
                        for k in range(8):
                            kc = half * 8 + k
                            S.tr(pb[:, k * 128:(k + 1) * 128], hb[:, kc * 128:(kc + 1) * 128], CB(C_ID))
                        evac(hnT[:, half * 8:(half + 1) * 8, jj * 128:(jj + 1) * 128],
                             pb.rr("p (k t) -> p k t", k=8))
                for (c0, r0) in fm_groups:
                    w = wst.get()
                    S.dma("sp", w[:, :, 0:128], Wv[:, :, c0:c0 + 128])
                    wb = wbf.get()
                    S.tt("pool", wb[:, :, 0:128], w[:, :, 0:128], nw.rr("p (k o) -> p k o", o=1).bc([128, 16, 128]), ALU.mult)
                    stg = stf.get()
                    for q0 in range(0, ntok, 512):
                        n = min(512, ntok - q0)
                        ps = PA.get()
                        for kc in range(16):
                            S.mm(ps[:, 0:n], wb[:, kc, 0:128], hnT[:, kc, q0:q0 + n], start=(kc == 0), stop=(kc == 15))
                        evac(stg[:, q0:q0 + n], ps[:, 0:n])
                    S.dma("act", ZF[r0:r0 + 128, sb0 * 128:sb0 * 128 + ntok], stg[:, 0:ntok])
                for (pieces, wd, z0) in tm_groups:
                    w = wst.get()
                    for (c0, pw, off) in pieces:
                        S.dma("sp", w[:, :, off:off + pw], Wv[:, :, c0:c0 + pw])
                    used = max(off + pw for (c0, pw, off) in pieces)
                    wb = wbf.get()
                    S.tt("pool", wb[:, :, 0:used], w[:, :, 0:used], nw.rr("p (k o) -> p k o", o=1).bc([128, 16, used]), ALU.mult)
                    for j in tiles:
                        jj = j - sb0
                        ps = PA.get()
                        for kc in range(16):
                            S.mm(ps[:, 0:used], hnT[:, kc, jj * 128:(jj + 1) * 128], wb[:, kc, 0:used], start=(kc == 0), stop=(kc == 15))
                        sg = stt_.get()
                        evac(sg[:, 0:used], ps[:, 0:used])
                        S.dma("act", ZT[j * 128:(j + 1) * 128, z0:z0 + wd], sg[:, 0:wd])

    def phaseA2(jobs):
        with ExitStack() as st:
            ltiles = []

            def lsb(name, shape, dt=F32):
                uid[0] += 1
                t_ = Tile(st.enter_context(nc.sbuf_tensor(f"{name}_{uid[0]}", list(shape), dt)).ap())
                ltiles.append(t_)
                return t_
            st.callback(lambda: S.barrier(ltiles + psA + psB + [psC]))
            xin = Pool([lsb(f"xin{i}", [128, Lp + 4], BF16) for i in range(3)])
            ob = Pool([lsb(f"ob{i}", [128, Lp], BF16) for i in range(3)])
            cw = Pool([lsb(f"cw{i}", [128, 8]) for i in range(3)])
            dgp = Pool([lsb(f"dg{i}", [128, 5, 128], BF16) for i in range(3)])
            tmpf = Pool([lsb(f"tf{i}", [128, 512]) for i in range(3)])
            sq = Pool([lsb(f"sq{i}", [128, 512], BF16) for i in range(3)])
            rs = Pool([lsb(f"rs{i}", [128, 512]) for i in range(3)])
            for x in xin.tiles:
                S.memset("pool", x[:, 0:2], 0.0)
                S.memset("pool", x[:, Lp + 2:Lp + 4], 0.0)
            for ji, job in enumerate(jobs):
                r0 = job["row"]
                x = xin.get()
                S.dma("sp", x[:, 2:Lp + 2], ZF[r0:r0 + 128, 0:Lp])
                c = cw.get()
                o = ob.get()
                conv = job["kind"] == "conv"
                nm = job["norm"]
                if conv:
                    S.dma("sp", c[:, 0:6], job["cw"])
                    dg = dgp.get()
                    for k in range(5):
                        S.ts("pool", dg[:, k, :], CB(C_ID), c[:, k:k + 1], None, ALU.mult)
                else:
                    S.dma("sp", c[:, 0:1], job["wcol"])
                for q0 in range(0, Lp, 512):
                    n = min(512, Lp - q0)
                    if conv:
                        ps = PA.get()
                        for k in range(5):
                            S.mm(ps[:, 0:n], dg[:, k, :], x[:, q0 + k:q0 + k + n], start=(k == 0), stop=(k == 4))
                        if nm is None:
                            S.act(o[:, q0:q0 + n], ps[:, 0:n], AF.Silu, bias=c[:, 5:6])
                            continue
                        t = tmpf.get()
                        S.act(t[:, 0:n], ps[:, 0:n], AF.Silu, bias=c[:, 5:6])
                        src = t[:, 0:n]
                    else:
                        src = x[:, 2 + q0:2 + q0 + n]
                    s2 = sq.get()
                    S.act(s2[:, 0:n], src, AF.Square)
                    ps2 = PA.get()
                    S.mm(ps2[:, 0:n], CB(C_ONES), s2[:, 0:n])
                    r = rs.get()
                    if nm == "l2q":
                        S.act(r[:, 0:n], ps2[:, 0:n], AF.Sqrt, bias=128.0 * EPS, scale=128.0)
                    elif nm == "l2k":
                        S.act(r[:, 0:n], ps2[:, 0:n], AF.Sqrt, bias=EPS, scale=1.0)
                    elif nm == "rmsq":
                        S.act(r[:, 0:n], ps2[:, 0:n], AF.Sqrt, bias=128.0 * EPS, scale=1.0)
                    else:
                        S.act(r[:, 0:n], ps2[:, 0:n], AF.Sqrt, bias=EPS, scale=1.0 / 128.0)
                    S.recip(r[:, 0:n], r[:, 0:n])
                    if conv:
                        S.tt("dve", o[:, q0:q0 + n], src, r[:, 0:n], ALU.mult)
                    else:
                        t = tmpf.get()
                        S.act(t[:, 0:n], src, AF.Copy, scale=c[:, 0:1])
                        S.tt("dve", o[:, q0:q0 + n], t[:, 0:n], r[:, 0:n], ALU.mult)
                S.dma("act", ZB[r0:r0 + 128, 0:Lp], o)

    def dirconst(d):
        if d == 0:
            return C_UFW, C_SUFW, C_NEGFW
        return C_UBW, C_SUBW, C_NEGBW

    def phaseC(Wout, Hin, Hout):
        with ExitStack() as st:
            ltiles = []

            def lsb(name, shape, dt=F32):
                uid[0] += 1
                t_ = Tile(st.enter_context(nc.sbuf_tensor(f"{name}_{uid[0]}", list(shape), dt)).ap())
                ltiles.append(t_)
                return t_
            st.callback(lambda: S.barrier(ltiles + psA + psB + [psC]))
            wst = Pool([lsb(f"cwst{i}", [128, 8, 512]) for i in range(2)])
            wb = lsb("cwb", [128, 32, 512], BF16)
            mt = Pool([lsb(f"cmt{i}", [128, 32, 128], BF16) for i in range(3)])
            hh = Pool([lsb(f"chh{i}", [128, 512]) for i in range(3)])
            Wv = Wout.rr("(kc p) n -> p kc n", p=128)
            MTv = MT.rr("(kc p) t -> p kc t", p=128)
            for cg in range(4):
                c0 = cg * 512
                for k4 in range(4):
                    w = wst.get()
                    S.dma("sp", w, Wv[:, k4 * 8:(k4 + 1) * 8, c0:c0 + 512])
                    S.copy("pool", wb[:, k4 * 8:(k4 + 1) * 8, :], w)
                for j in range(NT):
                    m = mt.get()
                    S.dma("sp", m, MTv[:, :, j * 128:(j + 1) * 128])
                    h = hh.get()
                    S.dma("sp", h, Hin[j * 128:(j + 1) * 128, c0:c0 + 512])
                    ps = PA.get()
                    for kc in range(32):
                        S.mm(ps, m[:, kc, :], wb[:, kc, :], start=(kc == 0), stop=(kc == 31))
                    S.tt("dve", h, h, ps, ALU.add)
                    S.dma("act", Hout[j * 128:(j + 1) * 128, c0:c0 + 512], h)

    ctx = dict(nc=nc, S=S, NT=NT, Lp=Lp, CF=CF, CB=CB, PA=PA, PB=PB, psC=psC, tmask=tmask, negm=negm,
               ZF=ZF, ZB=ZB, ZT=ZT, YF=YF, MT=MT, evac=evac, dirconst=dirconst, EV=EV, OD=OD, DBG=DBG)

    Hcur = H0
    for layer in range(DEPTH):
        i = layer // 2
        last = (layer == DEPTH - 1)
        Hout = Y if last else HB[layer % 2]
        if layer % 2 == 0:
            fm, tm = even_groups()
            phaseA(layer, Hcur, EV["w_in"][i], fm, tm)
            phaseA2(even_jobs(EV, i))
            mlstm_phase(ctx, i)
            ssd_phase(ctx, i)
            phaseC(EV["w_out"][i], Hcur, Hout)
        else:
            fm, tm = odd_groups()
            phaseA(layer, Hcur, OD["w_in"][i], fm, tm)
            phaseA2(odd_jobs(OD, i))
            na_phase(ctx, i)
            gdn_phase(ctx, i)
            phaseC(OD["w_out"][i], Hcur, Hout)
        Hcur = Hout
    scratch = dict(yf=YF, mt=MT)
    for name in DBG:
        S.dma("sp", DBG[name], scratch[name])
    for e in ("sp", "act", "pool"):
        S.drain(e, [Y] + list(DBG.values()))
    es.close()
    return nc, S


EV_ZF = dict(q=0, k=1024, x=2048, b=4096, c=5120)
EV_ZT = dict(v=0, o=2048, z=4096, zb=6144, g=8192, dt=8192 + 32)
OD_ZF = dict(qc=0, kc=2048, qd=4096, kd=6144, vd=8192)
OD_ZT = dict(vc=0, zc=2048, zd=4096, g=6144)


def even_groups():
    fm = []
    for name, src, n in (("q", 0, 1024), ("k", 1, 1024), ("x", 7, 2048), ("b", 8, 1024), ("c", 9, 1024)):
        for g in range(n // 128):
            fm.append((EOFF[src] + g * 128, EV_ZF[name] + g * 128))
    tm = []
    for name, src, n in (("v", 2, 2048), ("o", 3, 2048), ("z", 4, 2048), ("zb", 6, 2048)):
        for g in range(n // 256):
            tm.append(([(EOFF[src] + g * 256, 256, 0)], 256, EV_ZT[name] + g * 256))
    tm.append(([(EOFF[5], 32, 0), (EOFF[10], 64, 32)], 128, EV_ZT["g"]))
    return fm, tm


def odd_groups():
    fm = []
    for name, src in (("qc", 0), ("kc", 1), ("qd", 4), ("kd", 5), ("vd", 6)):
        for g in range(16):
            fm.append((OOFF[src] + g * 128, OD_ZF[name] + g * 128))
    tm = []
    for name, src in (("vc", 2), ("zc", 3), ("zd", 7)):
        for g in range(8):
            tm.append(([(OOFF[src] + g * 256, 256, 0)], 256, OD_ZT[name] + g * 256))
    tm.append(([(OOFF[8], 64, 0)], 128, OD_ZT["g"]))
    return fm, tm


def even_jobs(EV, i):
    jobs = []
    for g in range(8):
        jobs.append(dict(row=EV_ZF["q"] + g * 128, kind="conv", cw=EV["cab"][i, g], norm=None, post=1.0))
    for g in range(8):
        jobs.append(dict(row=EV_ZF["k"] + g * 128, kind="conv", cw=EV["cab"][i, 8 + g], norm=None, post=1.0))
    for g in range(32):
        jobs.append(dict(row=EV_ZF["x"] + g * 128, kind="conv", cw=EV["cbb"][i, g], norm=None, post=1.0))
    return jobs


def odd_jobs(OD, i):
    jobs = []
    for g in range(16):
        jobs.append(dict(row=OD_ZF["qc"] + g * 128, kind="rms", norm="rmsq", wcol=OD["qkn"][i, 0], post=1.0))
    for g in range(16):
        jobs.append(dict(row=OD_ZF["kc"] + g * 128, kind="rms", norm="rmsk", wcol=OD["qkn"][i, 1], post=1.0))
    for g in range(16):
        jobs.append(dict(row=OD_ZF["qd"] + g * 128, kind="conv", cw=OD["cdb"][i, g], norm="l2q", post=1.0))
    for g in range(16):
        jobs.append(dict(row=OD_ZF["kd"] + g * 128, kind="conv", cw=OD["cdb"][i, 16 + g], norm="l2k", post=1.0))
    for g in range(16):
        jobs.append(dict(row=OD_ZF["vd"] + g * 128, kind="conv", cw=OD["cdb"][i, 32 + g], norm=None, post=1.0))
    return jobs


def _softplus(S, out, in_, tmp):
    S.act(tmp, in_, AF.Exp)
    S.act(out, tmp, AF.Ln, bias=1.0, scale=1.0)


def _transpose_store(ctx, src_bf, MTrow0, c, mtp):
    S, PB, CB, MT, evac = ctx["S"], ctx["PB"], ctx["CB"], ctx["MT"], ctx["evac"]
    m = mtp.get()
    for half in range(2):
        pb = PB.get()
        for k in range(8):
            kc = half * 8 + k
            S.tr(pb[:, k * 128:(k + 1) * 128], src_bf[:, kc * 128:(kc + 1) * 128], CB(C_ID))
        evac(m[:, half * 8:(half + 1) * 8, :], pb.rr("p (k t) -> p k t", k=8))
    S.dma("act", MT[MTrow0:MTrow0 + 2048, c * 128:(c + 1) * 128].rr("(k p) t -> p k t", p=128), m)


def _group_rms(ctx, y, ngrp, gsz, scr, st8, wtile, out_bf):
    S = ctx["S"]
    S.act(scr, y, AF.Square)
    S.op("dve", lambda g: g.tensor_reduce(st8.ap[:, 0:ngrp], scr.ap.rearrange("p (g c) -> p g c", g=ngrp),
                                          mybir.AxisListType.X, ALU.add), [scr], [st8])
    S.act(st8[:, 0:ngrp], st8[:, 0:ngrp], AF.Sqrt, bias=EPS, scale=1.0 / gsz)
    S.recip(st8[:, 0:ngrp], st8[:, 0:ngrp])
    S.tt("dve", scr.rr("p (g c) -> p g c", g=ngrp), y.rr("p (g c) -> p g c", g=ngrp),
         st8[:, 0:ngrp].rr("p (g o) -> p g o", o=1).bc([128, ngrp, gsz]), ALU.mult)
    S.tt("pool", out_bf, scr, wtile, ALU.mult)


def mlstm_phase(ctx, i):
    nc, S, NT = ctx["nc"], ctx["S"], ctx["NT"]
    CF, CB, PA, PB, psC = ctx["CF"], ctx["CB"], ctx["PA"], ctx["PB"], ctx["psC"]
    tmask, negm, ZB, ZT, YF, EV = ctx["tmask"], ctx["negm"], ctx["ZB"], ctx["ZT"], ctx["YF"], ctx["EV"]
    with ExitStack() as st:
        ltiles = []

        def lsb(name, shape, dt=F32):
            t_ = Tile(st.enter_context(nc.sbuf_tensor(f"a{i}_" + name, list(shape), dt)).ap())
            ltiles.append(t_)
            return t_
        st.callback(lambda: S.barrier(ltiles + PA.tiles + PB.tiles + [psC]))
        qk = Pool([lsb(f"qk{j}", [128, 16, 128], BF16) for j in range(2)])
        vf = Pool([lsb(f"vf{j}", [128, 2048]) for j in range(2)])
        vb = Pool([lsb(f"vb{j}", [128, 8, 257], BF16) for j in range(2)])
        gt = Pool([lsb(f"g{j}", [128, 128]) for j in range(2)])
        gw = Pool([lsb(f"gw{j}", [128, 96]) for j in range(2)])
        lbp = Pool([lsb(f"lb{j}", [128, 128]) for j in range(3)])
        dtp = Pool([lsb(f"dt{j}", [128, 128]) for j in range(3)])
        spp = Pool([lsb(f"sp{j}", [128, 128], BF16) for j in range(3)])
        kwp = Pool([lsb(f"kw{j}", [128, 128], BF16) for j in range(3)])
        ysp = Pool([lsb(f"ys{j}", [128, 260]) for j in range(3)])
        yst = Pool([lsb(f"yst{j}", [128, 2048]) for j in range(2)])
        yfl = Pool([lsb(f"yfl{j}", [128, 2048]) for j in range(2)])
        oz = Pool([lsb(f"oz{j}", [128, 4096]) for j in range(2)])
        scr = lsb("scr", [128, 2048])
        st8 = lsb("st8", [128, 8])
        mbf = Pool([lsb(f"mbf{j}", [128, 2048], BF16) for j in range(2)])
        mtp = Pool([lsb(f"mtp{j}", [128, 16, 128], BF16) for j in range(2)])
        X = lsb("X", [128, 8, 257])
        Xb = lsb("Xb", [128, 8, 257], BF16)
        gbias = lsb("gbias", [128, 32])
        hnorm = lsb("hnorm", [128, 2048])
        S.dma("sp", gbias, EV["gbias"][i])
        S.dma("sp", hnorm, EV["hnorm"][i])
        for v in vb.tiles:
            S.memset("pool", v[:, :, 256:257], 1.0)
        for d in (0, 1):
            cU, cSU, cNEG = ctx["dirconst"](d)
            S.memset("pool", X, 0.0)
            S.memset("pool", Xb, 0.0)
            order = range(NT) if d == 0 else range(NT - 1, -1, -1)
            def chunk(c):
                tok = slice(c * 128, (c + 1) * 128)
                q = qk.get()
                S.dma("sp", q, ZB[0:2048, tok].rr("(h p) t -> p h t", p=128))
                v32 = vf.get()
                S.dma("sp", v32, ZT[tok, 0:2048])
                v = vb.get()
                S.copy("pool", v[:, :, 0:256], v32.rr("p (h c) -> p h c", h=8))
                g = gt.get()
                S.dma("sp", g, ZT[tok, 8192:8320])
                w = gw.get()
                S.tt("dve", w[:, 0:8], g[:, d * 8:d * 8 + 8], gbias[:, d * 8:d * 8 + 8], ALU.add)
                S.ts("dve", w[:, 0:8], w[:, 0:8], negm[:, c:c + 1], -0.5 * float(np.log(128.0)), ALU.add, ALU.add)
                S.tt("dve", w[:, 8:16], g[:, 16 + d * 8:24 + d * 8], gbias[:, 16 + d * 8:24 + d * 8], ALU.add)
                S.act(w[:, 16:24], w[:, 8:16], AF.Exp, scale=-1.0)
                S.act(w[:, 16:24], w[:, 16:24], AF.Ln, bias=1.0, scale=1.0)
                S.ts("dve", w[:, 8:16], w[:, 16:24], tmask[:, c:c + 1], -1.0, ALU.mult, ALU.mult)
                S.mm(psC[:, 0:8], CF(cU), w[:, 8:16])
                S.mm(psC[:, 8:16], CF(C_ONES), w[:, 8:16])
                S.copy("dve", w[:, 24:40], psC[:, 0:16])
                S.act(w[:, 40:56], w[:, 24:40], AF.Exp)
                S.tt("dve", w[:, 56:64], w[:, 32:40], w[:, 24:32], ALU.subtract)
                S.tt("dve", w[:, 56:64], w[:, 56:64], w[:, 0:8], ALU.add)
                S.act(w[:, 64:72], w[:, 56:64], AF.Exp)
                if d == 0:
                    yo = yst.get()
                else:
                    yo = yst.get()
                    yf = yfl.get()
                    S.dma("sp", yf, YF[tok, 0:2048])
                yield
                def stage1(h):
                    qT = q[:, h, :]
                    kT = q[:, 8 + h, :]
                    lb = lbp.get()
                    S.act(lb, CF(cSU), AF.Copy, scale=w[:, 8 + h:9 + h])
                    psD = PA.get()
                    S.mm(psD[:, 0:128], lb, CF(cU), start=True, stop=False)
                    S.mm(psD[:, 0:128], CF(C_ID), CF(cNEG), start=False, stop=True)
                    S.mm(psD[:, 128:256], kT, qT)
                    dtm = dtp.get()
                    S.act(dtm, psD[:, 0:128], AF.Exp, bias=w[:, h:h + 1], scale=1.0)
                    sp = spp.get()
                    S.tt("dve", sp, psD[:, 128:256], dtm, ALU.mult)
                    pt = PB.get()
                    S.tr(pt[:, 0:128], kT, CB(C_ID))
                    kw = kwp.get()
                    S.act(kw, pt[:, 0:128], AF.Copy, scale=w[:, 64 + h:65 + h])
                    return sp, kw

                def stage2(h, sp, kw):
                    qT = q[:, h, :]
                    ps1 = PA.get()
                    S.mm(ps1[:, 0:257], sp, v[:, h, :])
                    ps2 = PA.get()
                    S.mm(ps2[:, 0:257], qT, Xb[:, h, :])
                    ys = ysp.get()
                    S.act(ys[:, 0:257], ps2[:, 0:257], AF.Copy, scale=w[:, 40 + h:41 + h])
                    S.tt("dve", ys[:, 0:257], ys[:, 0:257], ps1[:, 0:257], ALU.add)
                    S.act(ys[:, 257:258], ys[:, 256:257], AF.Abs)
                    S.ts("dve", ys[:, 258:259], ys[:, 257:258], 1.0, None, ALU.max)
                    S.recip(ys[:, 259:260], ys[:, 258:259])
                    if d == 0:
                        S.ts("dve", yo[:, h * 256:(h + 1) * 256], ys[:, 0:256], ys[:, 259:260], None, ALU.mult)
                    else:
                        S.stt("dve", yo[:, h * 256:(h + 1) * 256], ys[:, 0:256], ys[:, 259:260],
                              yf[:, h * 256:(h + 1) * 256], ALU.mult, ALU.add)
                    ps3 = PA.get()
                    S.mm(ps3[:, 0:257], kw, v[:, h, :])
                    S.stt("dve", X[:, h, :], X[:, h, :], w[:, 48 + h:49 + h], ps3[:, 0:257], ALU.mult, ALU.add)
                    S.copy("act", Xb[:, h, :], X[:, h, :])

                pend = None
                for h in range(9):
                    cur = stage1(h) if h < 8 else None
                    if pend is not None:
                        stage2(h - 1, *pend)
                    pend = cur
                if d == 0:
                    S.dma("act", YF[tok, 0:2048], yo)
                else:
                    o_z = oz.get()
                    S.dma("sp", o_z, ZT[tok, 2048:6144])
                    mb = mbf.get()
                    _group_rms(ctx, yo, 8, 256, scr, st8, hnorm, mb)
                    S.act(o_z[:, 0:2048], o_z[:, 0:2048], AF.Sigmoid)
                    S.act(o_z[:, 2048:4096], o_z[:, 2048:4096], AF.Silu)
                    S.tt("pool", o_z[:, 0:2048], o_z[:, 0:2048], o_z[:, 2048:4096], ALU.mult)
                    S.tt("dve", mb, mb, o_z[:, 0:2048], ALU.mult)
                    _transpose_store(ctx, mb, 0, c, mtp)


            pend_c = None
            for c in list(order) + [None]:
                cur_c = None
                if c is not None:
                    cur_c = chunk(c)
                    next(cur_c)
                if pend_c is not None:
                    for _ in pend_c:
                        pass
                pend_c = cur_c
def ssd_phase(ctx, i):
    nc, S, NT = ctx["nc"], ctx["S"], ctx["NT"]
    CF, CB, PA, PB, psC = ctx["CF"], ctx["CB"], ctx["PA"], ctx["PB"], ctx["psC"]
    tmask, ZB, ZT, YF, EV, evac = ctx["tmask"], ctx["ZB"], ctx["ZT"], ctx["YF"], ctx["EV"], ctx["evac"]
    with ExitStack() as st:
        ltiles = []

        def lsb(name, shape, dt=F32):
            t_ = Tile(st.enter_context(nc.sbuf_tensor(f"b{i}_" + name, list(shape), dt)).ap())
            ltiles.append(t_)
            return t_
        st.callback(lambda: S.barrier(ltiles + PA.tiles + PB.tiles + [psC]))
        cbt = Pool([lsb(f"cbt{j}", [128, 16, 128], BF16) for j in range(2)])
        xt = Pool([lsb(f"xt{j}", [128, 16, 128], BF16) for j in range(2)])
        dtr = Pool([lsb(f"dtr{j}", [128, 128]) for j in range(2)])
        gw = Pool([lsb(f"gw{j}", [128, 256]) for j in range(2)])
        xs = Pool([lsb(f"xs{j}", [128, 2048], BF16) for j in range(2)])
        xdt = Pool([lsb(f"xdt{j}", [128, 32, 64], BF16) for j in range(2)])
        xw = Pool([lsb(f"xw{j}", [128, 32, 64], BF16) for j in range(2)])
        btm = Pool([lsb(f"btm{j}", [128, 8, 128], BF16) for j in range(2)])
        cbs = Pool([lsb(f"cbs{j}", [128, 128]) for j in range(2)])
        lbp = Pool([lsb(f"lb{j}", [128, 128]) for j in range(8)])
        dtp = Pool([lsb(f"dt{j}", [128, 512]) for j in range(3)])
        spp = Pool([lsb(f"sp{j}", [128, 4, 128], BF16) for j in range(3)])
        y1p = Pool([lsb(f"y1{j}", [128, 256]) for j in range(2)])
        yst = Pool([lsb(f"yst{j}", [128, 2048]) for j in range(2)])
        yfl = Pool([lsb(f"yfl{j}", [128, 2048]) for j in range(2)])
        zb = Pool([lsb(f"zb{j}", [128, 2048]) for j in range(2)])
        scr = lsb("scr", [128, 2048])
        st8 = lsb("st8", [128, 8])
        mbf = Pool([lsb(f"mbf{j}", [128, 2048], BF16) for j in range(2)])
        mtp = Pool([lsb(f"mtp{j}", [128, 16, 128], BF16) for j in range(2)])
        St = lsb("St", [128, 8, 256])
        Sb = lsb("Sb", [128, 8, 256], BF16)
        dtb = lsb("dtb", [128, 64])
        nA = lsb("nA", [128, 64])
        dskip = lsb("dskip", [128, 32])
        gnorm = lsb("gnorm", [128, 2048])
        S.dma("sp", dtb, EV["dtb"][i])
        S.dma("sp", nA, EV["alog"][i])
        S.dma("sp", dskip, EV["dskip"][i])
        S.dma("sp", gnorm, EV["gnorm"][i])
        S.act(nA, nA, AF.Exp)
        S.ts("dve", nA, nA, -1.0, None, ALU.mult)
        for d in (0, 1):
            cU, cSU, cNEG = ctx["dirconst"](d)
            S.memset("pool", St, 0.0)
            S.memset("pool", Sb, 0.0)
            order = range(NT) if d == 0 else range(NT - 1, -1, -1)
            def chunk(c):
                tok = slice(c * 128, (c + 1) * 128)
                cb_ = cbt.get()
                S.dma("sp", cb_, ZB[4096:6144, tok].rr("(g p) t -> p g t", p=128))
                x_ = xt.get()
                S.dma("sp", x_, ZB[2048:4096, tok].rr("(g p) t -> p g t", p=128))
                dr = dtr.get()
                S.dma("sp", dr, ZT[tok, 8192:8320])
                w = gw.get()
                S.tt("dve", w[:, 32:64], dr[:, 32 + d * 32:64 + d * 32], dtb[:, d * 32:(d + 1) * 32], ALU.add)
                S.act(w[:, 32:64], w[:, 32:64], AF.Exp)
                S.act(w[:, 32:64], w[:, 32:64], AF.Ln, bias=1.0, scale=1.0)
                S.ts("dve", w[:, 0:32], w[:, 32:64], tmask[:, c:c + 1], None, ALU.mult)
                S.tt("dve", w[:, 32:64], w[:, 0:32], nA[:, d * 32:(d + 1) * 32], ALU.mult)
                S.mm(psC[:, 0:32], CF(cU), w[:, 32:64])
                S.mm(psC[:, 32:64], CF(C_ONES), w[:, 32:64])
                S.copy("dve", w[:, 64:128], psC[:, 0:64])
                S.act(w[:, 128:192], w[:, 64:128], AF.Exp)
                S.tt("dve", w[:, 192:224], w[:, 96:128], w[:, 64:96], ALU.subtract)
                S.act(w[:, 192:224], w[:, 192:224], AF.Exp)
                xs_ = xs.get()
                xd = xdt.get()
                xw_ = xw.get()
                for half in range(2):
                    pb = PB.get()
                    for k in range(8):
                        S.tr(pb[:, k * 128:(k + 1) * 128], x_[:, half * 8 + k, :], CB(C_ID))
                    evac(xs_[:, half * 1024:(half + 1) * 1024], pb)
                S.tt("dve", xd, xs_.rr("p (h c) -> p h c", h=32),
                     w[:, 0:32].rr("p (h o) -> p h o", o=1).bc([128, 32, 64]), ALU.mult)
                S.tt("pool", xw_, xd, w[:, 192:224].rr("p (h o) -> p h o", o=1).bc([128, 32, 64]), ALU.mult)
                bt_ = btm.get()
                pb = PB.get()
                for g in range(8):
                    S.tr(pb[:, g * 128:(g + 1) * 128], cb_[:, g, :], CB(C_ID))
                evac(bt_, pb.rr("p (g t) -> p g t", g=8))
                yo = yst.get()
                if d == 1:
                    yf = yfl.get()
                    S.dma("sp", yf, YF[tok, 2048:4096])
                yield
                def stage1(g):
                    psCB = PA.get()
                    S.mm(psCB[:, 0:128], cb_[:, g, :], cb_[:, 8 + g, :])
                    cs = cbs.get()
                    S.tt("dve", cs, psCB[:, 0:128], CF(cU), ALU.mult)
                    psD = PA.get()
                    for r in range(4):
                        hd = g * 4 + r
                        lb = lbp.get()
                        S.act(lb, CF(cSU), AF.Copy, scale=w[:, 32 + hd:33 + hd])
                        S.mm(psD[:, r * 128:(r + 1) * 128], lb, CF(cU))
                    dtm = dtp.get()
                    S.act(dtm, psD, AF.Exp)
                    sp = spp.get()
                    S.tt("dve", sp, dtm.rr("p (r t) -> p r t", r=4),
                         cs.rr("p (o t) -> p o t", o=1).bc([128, 4, 128]), ALU.mult)
                    return sp

                def stage2(g, sp):
                    psY = PA.get()
                    for r in range(4):
                        S.mm(psY[:, r * 64:(r + 1) * 64], sp[:, r, :], xd[:, g * 4 + r, :])
                    psY2 = PA.get()
                    S.mm(psY2[:, 0:256], cb_[:, 8 + g, :], Sb[:, g, :])
                    y1 = y1p.get()
                    S.tt("dve", y1.rr("p (r c) -> p r c", r=4), psY2[:, 0:256].rr("p (r c) -> p r c", r=4),
                         w[:, 128 + g * 4:132 + g * 4].rr("p (r o) -> p r o", o=1).bc([128, 4, 64]), ALU.mult)
                    if d == 0:
                        S.tt("dve", yo[:, g * 256:(g + 1) * 256], y1, psY[:, 0:256], ALU.add)
                    else:
                        S.tt("dve", y1, y1, psY[:, 0:256], ALU.add)
                        S.tt("pool", yo[:, g * 256:(g + 1) * 256], y1, yf[:, g * 256:(g + 1) * 256], ALU.add)
                    psS = PA.get()
                    S.mm(psS[:, 0:256], bt_[:, g, :], xw_[:, g * 4:(g + 1) * 4, :].rr("p r c -> p (r c)"))
                    S.tt("pool", St[:, g, :].rr("p (r c) -> p r c", r=4), St[:, g, :].rr("p (r c) -> p r c", r=4),
                         w[:, 160 + g * 4:164 + g * 4].rr("p (r o) -> p r o", o=1).bc([128, 4, 64]), ALU.mult)
                    S.tt("dve", St[:, g, :], St[:, g, :], psS[:, 0:256], ALU.add)
                    S.copy("act", Sb[:, g, :], St[:, g, :])

                pend = None
                for g in range(9):
                    cur = stage1(g) if g < 8 else None
                    if pend is not None:
                        stage2(g - 1, pend)
                    pend = cur
                if d == 0:
                    S.dma("act", YF[tok, 2048:4096], yo)
                else:
                    z_ = zb.get()
                    S.dma("sp", z_, ZT[tok, 6144:8192])
                    S.tt("pool", scr.rr("p (h c) -> p h c", h=32), xs_.rr("p (h c) -> p h c", h=32),
                         dskip.rr("p (h o) -> p h o", o=1).bc([128, 32, 64]), ALU.mult)
                    S.tt("dve", yo, yo, scr, ALU.add)
                    S.act(z_, z_, AF.Silu)
                    S.tt("dve", yo, yo, z_, ALU.mult)
                    mb = mbf.get()
                    _group_rms(ctx, yo, 8, 256, scr, st8, gnorm, mb)
                    _transpose_store(ctx, mb, 2048, c, mtp)


            pend_c = None
            for c in list(order) + [None]:
                cur_c = None
                if c is not None:
                    cur_c = chunk(c)
                    next(cur_c)
                if pend_c is not None:
                    for _ in pend_c:
                        pass
                pend_c = cur_c
def na_phase(ctx, i):
    nc, S, NT = ctx["nc"], ctx["S"], ctx["NT"]
    CF, CB, PA, PB, psC = ctx["CF"], ctx["CB"], ctx["PA"], ctx["PB"], ctx["psC"]
    ZB, ZT, OD = ctx["ZB"], ctx["ZT"], ctx["OD"]
    with ExitStack() as st:
        ltiles = []

        def lsb(name, shape, dt=F32):
            t_ = Tile(st.enter_context(nc.sbuf_tensor(f"c{i}_" + name, list(shape), dt)).ap())
            ltiles.append(t_)
            return t_
        st.callback(lambda: S.barrier(ltiles + PA.tiles + PB.tiles + [psC]))
        biasb = lsb("biasb", [128, 16, 896], BF16)
        stg = Pool([lsb(f"stg{j}", [128, 896]) for j in range(2)])
        maskb = Pool([lsb(f"maskb{j}", [128, 896], BF16) for j in range(2)])
        kring = Pool([lsb(f"kr{j}", [128, 16, 128], BF16) for j in range(8)])
        vring = Pool([lsb(f"vr{j}", [128, 16, 129], BF16) for j in range(8)])
        vst = Pool([lsb(f"vst{j}", [128, 2048]) for j in range(2)])
        qp = Pool([lsb(f"q{j}", [128, 16, 128], BF16) for j in range(2)])
        ptp = Pool([lsb(f"pt{j}", [128, 512], BF16) for j in range(4)])
        pmp = Pool([lsb(f"pm{j}", [32, 128], BF16) for j in range(2)])
        rdp = Pool([lsb(f"rd{j}", [128, 2]) for j in range(4)])
        op_ = Pool([lsb(f"o{j}", [128, 2048]) for j in range(2)])
        zp = Pool([lsb(f"z{j}", [128, 2048]) for j in range(2)])
        mbf = Pool([lsb(f"mbf{j}", [128, 2048], BF16) for j in range(2)])
        mtp = Pool([lsb(f"mtp{j}", [128, 16, 128], BF16) for j in range(2)])
        kmeta = lsb("kmeta", [128, 16, 32], BF16)
        vmst = lsb("vmst", [32, 2048])
        vmeta = lsb("vmeta", [32, 16, 129], BF16)
        mbias = lsb("mbias", [32, 1])
        S.dma("sp", mbias, OD["mbias"])
        for h in range(16):
            sg = stg.get()
            S.dma("sp", sg, OD["bias"][i, h])
            S.copy("pool", biasb[:, h, :], sg)
        for v in vring.tiles:
            S.memset("pool", v[:, :, 128:129], 1.0)
        S.memset("pool", vmeta[:, :, 128:129], 1.0)
        S.dma("sp", kmeta, ZB[2048:4096, 96:128].rr("(h p) t -> p h t", p=128))
        S.dma("sp", vmst, ZT[96:128, 0:2048])
        S.copy("pool", vmeta[:, :, 0:128], vmst.rr("p (h c) -> p h c", h=16))
        loaded = {}

        def get_kv(ki):
            if ki not in loaded:
                k = kring.get()
                v = vring.get()
                S.dma("sp", k, ZB[2048:4096, ki * 128:(ki + 1) * 128].rr("(h p) t -> p h t", p=128))
                vs = vst.get()
                S.dma("sp", vs, ZT[ki * 128:(ki + 1) * 128, 0:2048])
                S.copy("pool", v[:, :, 0:128], vs.rr("p (h c) -> p h c", h=16))
                loaded[ki] = (k, v)
            return loaded[ki]

        for qi in range(NT):
            tok = slice(qi * 128, (qi + 1) * 128)
            q = qp.get()
            S.dma("sp", q, ZB[0:2048, tok].rr("(h p) t -> p h t", p=128))
            valid = [(di, qi + di - 3) for di in range(7) if 1 <= qi + di - 3 < NT]
            kv = {ki: get_kv(ki) for (_, ki) in valid}
            sg = stg.get()
            S.dma("sp", sg, OD["mask"][qi])
            mk = maskb.get()
            S.copy("pool", mk, sg)
            z_ = zp.get()
            S.dma("sp", z_, ZT[tok, 2048:4096])
            o_ = op_.get()
            for h in range(16):
                banks = []
                for b0 in range(0, len(valid), 4):
                    grp = valid[b0:b0 + 4]
                    ps = PA.get()
                    for sl, (di, ki) in enumerate(grp):
                        dst = ps[:, sl * 128:(sl + 1) * 128]
                        S.mm(dst, kv[ki][0][:, h, :], q[:, h, :], start=True, stop=False)
                        S.mm(dst, CB(C_ID), biasb[:, h, di * 128:(di + 1) * 128], start=False, stop=False)
                        S.mm(dst, CB(C_ID), mk[:, di * 128:(di + 1) * 128], start=False, stop=True)
                    pt = ptp.get()
                    n = len(grp) * 128
                    S.act(pt[:, 0:n], ps[:, 0:n], AF.Exp)
                    banks.append((pt, grp))
                S.mm(psC[0:32, 0:128], kmeta[:, h, :], q[:, h, :])
                pm = pmp.get()
                S.act(pm, psC[0:32, 0:128], AF.Exp, bias=mbias[:, 0:1], scale=1.0)
                po = PA.get()
                first = True
                for (pt, grp) in banks:
                    for sl, (di, ki) in enumerate(grp):
                        S.mm(po[:, 0:129], pt[:, sl * 128:(sl + 1) * 128], kv[ki][1][:, h, :], start=first, stop=False)
                        first = False
                S.mm(po[:, 0:129], pm, vmeta[:, h, :], start=first, stop=True)
                rd = rdp.get()
                S.recip(rd[:, 0:1], po[:, 128:129])
                S.act(o_[:, h * 128:(h + 1) * 128], po[:, 0:128], AF.Copy, scale=rd[:, 0:1])
            S.act(z_, z_, AF.Silu)
            mb = mbf.get()
            S.tt("dve", mb, o_, z_, ALU.mult)
            _transpose_store(ctx, mb, 0, qi, mtp)


def gdn_phase(ctx, i):
    nc, S, NT = ctx["nc"], ctx["S"], ctx["NT"]
    CF, CB, PA, PB, psC = ctx["CF"], ctx["CB"], ctx["PA"], ctx["PB"], ctx["psC"]
    tmask, ZB, ZT, YF, OD = ctx["tmask"], ctx["ZB"], ctx["ZT"], ctx["YF"], ctx["OD"]
    with ExitStack() as st:
        ltiles = []

        def lsb(name, shape, dt=F32):
            t_ = Tile(st.enter_context(nc.sbuf_tensor(f"d{i}_" + name, list(shape), dt)).ap())
            ltiles.append(t_)
            return t_
        st.callback(lambda: S.barrier(ltiles + PA.tiles + PB.tiles + [psC]))
        q3 = Pool([lsb(f"q{j}", [128, 16, 128], BF16) for j in range(2)])
        k3 = Pool([lsb(f"k{j}", [128, 16, 128], BF16) for j in range(2)])
        v3 = Pool([lsb(f"v{j}", [128, 16, 128], BF16) for j in range(2)])
        gt = Pool([lsb(f"g{j}", [128, 128]) for j in range(2)])
        gw = Pool([lsb(f"gw{j}", [128, 160]) for j in range(2)])
        lbp = Pool([lsb(f"lb{j}", [128, 128]) for j in range(8)])
        etp = Pool([lsb(f"et{j}", [128, 4, 128]) for j in range(2)])
        ep = Pool([lsb(f"e{j}", [128, 4, 128]) for j in range(2)])
        atp = Pool([lsb(f"at{j}", [128, 4, 128], BF16) for j in range(2)])
        CHDT = F32
        KDT = BF16
        CI = CF(C_ID) if CHDT == F32 else CB(C_ID)
        Pp = [Pool([lsb(f"P{sl}{j}", [128, 4, 128], CHDT) for j in range(2)]) for sl in range(2)]
        PTp = [Pool([lsb(f"PT{sl}{j}", [128, 4, 128], CHDT) for j in range(2)]) for sl in range(2)]
        TTp = [Pool([lsb(f"TT{sl}{j}", [128, 4, 128], CHDT) for j in range(2)]) for sl in range(2)]
        TTup = Pool([lsb(f"TTu{j}", [128, 4, 128], KDT) for j in range(2)])
        ktmp = Pool([lsb(f"ktm{j}", [128, 4, 128], BF16) for j in range(2)])
        vtmp = Pool([lsb(f"vtm{j}", [128, 4, 128], BF16) for j in range(2)])
        kbep = Pool([lsb(f"kbe{j}", [128, 4, 128], KDT) for j in range(2)])
        kwep = Pool([lsb(f"kwe{j}", [128, 4, 128], BF16) for j in range(2)])
        vbep = Pool([lsb(f"vbe{j}", [128, 4, 128], KDT) for j in range(2)])
        nwmp = Pool([lsb(f"nwm{j}", [128, 4, 128], BF16) for j in range(2)])
        ubp = Pool([lsb(f"ub{j}", [128, 4, 128], BF16) for j in range(2)])
        y1p = Pool([lsb(f"y1{j}", [128, 4, 128]) for j in range(2)])
        yst = Pool([lsb(f"yst{j}", [128, 2048]) for j in range(2)])
        yfl = Pool([lsb("yfl0", [128, 2048])])
        zdp = Pool([lsb("zd0", [128, 2048])])
        scr = lsb("scr", [128, 2048])
        st16 = lsb("st16", [128, 16])
        mbf = Pool([lsb(f"mbf{j}", [128, 2048], BF16) for j in range(2)])
        mtp = Pool([lsb(f"mtp{j}", [128, 16, 128], BF16) for j in range(2)])
        St = lsb("St", [128, 16, 128])
        Sb = lsb("Sb", [128, 16, 128], BF16)
        dtb = lsb("dtb", [128, 32])
        nA = lsb("nA", [128, 32])
        gn = lsb("gn", [128, 2048])
        S.dma("sp", dtb, OD["dtb"][i])
        S.dma("sp", nA, OD["alog"][i])
        S.dma("sp", gn, OD["gnorm"][i])
        S.act(nA, nA, AF.Exp)
        S.ts("dve", nA, nA, -1.0, None, ALU.mult)

        def bc4(v):
            return v.rr("p (h o) -> p h o", o=1).bc([128, 4, 128])

        def m4(cidx):
            return CF(cidx).rr("p (o t) -> p o t", o=1).bc([128, 4, 128])

        for d in (0, 1):
            cU, cSU, cNEG = ctx["dirconst"](d)
            cST = C_SBW if d == 0 else C_SFW
            S.memset("pool", St, 0.0)
            S.memset("pool", Sb, 0.0)
            order = range(NT) if d == 0 else range(NT - 1, -1, -1)
            for c in order:
                tok = slice(c * 128, (c + 1) * 128)
                q = q3.get()
                k = k3.get()
                v = v3.get()
                S.dma("sp", q, ZB[4096:6144, tok].rr("(h p) t -> p h t", p=128))
                S.dma("sp", k, ZB[6144:8192, tok].rr("(h p) t -> p h t", p=128))
                S.dma("sp", v, ZB[8192:10240, tok].rr("(h p) t -> p h t", p=128))
                g = gt.get()
                S.dma("sp", g, ZT[tok, 6144:6272])
                w = gw.get()
                S.act(w[:, 0:16], g[:, d * 16:(d + 1) * 16], AF.Sigmoid)
                S.ts("dve", w[:, 0:16], w[:, 0:16], tmask[:, c:c + 1], None, ALU.mult)
                S.tt("dve", w[:, 16:32], g[:, 32 + d * 16:48 + d * 16], dtb[:, d * 16:(d + 1) * 16], ALU.add)
                S.act(w[:, 16:32], w[:, 16:32], AF.Exp)
                S.act(w[:, 16:32], w[:, 16:32], AF.Ln, bias=1.0, scale=1.0)
                S.tt("dve", w[:, 16:32], w[:, 16:32], nA[:, d * 16:(d + 1) * 16], ALU.mult)
                S.ts("dve", w[:, 16:32], w[:, 16:32], tmask[:, c:c + 1], None, ALU.mult)
                S.mm(psC[:, 0:16], CF(cU), w[:, 16:32])
                S.mm(psC[:, 16:32], CF(C_ONES), w[:, 16:32])
                S.copy("dve", w[:, 32:64], psC[:, 0:32])
                S.act(w[:, 64:96], w[:, 32:64], AF.Exp)
                S.tt("dve", w[:, 96:112], w[:, 48:64], w[:, 32:48], ALU.subtract)
                S.act(w[:, 96:112], w[:, 96:112], AF.Exp)
                S.tt("dve", w[:, 112:128], w[:, 0:16], w[:, 64:80], ALU.mult)
                S.ts("dve", w[:, 128:144], w[:, 0:16], -1.0, None, ALU.mult)
                yo = yst.get()
                if d == 1:
                    yf = yfl.get()
                    S.dma("sp", yf, YF[tok, 0:2048])
                def g_stage1(qd, sl):
                    H = [qd * 4 + r for r in range(4)]
                    lbs = []
                    for r, hd in enumerate(H):
                        lb = lbp.get()
                        S.act(lb, CF(cSU), AF.Copy, scale=w[:, 16 + hd:17 + hd])
                        lbs.append(lb)
                    psET = PA.get()
                    for r in range(4):
                        S.mm(psET[:, r * 128:(r + 1) * 128], lbs[r], CF(cU))
                    et = etp.get()
                    S.act(et.rr("p h t -> p (h t)"), psET, AF.Exp)
                    psE = PA.get()
                    for r in range(4):
                        S.mm(psE[:, r * 128:(r + 1) * 128], CF(cU), lbs[r])
                    e_ = ep.get()
                    S.act(e_.rr("p h t -> p (h t)"), psE, AF.Exp)
                    psQK = PA.get()
                    for r, hd in enumerate(H):
                        S.mm(psQK[:, r * 128:(r + 1) * 128], k[:, hd, :], q[:, hd, :])
                    S.tt("pool", et, et, m4(cU), ALU.mult)
                    at = atp.get()
                    S.tt("dve", at, psQK.rr("p (h t) -> p h t", h=4), et, ALU.mult)
                    psKK = PA.get()
                    for r, hd in enumerate(H):
                        S.mm(psKK[:, r * 128:(r + 1) * 128], k[:, hd, :], k[:, hd, :])
                    S.tt("pool", e_, e_, m4(cST), ALU.mult)
                    S.tt("dve", e_, psKK.rr("p (h t) -> p h t", h=4), e_, ALU.mult)
                    P = Pp[sl].get()
                    S.tt("pool", P, e_, bc4(w[:, 128 + qd * 4:132 + qd * 4]), ALU.mult)
                    pbN = PA.get()
                    for r in range(4):
                        S.mm(pbN[:, r * 128:(r + 1) * 128], P[:, r, :], CI)
                    PT = PTp[sl].get()
                    S.copy("act", PT, pbN.rr("p (h t) -> p h t", h=4))
                    TT = TTp[sl].get()
                    S.tt("dve", TT, PT, m4(C_ID), ALU.add)
                    return dict(qd=qd, sl=sl, H=H, at=at, P=P, PT=PT, TT=TT)

                def g_mm1(t_, lev):
                    P, PT = t_["P"], t_["PT"]
                    bP = PA.get()
                    for r in range(4):
                        S.mm(bP[:, r * 128:(r + 1) * 128], PT[:, r, :], P[:, r, :])
                    t_["bP"] = bP
                    if lev < 5:
                        bPT = PA.get()
                        for r in range(4):
                            S.mm(bPT[:, r * 128:(r + 1) * 128], P[:, r, :], PT[:, r, :])
                        t_["bPT"] = bPT

                def g_ev1_mm2(t_, lev):
                    sl = t_["sl"]
                    Pn = Pp[sl].get()
                    S.copy("act", Pn, t_["bP"].rr("p (h t) -> p h t", h=4))
                    if lev < 5:
                        PTn = PTp[sl].get()
                        S.copy("dve", PTn, t_["bPT"].rr("p (h t) -> p h t", h=4))
                        t_["PT"] = PTn
                    t_["P"] = Pn
                    bT = PA.get()
                    for r in range(4):
                        S.mm(bT[:, r * 128:(r + 1) * 128], Pn[:, r, :], t_["TT"][:, r, :])
                    t_["bT"] = bT

                def g_ev2(t_):
                    TTn = TTp[t_["sl"]].get()
                    S.tt("dve", TTn, t_["TT"], t_["bT"].rr("p (h t) -> p h t", h=4), ALU.add)
                    t_["TT"] = TTn

                def g_stage2(t_):
                    qd, H, at, TT = t_["qd"], t_["H"], t_["at"], t_["TT"]
                    hs = slice(qd * 4, qd * 4 + 4)
                    TTu = TTup.get()
                    S.copy("act", TTu, TT)
                    pbK = PB.get()
                    for r, hd in enumerate(H):
                        S.tr(pbK[:, r * 128:(r + 1) * 128], k[:, hd, :], CB(C_ID))
                    for r, hd in enumerate(H):
                        S.tr(pbK[:, 512 + r * 128:512 + (r + 1) * 128], v[:, hd, :], CB(C_ID))
                    ktm = ktmp.get()
                    vtm = vtmp.get()
                    S.copy("act", ktm, pbK[:, 0:512].rr("p (h t) -> p h t", h=4))
                    S.copy("dve", vtm, pbK[:, 512:1024].rr("p (h t) -> p h t", h=4))
                    kbe = kbep.get()
                    kwe = kwep.get()
                    vbe = vbep.get()
                    S.tt("pool", kbe, ktm, bc4(w[:, 112 + qd * 4:116 + qd * 4]), ALU.mult)
                    S.tt("pool", kwe, ktm, bc4(w[:, 96 + qd * 4:100 + qd * 4]), ALU.mult)
                    S.tt("pool", vbe, vtm, bc4(w[:, qd * 4:qd * 4 + 4]), ALU.mult)
                    bW = PA.get()
                    for r in range(4):
                        S.mm(bW[:, r * 128:(r + 1) * 128], kbe[:, r, :], TTu[:, r, :])
                    nwm = nwmp.get()
                    S.act(nwm.rr("p h t -> p (h t)"), bW, AF.Copy, scale=-1.0)
                    bU = PA.get()
                    for r, hd in enumerate(H):
                        S.mm(bU[:, r * 128:(r + 1) * 128], TTu[:, r, :], vbe[:, r, :], start=True, stop=False)
                        S.mm(bU[:, r * 128:(r + 1) * 128], nwm[:, r, :], Sb[:, hd, :], start=False, stop=True)
                    ub = ubp.get()
                    S.copy("dve", ub, bU.rr("p (h t) -> p h t", h=4))
                    bO1 = PA.get()
                    for r in range(4):
                        S.mm(bO1[:, r * 128:(r + 1) * 128], at[:, r, :], ub[:, r, :])
                    bO2 = PA.get()
                    for r, hd in enumerate(H):
                        S.mm(bO2[:, r * 128:(r + 1) * 128], q[:, hd, :], Sb[:, hd, :])
                    y1 = y1p.get()
                    S.tt("dve", y1, bO2.rr("p (h t) -> p h t", h=4), bc4(w[:, 64 + qd * 4:68 + qd * 4]), ALU.mult)
                    ycols = yo[:, qd * 512:(qd + 1) * 512]
                    if d == 0:
                        S.tt("dve", ycols, y1.rr("p h t -> p (h t)"), bO1, ALU.add)
                    else:
                        S.tt("dve", y1.rr("p h t -> p (h t)"), y1.rr("p h t -> p (h t)"), bO1, ALU.add)
                        S.tt("pool", ycols, y1.rr("p h t -> p (h t)"), yf[:, qd * 512:(qd + 1) * 512], ALU.add)
                    bS = PA.get()
                    for r in range(4):
                        S.mm(bS[:, r * 128:(r + 1) * 128], kwe[:, r, :], ub[:, r, :])
                    S.tt("pool", St[:, hs, :], St[:, hs, :], bc4(w[:, 80 + qd * 4:84 + qd * 4]), ALU.mult)
                    S.tt("dve", St[:, hs, :], St[:, hs, :], bS.rr("p (h t) -> p h t", h=4), ALU.add)
                    S.copy("act", Sb[:, hs, :], St[:, hs, :])

                for pair in ((0,), (1,), (2,), (3,)):
                    sts = [g_stage1(qd, sl) for sl, qd in enumerate(pair)]
                    for lev in range(6):
                        for t_ in sts:
                            g_mm1(t_, lev)
                        for t_ in sts:
                            g_ev1_mm2(t_, lev)
                        for t_ in sts:
                            g_ev2(t_)
                    for t_ in sts:
                        g_stage2(t_)
                if d == 0:
                    S.dma("act", YF[tok, 0:2048], yo)
                else:
                    z_ = zdp.get()
                    S.dma("sp", z_, ZT[tok, 4096:6144])
                    mb = mbf.get()
                    _group_rms(ctx, yo, 16, 128, scr, st16, gn, mb)
                    S.act(z_, z_, AF.Silu)
                    S.tt("dve", mb, mb, z_, ALU.mult)
                    _transpose_store(ctx, mb, 2048, c, mtp)


def _bc(v, n=128):
    v = np.asarray(v, np.float32).reshape(-1)
    return np.ascontiguousarray(np.broadcast_to(v[None, :], (n, v.size)))


def _conv_pack(w, b):
    C = w.shape[1]
    o = np.concatenate([w.T, b[:, None]], axis=1).astype(np.float32)
    return np.ascontiguousarray(o.reshape(C // 128, 128, 6))


def _na_bias(rpb):
    p = np.arange(128)[:, None]
    f = np.arange(128)[None, :]
    kr, kc = p // 64, p % 64
    qr, qc = f // 64, f % 64
    c0 = np.clip(qc - 8, 0, 48)
    col_ok = (kc >= c0) & (kc < c0 + 16)
    dc = np.clip(kc - qc + 15, 0, 30)
    out = np.full((16, 128, 7, 128), NEG, np.float32)
    for di, dl in enumerate(range(-3, 4)):
        dr = 2 * dl + kr - qr
        ok = col_ok & (np.abs(dr) <= 7)
        dri = np.clip(dr + 7, 0, 14)
        vals = rpb[:, dri, dc]
        out[:, :, di, :] = np.where(ok[None], vals, NEG)
    return np.ascontiguousarray(out.reshape(16, 128, 7 * 128))


def _na_mask(NT, T):
    j0 = 1
    rows = T // 64
    jend = 1 + T // 128
    p = np.arange(128)[:, None]
    f = np.arange(128)[None, :]
    out = np.full((NT, 128, 7, 128), NEG, np.float32)
    for qi in range(j0, jend):
        q_row = 2 * (qi - j0) + f // 64
        r0 = np.clip(q_row - 4, 0, rows - 8)
        for di, dl in enumerate(range(-3, 4)):
            ki = qi + dl
            if ki < j0 or ki >= jend:
                continue
            k_row = 2 * (ki - j0) + p // 64
            ok = (k_row >= r0) & (k_row < r0 + 8)
            out[qi, :, di, :] = np.where(ok, 0.0, NEG)
    return np.ascontiguousarray(out.reshape(NT, 128, 7 * 128))


def prep_core_inputs(inp, seqs, NT, DEPTH):
    Lp = NT * 128
    n_even = (DEPTH + 1) // 2
    n_odd = DEPTH // 2
    f = lambda a: np.ascontiguousarray(np.asarray(a, np.float32))
    shared = {"consts": make_consts()}
    shared["normw"] = f(np.stack([np.asarray(inp["norm_w"][l]).reshape(16, 128).T for l in range(DEPTH)]))
    if n_even:
        shared["ev_w_in"] = f(inp["ev_w_in"][:n_even])
        shared["ev_w_out"] = f(inp["ev_w_out"][:n_even])
        shared["ev_cab"] = f(np.stack([_conv_pack(np.asarray(inp["ev_conv_a_w"][i]), np.asarray(inp["ev_conv_a_b"][i])) for i in range(n_even)]))
        shared["ev_cbb"] = f(np.stack([_conv_pack(np.asarray(inp["ev_conv_b_w"][i]), np.asarray(inp["ev_conv_b_b"][i])) for i in range(n_even)]))
        shared["ev_gbias"] = f(np.stack([_bc(np.concatenate([np.asarray(inp["ev_ig_b"][i]).ravel(), np.asarray(inp["ev_fg_b"][i]).ravel()])) for i in range(n_even)]))
        shared["ev_hnorm"] = f(np.stack([_bc(inp["ev_hnorm_a"][i]) for i in range(n_even)]))
        shared["ev_dtb"] = f(np.stack([_bc(inp["ev_dt_bias"][i]) for i in range(n_even)]))
        shared["ev_alog"] = f(np.stack([_bc(inp["ev_a_log"][i]) for i in range(n_even)]))
        shared["ev_dskip"] = f(np.stack([_bc(inp["ev_d_skip"][i]) for i in range(n_even)]))
        shared["ev_gnorm"] = f(np.stack([_bc(inp["ev_gnorm_b"][i]) for i in range(n_even)]))
    if n_odd:
        shared["od_w_in"] = f(inp["od_w_in"][:n_odd])
        shared["od_w_out"] = f(inp["od_w_out"][:n_odd])
        shared["od_qkn"] = f(np.stack([np.stack([np.asarray(inp["od_qn_w"][i]).reshape(128, 1), np.asarray(inp["od_kn_w"][i]).reshape(128, 1)], axis=0) for i in range(n_odd)]))
        shared["od_bias"] = f(np.stack([_na_bias(np.asarray(inp["od_rpb"][i])) for i in range(n_odd)]))
        shared["od_cdb"] = f(np.stack([_conv_pack(np.asarray(inp["od_conv_d_w"][i]), np.asarray(inp["od_conv_d_b"][i])) for i in range(n_odd)]))
        shared["od_dtb"] = f(np.stack([_bc(inp["od_dt_bias"][i]) for i in range(n_odd)]))
        shared["od_alog"] = f(np.stack([_bc(inp["od_a_log"][i]) for i in range(n_odd)]))
        shared["od_gnorm"] = f(np.stack([_bc(np.tile(np.asarray(inp["od_gnorm_d"][i]), 16)) for i in range(n_odd)]))
        shared["od_mbias"] = np.concatenate([np.full((16, 1), NEG, np.float32), np.zeros((16, 1), np.float32)])
    meta = np.asarray(inp["meta"], np.float32)
    maps = []
    mask_cache = {}
    for x in seqs:
        T = x.shape[0]
        h0 = np.zeros((Lp, D), np.float32)
        h0[128 - N_META:128] = meta
        h0[128:128 + T] = x
        tok = np.arange(Lp).reshape(NT, 128).T
        real = (tok >= 128 - N_META) & (tok < 128 + T)
        m = dict(shared)
        m["h0"] = h0
        m["tmask"] = np.ascontiguousarray(np.where(real, 1.0, 0.0).astype(np.float32))
        m["negm"] = np.ascontiguousarray(np.where(real, 0.0, NEG).astype(np.float32))
        if n_odd:
            if T not in mask_cache:
                mask_cache[T] = _na_mask(NT, T)
            m["od_mask"] = mask_cache[T]
        maps.append(m)
    return maps


_PROG_CACHE = {}


def run_model(inp, seqs, NT, DEPTH, dbg=None):
    key = (NT, DEPTH, tuple(dbg) if dbg else None)
    if key not in _PROG_CACHE:
        _PROG_CACHE[key] = build_program(NT, DEPTH, dbg)[0]
    nc = _PROG_CACHE[key]
    maps = prep_core_inputs(inp, seqs, NT, DEPTH)
    res = run_bass_kernel_spmd(nc, maps, core_ids=list(range(len(maps))))
    return res.results


def kernel(**inputs):
    xp = np.asarray(inputs["x_prompt"], np.float32)
    xs = np.asarray(inputs["x_sample"], np.float32)
    Ts = xs.shape[1]
    NT = (Ts + 128) // 128
    seqs = [xs[0], xs[1], xp[0], xp[1], xs[0], xs[1], xp[0], xp[1]]
    res = run_model(inputs, seqs, NT, 4)
    Lp = NT * 128
    ys = np.stack([res[0]["y"][128:128 + Ts], res[1]["y"][128:128 + Ts]]).astype(np.float32)
    Tp = xp.shape[1]
    yp = np.stack([res[2]["y"][128:128 + Tp], res[3]["y"][128:128 + Tp]]).astype(np.float32)
    return (yp, ys)
```

```python
import numpy as np
from contextlib import ExitStack
import concourse.bass as bass
import concourse.mybir as mybir
from concourse.bass_utils import run_bass_kernel_spmd

F32 = mybir.dt.float32
BF16 = mybir.dt.bfloat16
AF = mybir.ActivationFunctionType
ALU = mybir.AluOpType

D = 2048
N_META = 16
EPS = 1e-6
NEG = -30000.0
EVEN_SIZES = (1024, 1024, 2048, 2048, 2048, 32, 2048, 2048, 1024, 1024, 64)
ODD_SIZES = (2048, 2048, 2048, 2048, 2048, 2048, 2048, 2048, 64)
E_IN = sum(EVEN_SIZES)
O_IN = sum(ODD_SIZES)


def _offs(sizes):
    o, acc = [], 0
    for s in sizes:
        o.append(acc)
        acc += s
    return o


EOFF = _offs(EVEN_SIZES)
OOFF = _offs(ODD_SIZES)


class Track:
    __slots__ = ("writers", "readers")

    def __init__(self):
        self.writers = {}
        self.readers = {}


class View:
    __slots__ = ("ap", "tr")

    def __init__(self, ap, tr):
        self.ap = ap
        self.tr = tr

    def __getitem__(self, idx):
        return View(self.ap[idx], self.tr)

    def bc(self, shape):
        return View(self.ap.to_broadcast(shape), self.tr)

    def rr(self, s, **kw):
        return View(self.ap.rearrange(s, **kw), self.tr)


class Tile(View):
    def __init__(self, ap):
        View.__init__(self, ap, Track())


class Sched:
    SEM_ROT = 30000

    def __init__(self, nc, n_dma_sems=12):
        self.nc = nc
        self.eng = {"pe": nc.tensor, "act": nc.scalar, "dve": nc.vector,
                    "pool": nc.gpsimd, "sp": nc.sync}
        self.sem = {}
        self.cnt = {}
        self.semid = 0
        for e in self.eng:
            self._new_sem(e)
        self.known = {e: {} for e in self.eng}
        self.dma_sems = {}
        for q in ("sp", "act", "pool"):
            lst = []
            for i in range(n_dma_sems):
                s = nc.alloc_semaphore(name=f"dq_{q}_{i}")
                lst.append([s, 0, f"dq_{q}_{i}"])
            self.dma_sems[q] = lst
        self.dma_rr = {q: 0 for q in self.dma_sems}
        self.ninst = 0

    def _new_sem(self, e):
        self.semid += 1
        key = f"s_{e}_{self.semid}"
        self.sem[e] = (self.nc.alloc_semaphore(name=key), key)
        self.cnt[e] = 0

    def _wait(self, e, key, sem, val):
        k = self.known[e]
        if k.get(key, 0) >= val:
            return
        self.eng[e].wait_ge(sem, val)
        k[key] = val
        self.ninst += 1

    def _deps(self, e, reads, writes):
        mykey = self.sem[e][1]
        for r in reads:
            for key, (sem, val) in r.tr.writers.items():
                if key == mykey and e == "pe":
                    continue
                self._wait(e, key, sem, val)
        for w in writes:
            for key, (sem, val) in w.tr.writers.items():
                if key == mykey:
                    continue
                self._wait(e, key, sem, val)
            for key, (sem, val) in w.tr.readers.items():
                if key == mykey:
                    continue
                self._wait(e, key, sem, val)

    def _record(self, key, sem, val, reads, writes):
        for r in reads:
            r.tr.readers[key] = (sem, val)
        for w in writes:
            w.tr.writers = {key: (sem, val)}
            w.tr.readers = {}

    def op(self, e, fn, reads, writes):
        if self.cnt[e] >= self.SEM_ROT:
            self._new_sem(e)
        self._deps(e, reads, writes)
        inst = fn(self.eng[e])
        sem, key = self.sem[e]
        self.cnt[e] += 1
        inst.then_inc(sem, 1)
        self._record(key, sem, self.cnt[e], reads, writes)
        self.ninst += 1
        return inst

    def dma(self, q, out, in_, **kw):
        lst = self.dma_sems[q]
        i = self.dma_rr[q]
        self.dma_rr[q] = (i + 1) % len(lst)
        ent = lst[i]
        sem, val, key = ent
        if val > 0:
            self._wait(q, key, sem, val)
        self._deps(q, [in_], [out])
        inst = self.eng[q].dma_start(out=out.ap, in_=in_.ap, **kw)
        ent[1] = val + 16
        inst.then_inc(sem, 16)
        self._record(key, sem, ent[1], [in_], [out])
        self.ninst += 1
        return inst

    def drain(self, e, views):
        self._deps(e, views, views)

    def barrier(self, views):
        for e in self.eng:
            self._deps(e, views, views)

    def mm(self, out, lhsT, rhs, start=True, stop=True):
        rd = [lhsT, rhs] + ([] if start else [out])
        return self.op("pe", lambda g: g.matmul(out.ap, lhsT.ap, rhs.ap, start=start, stop=stop), rd, [out])

    def tr(self, out, in_, ident):
        return self.op("pe", lambda g: g.transpose(out.ap, in_.ap, ident.ap), [in_, ident], [out])

    def act(self, out, in_, func, bias=None, scale=None, accum=None):
        rd = [in_]
        kw = {}
        if bias is not None:
            if isinstance(bias, View):
                rd.append(bias)
                kw["bias"] = bias.ap
            else:
                kw["bias"] = bias
        if scale is not None:
            if isinstance(scale, View):
                rd.append(scale)
                kw["scale"] = scale.ap
            else:
                kw["scale"] = scale
        wr = [out]
        if accum is not None:
            wr.append(accum)
            kw["accum_out"] = accum.ap
        return self.op("act", lambda g: g.activation(out.ap, in_.ap, func, **kw), rd, wr)

    def ts(self, e, out, in0, s1, s2, op0, op1=None):
        rd = [in0]
        a1, a2 = s1, s2
        if isinstance(s1, View):
            rd.append(s1)
            a1 = s1.ap
        if isinstance(s2, View):
            rd.append(s2)
            a2 = s2.ap
        if op1 is None:
            return self.op(e, lambda g: g.tensor_scalar(out.ap, in0.ap, a1, a2, op0), rd, [out])
        return self.op(e, lambda g: g.tensor_scalar(out.ap, in0.ap, a1, a2, op0, op1), rd, [out])

    def stt(self, e, out, in0, s, in1, op0, op1):
        e = "dve"
        rd = [in0, in1]
        a = s
        if isinstance(s, View):
            rd.append(s)
            a = s.ap
        return self.op(e, lambda g: g.scalar_tensor_tensor(out.ap, in0.ap, a, in1.ap, op0, op1), rd, [out])

    def tt(self, e, out, in0, in1, op):
        return self.op(e, lambda g: g.tensor_tensor(out.ap, in0.ap, in1.ap, op), [in0, in1], [out])

    def copy(self, e, out, in_):
        if e == "act":
            return self.op(e, lambda g: g.copy(out.ap, in_.ap), [in_], [out])
        return self.op(e, lambda g: g.tensor_copy(out.ap, in_.ap), [in_], [out])

    def memset(self, e, out, val):
        return self.op(e, lambda g: g.memset(out.ap, val), [], [out])

    def recip(self, out, in_):
        return self.op("dve", lambda g: g.reciprocal(out.ap, in_.ap), [in_], [out])


class Pool:
    def __init__(self, tiles):
        self.tiles = tiles
        self.i = 0

    def get(self):
        t = self.tiles[self.i]
        self.i = (self.i + 1) % len(self.tiles)
        return t


C_ID, C_UFW, C_UBW, C_SUFW, C_SUBW, C_NEGFW, C_NEGBW, C_ONES, C_SFW, C_SBW = range(10)
N_CONST = 10


def make_consts():
    p = np.arange(128)[:, None]
    f = np.arange(128)[None, :]
    c = np.zeros((N_CONST, 128, 128), np.float32)
    c[C_ID] = (p == f)
    c[C_UFW] = (p <= f)
    c[C_UBW] = (p >= f)
    c[C_SUFW] = (p > f)
    c[C_SUBW] = (p < f)
    c[C_NEGFW] = np.where(p <= f, 0.0, NEG)
    c[C_NEGBW] = np.where(p >= f, 0.0, NEG)
    c[C_ONES] = 1.0
    c[C_SFW] = (p < f)
    c[C_SBW] = (p > f)
    return c


def build_program(NT, DEPTH, dbg=None):
    Lp = NT * 128
    nc = bass.Bass("TRN2", target_bir_lowering=False)
    S = Sched(nc)
    n_even = (DEPTH + 1) // 2
    n_odd = DEPTH // 2

    def din(name, shape, dt=F32):
        return Tile(nc.dram_tensor(name, list(shape), dt, kind="ExternalInput").ap())

    def dscratch(name, shape, dt=F32):
        return Tile(nc.dram_tensor(name, list(shape), dt).ap())

    H0 = din("h0", [Lp, D])
    TMASK = din("tmask", [128, NT])
    NEGM = din("negm", [128, NT])
    CONSTS = din("consts", [N_CONST, 128, 128])
    NORMW = din("normw", [DEPTH, 128, 16])
    EV = {}
    OD = {}
    if n_even:
        EV["w_in"] = din("ev_w_in", [n_even, D, E_IN])
        EV["w_out"] = din("ev_w_out", [n_even, 4096, D])
        EV["cab"] = din("ev_cab", [n_even, 16, 128, 6])
        EV["cbb"] = din("ev_cbb", [n_even, 32, 128, 6])
        EV["gbias"] = din("ev_gbias", [n_even, 128, 32])
        EV["hnorm"] = din("ev_hnorm", [n_even, 128, 2048])
        EV["dtb"] = din("ev_dtb", [n_even, 128, 64])
        EV["alog"] = din("ev_alog", [n_even, 128, 64])
        EV["dskip"] = din("ev_dskip", [n_even, 128, 32])
        EV["gnorm"] = din("ev_gnorm", [n_even, 128, 2048])
    if n_odd:
        OD["w_in"] = din("od_w_in", [n_odd, D, O_IN])
        OD["w_out"] = din("od_w_out", [n_odd, 4096, D])
        OD["qkn"] = din("od_qkn", [n_odd, 2, 128, 1])
        OD["bias"] = din("od_bias", [n_odd, 16, 128, 7 * 128])
        OD["mask"] = din("od_mask", [NT, 128, 7 * 128])
        OD["mbias"] = din("od_mbias", [32, 1])
        OD["cdb"] = din("od_cdb", [n_odd, 48, 128, 6])
        OD["dtb"] = din("od_dtb", [n_odd, 128, 32])
        OD["alog"] = din("od_alog", [n_odd, 128, 32])
        OD["gnorm"] = din("od_gnorm", [n_odd, 128, 2048])
    Y = Tile(nc.dram_tensor("y", [Lp, D], F32, kind="ExternalOutput").ap())
    DBG = {}
    if dbg:
        for name, shape, dt in dbg:
            DBG[name] = Tile(nc.dram_tensor("dbg_" + name, list(shape), dt, kind="ExternalOutput").ap())

    HB = [dscratch("hb0", [Lp, D]), dscratch("hb1", [Lp, D])]
    class RowSplit:
        def __init__(self, name, nblk, dt):
            self.blk = [dscratch(f"{name}{b}", [2048, Lp], dt) for b in range(nblk)]

        def __getitem__(self, idx):
            r, c = idx
            b = r.start // 2048
            assert (r.stop - 1) // 2048 == b
            return self.blk[b][r.start - b * 2048:r.stop - b * 2048, c]

    class ColSplit:
        def __init__(self, name, cut, width, dt):
            self.cut = cut
            self.a = dscratch(name + "a", [Lp, cut], dt)
            self.b = dscratch(name + "b", [Lp, width - cut], dt)

        def __getitem__(self, idx):
            r, c = idx
            if c.start >= self.cut:
                return self.b[r, c.start - self.cut:c.stop - self.cut]
            assert c.stop <= self.cut
            return self.a[r, c]

    ZF = RowSplit("zf", 5, BF16)
    ZB = RowSplit("zb", 5, BF16)
    ZT = ColSplit("zt", 6144, 8320, F32)
    YF = dscratch("yf", [Lp, 4096])
    MT = dscratch("mt", [4096, Lp], BF16)

    es = ExitStack()
    uid = [0]

    def sb(name, shape, dt=F32):
        return Tile(es.enter_context(nc.sbuf_tensor("g_" + name, list(shape), dt)).ap())

    cf = sb("cf", [128, N_CONST, 128], F32)
    cb = sb("cb", [128, N_CONST, 128], BF16)
    tmask = sb("tmask", [128, NT], F32)
    negm = sb("negm", [128, NT], F32)
    S.dma("sp", cf, CONSTS.rr("c p f -> p c f"))
    S.dma("sp", tmask, TMASK)
    S.dma("sp", negm, NEGM)
    S.copy("dve", cb, cf)

    def CF(i):
        return cf[:, i, :]

    def CB(i):
        return cb[:, i, :]

    psA = [Tile(nc.alloc_psum_tensor(f"psA{i}", [128, 512], F32).ap()) for i in range(5)]
    psB = [Tile(nc.alloc_psum_tensor(f"psB{i}", [128, 1024], BF16).ap()) for i in range(2)]
    psC = Tile(nc.alloc_psum_tensor("psC", [128, 512], F32).ap())
    PA = Pool(psA)
    PB = Pool(psB)

    evac_rr = [0]

    def evac(out, in_):
        evac_rr[0] ^= 1
        S.copy("act" if evac_rr[0] else "dve", out, in_)

    def phaseA(layer, Hin, Win, fm_groups, tm_groups):
        with ExitStack() as st:
            ltiles = []

            def lsb(name, shape, dt=F32):
                uid[0] += 1
                t_ = Tile(st.enter_context(nc.sbuf_tensor(f"{name}_{uid[0]}", list(shape), dt)).ap())
                ltiles.append(t_)
                return t_
            st.callback(lambda: S.barrier(ltiles + psA + psB + [psC]))
            SBT = 16
            hnT = lsb("hnT", [128, 16, SBT * 128], BF16)
            ht = Pool([lsb(f"ht{i}", [128, D]) for i in range(2)])
            hnb = Pool([lsb(f"hnb{i}", [128, D], BF16) for i in range(2)])
            junk = lsb("junk", [128, D], BF16)
            st4 = Pool([lsb(f"st4_{i}", [128, 4]) for i in range(2)])
            wst = Pool([lsb(f"wst{i}", [128, 16, 256]) for i in range(2)])
            wbf = Pool([lsb(f"wbf{i}", [128, 16, 256], BF16) for i in range(2)])
            stf = Pool([lsb(f"stf{i}", [128, SBT * 128], BF16) for i in range(2)])
            stt_ = Pool([lsb(f"stt{i}", [128, 256]) for i in range(4)])
            for t_ in stt_.tiles:
                S.memset("pool", t_, 0.0)
            nw = lsb("nw", [128, 16])
            S.dma("sp", nw, NORMW[layer])
            Wv = Win.rr("(kc p) n -> p kc n", p=128)
            for sb0 in range(0, NT, SBT):
                tiles = list(range(sb0, min(sb0 + SBT, NT)))
                ntok = len(tiles) * 128
                for j in tiles:
                    h = ht.get()
                    S.dma("sp", h, Hin[j * 128:(j + 1) * 128, :])
                    s4 = st4.get()
                    S.act(junk, h, AF.Square, accum=s4[:, 0:1])
                    S.act(s4[:, 1:2], s4[:, 0:1], AF.Sqrt, bias=EPS, scale=1.0 / D)
                    S.recip(s4[:, 2:3], s4[:, 1:2])
                    S.tt("dve", s4[:, 3:4], s4[:, 2:3], tmask[:, j:j + 1], ALU.mult)
                    hb = hnb.get()
                    S.act(hb, h, AF.Copy, scale=s4[:, 3:4])
                    jj = j - sb0
                    for half in range(2):
                        pb = PB.get()
                        for k in range(8):
                            kc = half * 8 + k
                            S.tr(pb[:, k * 128:(k + 1) * 128], hb[:, kc * 128:(kc + 1) * 128], CB(C_ID))
                        evac(hnT[:, half * 8:(half + 1) * 8, jj * 128:(jj + 1) * 128],
                             pb.rr("p (k t) -> p k t", k=8))
                for (c0, r0) in fm_groups:
                    w = wst.get()
                    S.dma("sp", w[:, :, 0:128], Wv[:, :, c0:c0 + 128])
                    wb = wbf.get()
                    S.tt("pool", wb[:, :, 0:128], w[:, :, 0:128], nw.rr("p (k o) -> p k o", o=1).bc([128, 16, 128]), ALU.mult)
                    stg = stf.get()
                    for q0 in range(0, ntok, 512):
                        n = min(512, ntok - q0)
                        ps = PA.get()
                        for kc in range(16):
                            S.mm(ps[:, 0:n], wb[:, kc, 0:128], hnT[:, kc, q0:q0 + n], start=(kc == 0), stop=(kc == 15))
                        evac(stg[:, q0:q0 + n], ps[:, 0:n])
                    S.dma("act", ZF[r0:r0 + 128, sb0 * 128:sb0 * 128 + ntok], stg[:, 0:ntok])
                for (pieces, wd, z0) in tm_groups:
                    w = wst.get()
                    for (c0, pw, off) in pieces:
                        S.dma("sp", w[:, :, off:off + pw], Wv[:, :, c0:c0 + pw])
                    used = max(off + pw for (c0, pw, off) in pieces)
                    wb = wbf.get()
                    S.tt("pool", wb[:, :, 0:used], w[:, :, 0:used], nw.rr("p (k o) -> p k o", o=1).bc([128, 16, used]), ALU.mult)
                    for j in tiles:
                        jj = j - sb0
                        ps = PA.get()
                        for kc in range(16):
                            S.mm(ps[:, 0:used], hnT[:, kc, jj * 128:(jj + 1) * 128], wb[:, kc, 0:used], start=(kc == 0), stop=(kc == 15))
                        sg = stt_.get()
                        evac(sg[:, 0:used], ps[:, 0:used])
                        S.dma("act", ZT[j * 128:(j + 1) * 128, z0:z0 + wd], sg[:, 0:wd])

    def phaseA2(jobs):
        with ExitStack() as st:
            ltiles = []

            def lsb(name, shape, dt=F32):
                uid[0] += 1
                t_ = Tile(st.enter_context(nc.sbuf_tensor(f"{name}_{uid[0]}", list(shape), dt)).ap())
                ltiles.append(t_)
                return t_
            st.callback(lambda: S.barrier(ltiles + psA + psB + [psC]))
            xin = Pool([lsb(f"xin{i}", [128, Lp + 4], BF16) for i in range(3)])
            ob = Pool([lsb(f"ob{i}", [128, Lp], BF16) for i in range(3)])
            cw = Pool([lsb(f"cw{i}", [128, 8]) for i in range(3)])
            dgp = Pool([lsb(f"dg{i}", [128, 5, 128], BF16) for i in range(3)])
            tmpf = Pool([lsb(f"tf{i}", [128, 512]) for i in range(3)])
            sq = Pool([lsb(f"sq{i}", [128, 512], BF16) for i in range(3)])
            rs = Pool([lsb(f"rs{i}", [128, 512]) for i in range(3)])
            for x in xin.tiles:
                S.memset("pool", x[:, 0:2], 0.0)
                S.memset("pool", x[:, Lp + 2:Lp + 4], 0.0)
            for ji, job in enumerate(jobs):
                r0 = job["row"]
                x = xin.get()
                S.dma("sp", x[:, 2:Lp + 2], ZF[r0:r0 + 128, 0:Lp])
                c = cw.get()
                o = ob.get()
                conv = job["kind"] == "conv"
                nm = job["norm"]
                if conv:
                    S.dma("sp", c[:, 0:6], job["cw"])
                    dg = dgp.get()
                    for k in range(5):
                        S.ts("pool", dg[:, k, :], CB(C_ID), c[:, k:k + 1], None, ALU.mult)
                else:
                    S.dma("sp", c[:, 0:1], job["wcol"])
                for q0 in range(0, Lp, 512):
                    n = min(512, Lp - q0)
                    if conv:
                        ps = PA.get()
                        for k in range(5):
                            S.mm(ps[:, 0:n], dg[:, k, :], x[:, q0 + k:q0 + k + n], start=(k == 0), stop=(k == 4))
                        if nm is None:
                            S.act(o[:, q0:q0 + n], ps[:, 0:n], AF.Silu, bias=c[:, 5:6])
                            continue
                        t = tmpf.get()
                        S.act(t[:, 0:n], ps[:, 0:n], AF.Silu, bias=c[:, 5:6])
                        src = t[:, 0:n]
                    else:
                        src = x[:, 2 + q0:2 + q0 + n]
                    s2 = sq.get()
                    S.act(s2[:, 0:n], src, AF.Square)
                    ps2 = PA.get()
                    S.mm(ps2[:, 0:n], CB(C_ONES), s2[:, 0:n])
                    r = rs.get()
                    if nm == "l2q":
                        S.act(r[:, 0:n], ps2[:, 0:n], AF.Sqrt, bias=128.0 * EPS, scale=128.0)
                    elif nm == "l2k":
                        S.act(r[:, 0:n], ps2[:, 0:n], AF.Sqrt, bias=EPS, scale=1.0)
                    elif nm == "rmsq":
                        S.act(r[:, 0:n], ps2[:, 0:n], AF.Sqrt, bias=128.0 * EPS, scale=1.0)
                    else:
                        S.act(r[:, 0:n], ps2[:, 0:n], AF.Sqrt, bias=EPS, scale=1.0 / 128.0)
                    S.recip(r[:, 0:n], r[:, 0:n])
                    if conv:
                        S.tt("dve", o[:, q0:q0 + n], src, r[:, 0:n], ALU.mult)
                    else:
                        t = tmpf.get()
                        S.act(t[:, 0:n], src, AF.Copy, scale=c[:, 0:1])
                        S.tt("dve", o[:, q0:q0 + n], t[:, 0:n], r[:, 0:n], ALU.mult)
                S.dma("act", ZB[r0:r0 + 128, 0:Lp], o)

    def dirconst(d):
        if d == 0:
            return C_UFW, C_SUFW, C_NEGFW
        return C_UBW, C_SUBW, C_NEGBW

    def phaseC(Wout, Hin, Hout):
        with ExitStack() as st:
            ltiles = []

            def lsb(name, shape, dt=F32):
                uid[0] += 1
                t_ = Tile(st.enter_context(nc.sbuf_tensor(f"{name}_{uid[0]}", list(shape), dt)).ap())
                ltiles.append(t_)
                return t_
            st.callback(lambda: S.barrier(ltiles + psA + psB + [psC]))
            wst = Pool([lsb(f"cwst{i}", [128, 8, 512]) for i in range(2)])
            wb = lsb("cwb", [128, 32, 512], BF16)
            mt = Pool([lsb(f"cmt{i}", [128, 32, 128], BF16) for i in range(3)])
            hh = Pool([lsb(f"chh{i}", [128, 512]) for i in range(3)])
            Wv = Wout.rr("(kc p) n -> p kc n", p=128)
            MTv = MT.rr("(kc p) t -> p kc t", p=128)
            for cg in range(4):
                c0 = cg * 512
                for k4 in range(4):
                    w = wst.get()
                    S.dma("sp", w, Wv[:, k4 * 8:(k4 + 1) * 8, c0:c0 + 512])
                    S.copy("pool", wb[:, k4 * 8:(k4 + 1) * 8, :], w)
                for j in range(NT):
                    m = mt.get()
                    S.dma("sp", m, MTv[:, :, j * 128:(j + 1) * 128])
                    h = hh.get()
                    S.dma("sp", h, Hin[j * 128:(j + 1) * 128, c0:c0 + 512])
                    ps = PA.get()
                    for kc in range(32):
                        S.mm(ps, m[:, kc, :], wb[:, kc, :], start=(kc == 0), stop=(kc == 31))
                    S.tt("dve", h, h, ps, ALU.add)
                    S.dma("act", Hout[j * 128:(j + 1) * 128, c0:c0 + 512], h)

    ctx = dict(nc=nc, S=S, NT=NT, Lp=Lp, CF=CF, CB=CB, PA=PA, PB=PB, psC=psC, tmask=tmask, negm=negm,
               ZF=ZF, ZB=ZB, ZT=ZT, YF=YF, MT=MT, evac=evac, dirconst=dirconst, EV=EV, OD=OD, DBG=DBG)

    Hcur = H0
    for layer in range(DEPTH):
        i = layer // 2
        last = (layer == DEPTH - 1)
        Hout = Y if last else HB[layer % 2]
        if layer % 2 == 0:
            fm, tm = even_groups()
            phaseA(layer, Hcur, EV["w_in"][i], fm, tm)
            phaseA2(even_jobs(EV, i))
            mlstm_phase(ctx, i)
            ssd_phase(ctx, i)
            phaseC(EV["w_out"][i], Hcur, Hout)
        else:
            fm, tm = odd_groups()
            phaseA(layer, Hcur, OD["w_in"][i], fm, tm)
            phaseA2(odd_jobs(OD, i))
            na_phase(ctx, i)
            gdn_phase(ctx, i)
            phaseC(OD["w_out"][i], Hcur, Hout)
        Hcur = Hout
    scratch = dict(yf=YF, mt=MT)
    for name in DBG:
        S.dma("sp", DBG[name], scratch[name])
    for e in ("sp", "act", "pool"):
        S.drain(e, [Y] + list(DBG.values()))
    es.close()
    return nc, S


EV_ZF = dict(q=0, k=1024, x=2048, b=4096, c=5120)
EV_ZT = dict(v=0, o=2048, z=4096, zb=6144, g=8192, dt=8192 + 32)
OD_ZF = dict(qc=0, kc=2048, qd=4096, kd=6144, vd=8192)
OD_ZT = dict(vc=0, zc=2048, zd=4096, g=6144)


def even_groups():
    fm = []
    for name, src, n in (("q", 0, 1024), ("k", 1, 1024), ("x", 7, 2048), ("b", 8, 1024), ("c", 9, 1024)):
        for g in range(n // 128):
            fm.append((EOFF[src] + g * 128, EV_ZF[name] + g * 128))
    tm = []
    for name, src, n in (("v", 2, 2048), ("o", 3, 2048), ("z", 4, 2048), ("zb", 6, 2048)):
        for g in range(n // 256):
            tm.append(([(EOFF[src] + g * 256, 256, 0)], 256, EV_ZT[name] + g * 256))
    tm.append(([(EOFF[5], 32, 0), (EOFF[10], 64, 32)], 128, EV_ZT["g"]))
    return fm, tm


def odd_groups():
    fm = []
    for name, src in (("qc", 0), ("kc", 1), ("qd", 4), ("kd", 5), ("vd", 6)):
        for g in range(16):
            fm.append((OOFF[src] + g * 128, OD_ZF[name] + g * 128))
    tm = []
    for name, src in (("vc", 2), ("zc", 3), ("zd", 7)):
        for g in range(8):
            tm.append(([(OOFF[src] + g * 256, 256, 0)], 256, OD_ZT[name] + g * 256))
    tm.append(([(OOFF[8], 64, 0)], 128, OD_ZT["g"]))
    return fm, tm


def even_jobs(EV, i):
    jobs = []
    for g in range(8):
        jobs.append(dict(row=EV_ZF["q"] + g * 128, kind="conv", cw=EV["cab"][i, g], norm=None, post=1.0))
    for g in range(8):
        jobs.append(dict(row=EV_ZF["k"] + g * 128, kind="conv", cw=EV["cab"][i, 8 + g], norm=None, post=1.0))
    for g in range(32):
        jobs.append(dict(row=EV_ZF["x"] + g * 128, kind="conv", cw=EV["cbb"][i, g], norm=None, post=1.0))
    return jobs


def odd_jobs(OD, i):
    jobs = []
    for g in range(16):
        jobs.append(dict(row=OD_ZF["qc"] + g * 128, kind="rms", norm="rmsq", wcol=OD["qkn"][i, 0], post=1.0))
    for g in range(16):
        jobs.append(dict(row=OD_ZF["kc"] + g * 128, kind="rms", norm="rmsk", wcol=OD["qkn"][i, 1], post=1.0))
    for g in range(16):
        jobs.append(dict(row=OD_ZF["qd"] + g * 128, kind="conv", cw=OD["cdb"][i, g], norm="l2q", post=1.0))
    for g in range(16):
        jobs.append(dict(row=OD_ZF["kd"] + g * 128, kind="conv", cw=OD["cdb"][i, 16 + g], norm="l2k", post=1.0))
    for g in range(16):
        jobs.append(dict(row=OD_ZF["vd"] + g * 128, kind="conv", cw=OD["cdb"][i, 32 + g], norm=None, post=1.0))
    return jobs


def _softplus(S, out, in_, tmp):
    S.act(tmp, in_, AF.Exp)
    S.act(out, tmp, AF.Ln, bias=1.0, scale=1.0)


def _transpose_store(ctx, src_bf, MTrow0, c, mtp):
    S, PB, CB, MT, evac = ctx["S"], ctx["PB"], ctx["CB"], ctx["MT"], ctx["evac"]
    m = mtp.get()
    for half in range(2):
        pb = PB.get()
        for k in range(8):
            kc = half * 8 + k
            S.tr(pb[:, k * 128:(k + 1) * 128], src_bf[:, kc * 128:(kc + 1) * 128], CB(C_ID))
        evac(m[:, half * 8:(half + 1) * 8, :], pb.rr("p (k t) -> p k t", k=8))
    S.dma("act", MT[MTrow0:MTrow0 + 2048, c * 128:(c + 1) * 128].rr("(k p) t -> p k t", p=128), m)


def _group_rms(ctx, y, ngrp, gsz, scr, st8, wtile, out_bf):
    S = ctx["S"]
    S.act(scr, y, AF.Square)
    S.op("dve", lambda g: g.tensor_reduce(st8.ap[:, 0:ngrp], scr.ap.rearrange("p (g c) -> p g c", g=ngrp),
                                          mybir.AxisListType.X, ALU.add), [scr], [st8])
    S.act(st8[:, 0:ngrp], st8[:, 0:ngrp], AF.Sqrt, bias=EPS, scale=1.0 / gsz)
    S.recip(st8[:, 0:ngrp], st8[:, 0:ngrp])
    S.tt("dve", scr.rr("p (g c) -> p g c", g=ngrp), y.rr("p (g c) -> p g c", g=ngrp),
         st8[:, 0:ngrp].rr("p (g o) -> p g o", o=1).bc([128, ngrp, gsz]), ALU.mult)
    S.tt("pool", out_bf, scr, wtile, ALU.mult)


def mlstm_phase(ctx, i):
    nc, S, NT = ctx["nc"], ctx["S"], ctx["NT"]
    CF, CB, PA, PB, psC = ctx["CF"], ctx["CB"], ctx["PA"], ctx["PB"], ctx["psC"]
    tmask, negm, ZB, ZT, YF, EV = ctx["tmask"], ctx["negm"], ctx["ZB"], ctx["ZT"], ctx["YF"], ctx["EV"]
    with ExitStack() as st:
        ltiles = []

        def lsb(name, shape, dt=F32):
            t_ = Tile(st.enter_context(nc.sbuf_tensor(f"a{i}_" + name, list(shape), dt)).ap())
            ltiles.append(t_)
            return t_
        st.callback(lambda: S.barrier(ltiles + PA.tiles + PB.tiles + [psC]))
        qk = Pool([lsb(f"qk{j}", [128, 16, 128], BF16) for j in range(2)])
        vf = Pool([lsb(f"vf{j}", [128, 2048]) for j in range(2)])
        vb = Pool([lsb(f"vb{j}", [128, 8, 257], BF16) for j in range(2)])
        gt = Pool([lsb(f"g{j}", [128, 128]) for j in range(2)])
        gw = Pool([lsb(f"gw{j}", [128, 96]) for j in range(2)])
        lbp = Pool([lsb(f"lb{j}", [128, 128]) for j in range(3)])
        dtp = Pool([lsb(f"dt{j}", [128, 128]) for j in range(3)])
        spp = Pool([lsb(f"sp{j}", [128, 128], BF16) for j in range(3)])
        kwp = Pool([lsb(f"kw{j}", [128, 128], BF16) for j in range(3)])
        ysp = Pool([lsb(f"ys{j}", [128, 260]) for j in range(3)])
        yst = Pool([lsb(f"yst{j}", [128, 2048]) for j in range(2)])
        yfl = Pool([lsb(f"yfl{j}", [128, 2048]) for j in range(2)])
        oz = Pool([lsb(f"oz{j}", [128, 4096]) for j in range(2)])
        scr = lsb("scr", [128, 2048])
        st8 = lsb("st8", [128, 8])
        mbf = Pool([lsb(f"mbf{j}", [128, 2048], BF16) for j in range(2)])
        mtp = Pool([lsb(f"mtp{j}", [128, 16, 128], BF16) for j in range(2)])
        X = lsb("X", [128, 8, 257])
        Xb = lsb("Xb", [128, 8, 257], BF16)
        gbias = lsb("gbias", [128, 32])
        hnorm = lsb("hnorm", [128, 2048])
        S.dma("sp", gbias, EV["gbias"][i])
        S.dma("sp", hnorm, EV["hnorm"][i])
        for v in vb.tiles:
            S.memset("pool", v[:, :, 256:257], 1.0)
        for d in (0, 1):
            cU, cSU, cNEG = ctx["dirconst"](d)
            S.memset("pool", X, 0.0)
            S.memset("pool", Xb, 0.0)
            order = range(NT) if d == 0 else range(NT - 1, -1, -1)
            for c in order:
                tok = slice(c * 128, (c + 1) * 128)
                q = qk.get()
                S.dma("sp", q, ZB[0:2048, tok].rr("(h p) t -> p h t", p=128))
                v32 = vf.get()
                S.dma("sp", v32, ZT[tok, 0:2048])
                v = vb.get()
                S.copy("pool", v[:, :, 0:256], v32.rr("p (h c) -> p h c", h=8))
                g = gt.get()
                S.dma("sp", g, ZT[tok, 8192:8320])
                w = gw.get()
                S.tt("dve", w[:, 0:8], g[:, d * 8:d * 8 + 8], gbias[:, d * 8:d * 8 + 8], ALU.add)
                S.ts("dve", w[:, 0:8], w[:, 0:8], negm[:, c:c + 1], -0.5 * float(np.log(128.0)), ALU.add, ALU.add)
                S.tt("dve", w[:, 8:16], g[:, 16 + d * 8:24 + d * 8], gbias[:, 16 + d * 8:24 + d * 8], ALU.add)
                S.act(w[:, 16:24], w[:, 8:16], AF.Exp, scale=-1.0)
                S.act(w[:, 16:24], w[:, 16:24], AF.Ln, bias=1.0, scale=1.0)
                S.ts("dve", w[:, 8:16], w[:, 16:24], tmask[:, c:c + 1], -1.0, ALU.mult, ALU.mult)
                S.mm(psC[:, 0:8], CF(cU), w[:, 8:16])
                S.mm(psC[:, 8:16], CF(C_ONES), w[:, 8:16])
                S.copy("dve", w[:, 24:40], psC[:, 0:16])
                S.act(w[:, 40:56], w[:, 24:40], AF.Exp)
                S.tt("dve", w[:, 56:64], w[:, 32:40], w[:, 24:32], ALU.subtract)
                S.tt("dve", w[:, 56:64], w[:, 56:64], w[:, 0:8], ALU.add)
                S.act(w[:, 64:72], w[:, 56:64], AF.Exp)
                if d == 0:
                    yo = yst.get()
                else:
                    yo = yst.get()
                    yf = yfl.get()
                    S.dma("sp", yf, YF[tok, 0:2048])
                def stage1(h):
                    qT = q[:, h, :]
                    kT = q[:, 8 + h, :]
                    lb = lbp.get()
                    S.act(lb, CF(cSU), AF.Copy, scale=w[:, 8 + h:9 + h])
                    psD = PA.get()
                    S.mm(psD[:, 0:128], lb, CF(cU), start=True, stop=False)
                    S.mm(psD[:, 0:128], CF(C_ID), CF(cNEG), start=False, stop=True)
                    S.mm(psD[:, 128:256], kT, qT)
                    dtm = dtp.get()
                    S.act(dtm, psD[:, 0:128], AF.Exp, bias=w[:, h:h + 1], scale=1.0)
                    sp = spp.get()
                    S.tt("dve", sp, psD[:, 128:256], dtm, ALU.mult)
                    pt = PB.get()
                    S.tr(pt[:, 0:128], kT, CB(C_ID))
                    kw = kwp.get()
                    S.act(kw, pt[:, 0:128], AF.Copy, scale=w[:, 64 + h:65 + h])
                    return sp, kw

                def stage2(h, sp, kw):
                    qT = q[:, h, :]
                    ps1 = PA.get()
                    S.mm(ps1[:, 0:257], sp, v[:, h, :])
                    ps2 = PA.get()
                    S.mm(ps2[:, 0:257], qT, Xb[:, h, :])
                    ys = ysp.get()
                    S.act(ys[:, 0:257], ps2[:, 0:257], AF.Copy, scale=w[:, 40 + h:41 + h])
                    S.tt("dve", ys[:, 0:257], ys[:, 0:257], ps1[:, 0:257], ALU.add)
                    S.act(ys[:, 257:258], ys[:, 256:257], AF.Abs)
                    S.ts("dve", ys[:, 258:259], ys[:, 257:258], 1.0, None, ALU.max)
                    S.recip(ys[:, 259:260], ys[:, 258:259])
                    if d == 0:
                        S.ts("dve", yo[:, h * 256:(h + 1) * 256], ys[:, 0:256], ys[:, 259:260], None, ALU.mult)
                    else:
                        S.stt("dve", yo[:, h * 256:(h + 1) * 256], ys[:, 0:256], ys[:, 259:260],
                              yf[:, h * 256:(h + 1) * 256], ALU.mult, ALU.add)
                    ps3 = PA.get()
                    S.mm(ps3[:, 0:257], kw, v[:, h, :])
                    S.stt("dve", X[:, h, :], X[:, h, :], w[:, 48 + h:49 + h], ps3[:, 0:257], ALU.mult, ALU.add)
                    S.copy("act", Xb[:, h, :], X[:, h, :])

                pend = None
                for h in range(9):
                    cur = stage1(h) if h < 8 else None
                    if pend is not None:
                        stage2(h - 1, *pend)
                    pend = cur
                if d == 0:
                    S.dma("act", YF[tok, 0:2048], yo)
                else:
                    o_z = oz.get()
                    S.dma("sp", o_z, ZT[tok, 2048:6144])
                    mb = mbf.get()
                    _group_rms(ctx, yo, 8, 256, scr, st8, hnorm, mb)
                    S.act(o_z[:, 0:2048], o_z[:, 0:2048], AF.Sigmoid)
                    S.act(o_z[:, 2048:4096], o_z[:, 2048:4096], AF.Silu)
                    S.tt("pool", o_z[:, 0:2048], o_z[:, 0:2048], o_z[:, 2048:4096], ALU.mult)
                    S.tt("dve", mb, mb, o_z[:, 0:2048], ALU.mult)
                    _transpose_store(ctx, mb, 0, c, mtp)


def ssd_phase(ctx, i):
    nc, S, NT = ctx["nc"], ctx["S"], ctx["NT"]
    CF, CB, PA, PB, psC = ctx["CF"], ctx["CB"], ctx["PA"], ctx["PB"], ctx["psC"]
    tmask, ZB, ZT, YF, EV, evac = ctx["tmask"], ctx["ZB"], ctx["ZT"], ctx["YF"], ctx["EV"], ctx["evac"]
    with ExitStack() as st:
        ltiles = []

        def lsb(name, shape, dt=F32):
            t_ = Tile(st.enter_context(nc.sbuf_tensor(f"b{i}_" + name, list(shape), dt)).ap())
            ltiles.append(t_)
            return t_
        st.callback(lambda: S.barrier(ltiles + PA.tiles + PB.tiles + [psC]))
        cbt = Pool([lsb(f"cbt{j}", [128, 16, 128], BF16) for j in range(2)])
        xt = Pool([lsb(f"xt{j}", [128, 16, 128], BF16) for j in range(2)])
        dtr = Pool([lsb(f"dtr{j}", [128, 128]) for j in range(2)])
        gw = Pool([lsb(f"gw{j}", [128, 256]) for j in range(2)])
        xs = Pool([lsb(f"xs{j}", [128, 2048], BF16) for j in range(2)])
        xdt = Pool([lsb(f"xdt{j}", [128, 32, 64], BF16) for j in range(2)])
        xw = Pool([lsb(f"xw{j}", [128, 32, 64], BF16) for j in range(2)])
        btm = Pool([lsb(f"btm{j}", [128, 8, 128], BF16) for j in range(2)])
        cbs = Pool([lsb(f"cbs{j}", [128, 128]) for j in range(2)])
        lbp = Pool([lsb(f"lb{j}", [128, 128]) for j in range(8)])
        dtp = Pool([lsb(f"dt{j}", [128, 512]) for j in range(3)])
        spp = Pool([lsb(f"sp{j}", [128, 4, 128], BF16) for j in range(3)])
        y1p = Pool([lsb(f"y1{j}", [128, 256]) for j in range(2)])
        yst = Pool([lsb(f"yst{j}", [128, 2048]) for j in range(2)])
        yfl = Pool([lsb(f"yfl{j}", [128, 2048]) for j in range(2)])
        zb = Pool([lsb(f"zb{j}", [128, 2048]) for j in range(2)])
        scr = lsb("scr", [128, 2048])
        st8 = lsb("st8", [128, 8])
        mbf = Pool([lsb(f"mbf{j}", [128, 2048], BF16) for j in range(2)])
        mtp = Pool([lsb(f"mtp{j}", [128, 16, 128], BF16) for j in range(2)])
        St = lsb("St", [128, 8, 256])
        Sb = lsb("Sb", [128, 8, 256], BF16)
        dtb = lsb("dtb", [128, 64])
        nA = lsb("nA", [128, 64])
        dskip = lsb("dskip", [128, 32])
        gnorm = lsb("gnorm", [128, 2048])
        S.dma("sp", dtb, EV["dtb"][i])
        S.dma("sp", nA, EV["alog"][i])
        S.dma("sp", dskip, EV["dskip"][i])
        S.dma("sp", gnorm, EV["gnorm"][i])
        S.act(nA, nA, AF.Exp)
        S.ts("dve", nA, nA, -1.0, None, ALU.mult)
        for d in (0, 1):
            cU, cSU, cNEG = ctx["dirconst"](d)
            S.memset("pool", St, 0.0)
            S.memset("pool", Sb, 0.0)
            order = range(NT) if d == 0 else range(NT - 1, -1, -1)
            for c in order:
                tok = slice(c * 128, (c + 1) * 128)
                cb_ = cbt.get()
                S.dma("sp", cb_, ZB[4096:6144, tok].rr("(g p) t -> p g t", p=128))
                x_ = xt.get()
                S.dma("sp", x_, ZB[2048:4096, tok].rr("(g p) t -> p g t", p=128))
                dr = dtr.get()
                S.dma("sp", dr, ZT[tok, 8192:8320])
                w = gw.get()
                S.tt("dve", w[:, 32:64], dr[:, 32 + d * 32:64 + d * 32], dtb[:, d * 32:(d + 1) * 32], ALU.add)
                S.act(w[:, 32:64], w[:, 32:64], AF.Exp)
                S.act(w[:, 32:64], w[:, 32:64], AF.Ln, bias=1.0, scale=1.0)
                S.ts("dve", w[:, 0:32], w[:, 32:64], tmask[:, c:c + 1], None, ALU.mult)
                S.tt("dve", w[:, 32:64], w[:, 0:32], nA[:, d * 32:(d + 1) * 32], ALU.mult)
                S.mm(psC[:, 0:32], CF(cU), w[:, 32:64])
                S.mm(psC[:, 32:64], CF(C_ONES), w[:, 32:64])
                S.copy("dve", w[:, 64:128], psC[:, 0:64])
                S.act(w[:, 128:192], w[:, 64:128], AF.Exp)
                S.tt("dve", w[:, 192:224], w[:, 96:128], w[:, 64:96], ALU.subtract)
                S.act(w[:, 192:224], w[:, 192:224], AF.Exp)
                xs_ = xs.get()
                xd = xdt.get()
                xw_ = xw.get()
                for half in range(2):
                    pb = PB.get()
                    for k in range(8):
                        S.tr(pb[:, k * 128:(k + 1) * 128], x_[:, half * 8 + k, :], CB(C_ID))
                    evac(xs_[:, half * 1024:(half + 1) * 1024], pb)
                S.tt("dve", xd, xs_.rr("p (h c) -> p h c", h=32),
                     w[:, 0:32].rr("p (h o) -> p h o", o=1).bc([128, 32, 64]), ALU.mult)
                S.tt("pool", xw_, xd, w[:, 192:224].rr("p (h o) -> p h o", o=1).bc([128, 32, 64]), ALU.mult)
                bt_ = btm.get()
                pb = PB.get()
                for g in range(8):
                    S.tr(pb[:, g * 128:(g + 1) * 128], cb_[:, g, :], CB(C_ID))
                evac(bt_, pb.rr("p (g t) -> p g t", g=8))
                yo = yst.get()
                if d == 1:
                    yf = yfl.get()
                    S.dma("sp", yf, YF[tok, 2048:4096])
                def stage1(g):
                    psCB = PA.get()
                    S.mm(psCB[:, 0:128], cb_[:, g, :], cb_[:, 8 + g, :])
                    cs = cbs.get()
                    S.tt("dve", cs, psCB[:, 0:128], CF(cU), ALU.mult)
                    psD = PA.get()
                    for r in range(4):
                        hd = g * 4 + r
                        lb = lbp.get()
                        S.act(lb, CF(cSU), AF.Copy, scale=w[:, 32 + hd:33 + hd])
                        S.mm(psD[:, r * 128:(r + 1) * 128], lb, CF(cU))
                    dtm = dtp.get()
                    S.act(dtm, psD, AF.Exp)
                    sp = spp.get()
                    S.tt("dve", sp, dtm.rr("p (r t) -> p r t", r=4),
                         cs.rr("p (o t) -> p o t", o=1).bc([128, 4, 128]), ALU.mult)
                    return sp

                def stage2(g, sp):
                    psY = PA.get()
                    for r in range(4):
                        S.mm(psY[:, r * 64:(r + 1) * 64], sp[:, r, :], xd[:, g * 4 + r, :])
                    psY2 = PA.get()
                    S.mm(psY2[:, 0:256], cb_[:, 8 + g, :], Sb[:, g, :])
                    y1 = y1p.get()
                    S.tt("dve", y1.rr("p (r c) -> p r c", r=4), psY2[:, 0:256].rr("p (r c) -> p r c", r=4),
                         w[:, 128 + g * 4:132 + g * 4].rr("p (r o) -> p r o", o=1).bc([128, 4, 64]), ALU.mult)
                    if d == 0:
                        S.tt("dve", yo[:, g * 256:(g + 1) * 256], y1, psY[:, 0:256], ALU.add)
                    else:
                        S.tt("dve", y1, y1, psY[:, 0:256], ALU.add)
                        S.tt("pool", yo[:, g * 256:(g + 1) * 256], y1, yf[:, g * 256:(g + 1) * 256], ALU.add)
                    psS = PA.get()
                    S.mm(psS[:, 0:256], bt_[:, g, :], xw_[:, g * 4:(g + 1) * 4, :].rr("p r c -> p (r c)"))
                    S.tt("dve", St[:, g, :].rr("p (r c) -> p r c", r=4), St[:, g, :].rr("p (r c) -> p r c", r=4),
                         w[:, 160 + g * 4:164 + g * 4].rr("p (r o) -> p r o", o=1).bc([128, 4, 64]), ALU.mult)
                    S.tt("dve", St[:, g, :], St[:, g, :], psS[:, 0:256], ALU.add)
                    S.copy("act", Sb[:, g, :], St[:, g, :])

                pend = None
                for g in range(9):
                    cur = stage1(g) if g < 8 else None
                    if pend is not None:
                        stage2(g - 1, pend)
                    pend = cur
                if d == 0:
                    S.dma("act", YF[tok, 2048:4096], yo)
                else:
                    z_ = zb.get()
                    S.dma("sp", z_, ZT[tok, 6144:8192])
                    S.tt("pool", scr.rr("p (h c) -> p h c", h=32), xs_.rr("p (h c) -> p h c", h=32),
                         dskip.rr("p (h o) -> p h o", o=1).bc([128, 32, 64]), ALU.mult)
                    S.tt("dve", yo, yo, scr, ALU.add)
                    S.act(z_, z_, AF.Silu)
                    S.tt("dve", yo, yo, z_, ALU.mult)
                    mb = mbf.get()
                    _group_rms(ctx, yo, 8, 256, scr, st8, gnorm, mb)
                    _transpose_store(ctx, mb, 2048, c, mtp)


def na_phase(ctx, i):
    nc, S, NT = ctx["nc"], ctx["S"], ctx["NT"]
    CF, CB, PA, PB, psC = ctx["CF"], ctx["CB"], ctx["PA"], ctx["PB"], ctx["psC"]
    ZB, ZT, OD = ctx["ZB"], ctx["ZT"], ctx["OD"]
    with ExitStack() as st:
        ltiles = []

        def lsb(name, shape, dt=F32):
            t_ = Tile(st.enter_context(nc.sbuf_tensor(f"c{i}_" + name, list(shape), dt)).ap())
            ltiles.append(t_)
            return t_
        st.callback(lambda: S.barrier(ltiles + PA.tiles + PB.tiles + [psC]))
        biasb = lsb("biasb", [128, 16, 896], BF16)
        stg = Pool([lsb(f"stg{j}", [128, 896]) for j in range(2)])
        maskb = Pool([lsb(f"maskb{j}", [128, 896], BF16) for j in range(2)])
        kring = Pool([lsb(f"kr{j}", [128, 16, 128], BF16) for j in range(8)])
        vring = Pool([lsb(f"vr{j}", [128, 16, 129], BF16) for j in range(8)])
        vst = Pool([lsb(f"vst{j}", [128, 2048]) for j in range(2)])
        qp = Pool([lsb(f"q{j}", [128, 16, 128], BF16) for j in range(2)])
        ptp = Pool([lsb(f"pt{j}", [128, 512], BF16) for j in range(4)])
        pmp = Pool([lsb(f"pm{j}", [32, 128], BF16) for j in range(2)])
        rdp = Pool([lsb(f"rd{j}", [128, 2]) for j in range(4)])
        op_ = Pool([lsb(f"o{j}", [128, 2048]) for j in range(2)])
        zp = Pool([lsb(f"z{j}", [128, 2048]) for j in range(2)])
        mbf = Pool([lsb(f"mbf{j}", [128, 2048], BF16) for j in range(2)])
        mtp = Pool([lsb(f"mtp{j}", [128, 16, 128], BF16) for j in range(2)])
        kmeta = lsb("kmeta", [128, 16, 32], BF16)
        vmst = lsb("vmst", [32, 2048])
        vmeta = lsb("vmeta", [32, 16, 129], BF16)
        mbias = lsb("mbias", [32, 1])
        S.dma("sp", mbias, OD["mbias"])
        for h in range(16):
            sg = stg.get()
            S.dma("sp", sg, OD["bias"][i, h])
            S.copy("pool", biasb[:, h, :], sg)
        for v in vring.tiles:
            S.memset("pool", v[:, :, 128:129], 1.0)
        S.memset("pool", vmeta[:, :, 128:129], 1.0)
        S.dma("sp", kmeta, ZB[2048:4096, 96:128].rr("(h p) t -> p h t", p=128))
        S.dma("sp", vmst, ZT[96:128, 0:2048])
        S.copy("pool", vmeta[:, :, 0:128], vmst.rr("p (h c) -> p h c", h=16))
        loaded = {}

        def get_kv(ki):
            if ki not in loaded:
                k = kring.get()
                v = vring.get()
                S.dma("sp", k, ZB[2048:4096, ki * 128:(ki + 1) * 128].rr("(h p) t -> p h t", p=128))
                vs = vst.get()
                S.dma("sp", vs, ZT[ki * 128:(ki + 1) * 128, 0:2048])
                S.copy("pool", v[:, :, 0:128], vs.rr("p (h c) -> p h c", h=16))
                loaded[ki] = (k, v)
            return loaded[ki]

        for qi in range(NT):
            tok = slice(qi * 128, (qi + 1) * 128)
            q = qp.get()
            S.dma("sp", q, ZB[0:2048, tok].rr("(h p) t -> p h t", p=128))
            valid = [(di, qi + di - 3) for di in range(7) if 1 <= qi + di - 3 < NT]
            kv = {ki: get_kv(ki) for (_, ki) in valid}
            sg = stg.get()
            S.dma("sp", sg, OD["mask"][qi])
            mk = maskb.get()
            S.copy("pool", mk, sg)
            z_ = zp.get()
            S.dma("sp", z_, ZT[tok, 2048:4096])
            o_ = op_.get()
            for h in range(16):
                banks = []
                for b0 in range(0, len(valid), 4):
                    grp = valid[b0:b0 + 4]
                    ps = PA.get()
                    for sl, (di, ki) in enumerate(grp):
                        dst = ps[:, sl * 128:(sl + 1) * 128]
                        S.mm(dst, kv[ki][0][:, h, :], q[:, h, :], start=True, stop=False)
                        S.mm(dst, CB(C_ID), biasb[:, h, di * 128:(di + 1) * 128], start=False, stop=False)
                        S.mm(dst, CB(C_ID), mk[:, di * 128:(di + 1) * 128], start=False, stop=True)
                    pt = ptp.get()
                    n = len(grp) * 128
                    S.act(pt[:, 0:n], ps[:, 0:n], AF.Exp)
                    banks.append((pt, grp))
                S.mm(psC[0:32, 0:128], kmeta[:, h, :], q[:, h, :])
                pm = pmp.get()
                S.act(pm, psC[0:32, 0:128], AF.Exp, bias=mbias[:, 0:1], scale=1.0)
                po = PA.get()
                first = True
                for (pt, grp) in banks:
                    for sl, (di, ki) in enumerate(grp):
                        S.mm(po[:, 0:129], pt[:, sl * 128:(sl + 1) * 128], kv[ki][1][:, h, :], start=first, stop=False)
                        first = False
                S.mm(po[:, 0:129], pm, vmeta[:, h, :], start=first, stop=True)
                rd = rdp.get()
                S.recip(rd[:, 0:1], po[:, 128:129])
                S.act(o_[:, h * 128:(h + 1) * 128], po[:, 0:128], AF.Copy, scale=rd[:, 0:1])
            S.act(z_, z_, AF.Silu)
            mb = mbf.get()
            S.tt("dve", mb, o_, z_, ALU.mult)
            _transpose_store(ctx, mb, 0, qi, mtp)


def gdn_phase(ctx, i):
    nc, S, NT = ctx["nc"], ctx["S"], ctx["NT"]
    CF, CB, PA, PB, psC = ctx["CF"], ctx["CB"], ctx["PA"], ctx["PB"], ctx["psC"]
    tmask, ZB, ZT, YF, OD = ctx["tmask"], ctx["ZB"], ctx["ZT"], ctx["YF"], ctx["OD"]
    with ExitStack() as st:
        ltiles = []

        def lsb(name, shape, dt=F32):
            t_ = Tile(st.enter_context(nc.sbuf_tensor(f"d{i}_" + name, list(shape), dt)).ap())
            ltiles.append(t_)
            return t_
        st.callback(lambda: S.barrier(ltiles + PA.tiles + PB.tiles + [psC]))
        q3 = Pool([lsb(f"q{j}", [128, 16, 128], BF16) for j in range(2)])
        k3 = Pool([lsb(f"k{j}", [128, 16, 128], BF16) for j in range(2)])
        v3 = Pool([lsb(f"v{j}", [128, 16, 128], BF16) for j in range(2)])
        gt = Pool([lsb(f"g{j}", [128, 128]) for j in range(2)])
        gw = Pool([lsb(f"gw{j}", [128, 160]) for j in range(2)])
        lbp = Pool([lsb(f"lb{j}", [128, 128]) for j in range(8)])
        etp = Pool([lsb(f"et{j}", [128, 4, 128]) for j in range(2)])
        ep = Pool([lsb(f"e{j}", [128, 4, 128]) for j in range(2)])
        atp = Pool([lsb(f"at{j}", [128, 4, 128], BF16) for j in range(2)])
        CHDT = F32
        KDT = BF16
        CI = CF(C_ID) if CHDT == F32 else CB(C_ID)
        Pp = [Pool([lsb(f"P{sl}{j}", [128, 4, 128], CHDT) for j in range(2)]) for sl in range(2)]
        PTp = [Pool([lsb(f"PT{sl}{j}", [128, 4, 128], CHDT) for j in range(2)]) for sl in range(2)]
        TTp = [Pool([lsb(f"TT{sl}{j}", [128, 4, 128], CHDT) for j in range(2)]) for sl in range(2)]
        TTup = Pool([lsb(f"TTu{j}", [128, 4, 128], KDT) for j in range(2)])
        ktmp = Pool([lsb(f"ktm{j}", [128, 4, 128], BF16) for j in range(2)])
        vtmp = Pool([lsb(f"vtm{j}", [128, 4, 128], BF16) for j in range(2)])
        kbep = Pool([lsb(f"kbe{j}", [128, 4, 128], KDT) for j in range(2)])
        kwep = Pool([lsb(f"kwe{j}", [128, 4, 128], BF16) for j in range(2)])
        vbep = Pool([lsb(f"vbe{j}", [128, 4, 128], KDT) for j in range(2)])
        nwmp = Pool([lsb(f"nwm{j}", [128, 4, 128], BF16) for j in range(2)])
        ubp = Pool([lsb(f"ub{j}", [128, 4, 128], BF16) for j in range(2)])
        y1p = Pool([lsb(f"y1{j}", [128, 4, 128]) for j in range(2)])
        yst = Pool([lsb(f"yst{j}", [128, 2048]) for j in range(2)])
        yfl = Pool([lsb("yfl0", [128, 2048])])
        zdp = Pool([lsb("zd0", [128, 2048])])
        scr = lsb("scr", [128, 2048])
        st16 = lsb("st16", [128, 16])
        mbf = Pool([lsb(f"mbf{j}", [128, 2048], BF16) for j in range(2)])
        mtp = Pool([lsb(f"mtp{j}", [128, 16, 128], BF16) for j in range(2)])
        St = lsb("St", [128, 16, 128])
        Sb = lsb("Sb", [128, 16, 128], BF16)
        dtb = lsb("dtb", [128, 32])
        nA = lsb("nA", [128, 32])
        gn = lsb("gn", [128, 2048])
        S.dma("sp", dtb, OD["dtb"][i])
        S.dma("sp", nA, OD["alog"][i])
        S.dma("sp", gn, OD["gnorm"][i])
        S.act(nA, nA, AF.Exp)
        S.ts("dve", nA, nA, -1.0, None, ALU.mult)

        def bc4(v):
            return v.rr("p (h o) -> p h o", o=1).bc([128, 4, 128])

        def m4(cidx):
            return CF(cidx).rr("p (o t) -> p o t", o=1).bc([128, 4, 128])

        for d in (0, 1):
            cU, cSU, cNEG = ctx["dirconst"](d)
            cST = C_SBW if d == 0 else C_SFW
            S.memset("pool", St, 0.0)
            S.memset("pool", Sb, 0.0)
            order = range(NT) if d == 0 else range(NT - 1, -1, -1)
            for c in order:
                tok = slice(c * 128, (c + 1) * 128)
                q = q3.get()
                k = k3.get()
                v = v3.get()
                S.dma("sp", q, ZB[4096:6144, tok].rr("(h p) t -> p h t", p=128))
                S.dma("sp", k, ZB[6144:8192, tok].rr("(h p) t -> p h t", p=128))
                S.dma("sp", v, ZB[8192:10240, tok].rr("(h p) t -> p h t", p=128))
                g = gt.get()
                S.dma("sp", g, ZT[tok, 6144:6272])
                w = gw.get()
                S.act(w[:, 0:16], g[:, d * 16:(d + 1) * 16], AF.Sigmoid)
                S.ts("dve", w[:, 0:16], w[:, 0:16], tmask[:, c:c + 1], None, ALU.mult)
                S.tt("dve", w[:, 16:32], g[:, 32 + d * 16:48 + d * 16], dtb[:, d * 16:(d + 1) * 16], ALU.add)
                S.act(w[:, 16:32], w[:, 16:32], AF.Exp)
                S.act(w[:, 16:32], w[:, 16:32], AF.Ln, bias=1.0, scale=1.0)
                S.tt("dve", w[:, 16:32], w[:, 16:32], nA[:, d * 16:(d + 1) * 16], ALU.mult)
                S.ts("dve", w[:, 16:32], w[:, 16:32], tmask[:, c:c + 1], None, ALU.mult)
                S.mm(psC[:, 0:16], CF(cU), w[:, 16:32])
                S.mm(psC[:, 16:32], CF(C_ONES), w[:, 16:32])
                S.copy("dve", w[:, 32:64], psC[:, 0:32])
                S.act(w[:, 64:96], w[:, 32:64], AF.Exp)
                S.tt("dve", w[:, 96:112], w[:, 48:64], w[:, 32:48], ALU.subtract)
                S.act(w[:, 96:112], w[:, 96:112], AF.Exp)
                S.tt("dve", w[:, 112:128], w[:, 0:16], w[:, 64:80], ALU.mult)
                S.ts("dve", w[:, 128:144], w[:, 0:16], -1.0, None, ALU.mult)
                yo = yst.get()
                if d == 1:
                    yf = yfl.get()
                    S.dma("sp", yf, YF[tok, 0:2048])
                def g_stage1(qd, sl):
                    H = [qd * 4 + r for r in range(4)]
                    lbs = []
                    for r, hd in enumerate(H):
                        lb = lbp.get()
                        S.act(lb, CF(cSU), AF.Copy, scale=w[:, 16 + hd:17 + hd])
                        lbs.append(lb)
                    psET = PA.get()
                    for r in range(4):
                        S.mm(psET[:, r * 128:(r + 1) * 128], lbs[r], CF(cU))
                    et = etp.get()
                    S.act(et.rr("p h t -> p (h t)"), psET, AF.Exp)
                    psE = PA.get()
                    for r in range(4):
                        S.mm(psE[:, r * 128:(r + 1) * 128], CF(cU), lbs[r])
                    e_ = ep.get()
                    S.act(e_.rr("p h t -> p (h t)"), psE, AF.Exp)
                    psQK = PA.get()
                    for r, hd in enumerate(H):
                        S.mm(psQK[:, r * 128:(r + 1) * 128], k[:, hd, :], q[:, hd, :])
                    S.tt("dve", et, et, m4(cU), ALU.mult)
                    at = atp.get()
                    S.tt("dve", at, psQK.rr("p (h t) -> p h t", h=4), et, ALU.mult)
                    psKK = PA.get()
                    for r, hd in enumerate(H):
                        S.mm(psKK[:, r * 128:(r + 1) * 128], k[:, hd, :], k[:, hd, :])
                    S.tt("dve", e_, e_, m4(cST), ALU.mult)
                    S.tt("dve", e_, psKK.rr("p (h t) -> p h t", h=4), e_, ALU.mult)
                    P = Pp[sl].get()
                    S.tt("dve", P, e_, bc4(w[:, 128 + qd * 4:132 + qd * 4]), ALU.mult)
                    pbN = PA.get()
                    for r in range(4):
                        S.mm(pbN[:, r * 128:(r + 1) * 128], P[:, r, :], CI)
                    PT = PTp[sl].get()
                    S.copy("act", PT, pbN.rr("p (h t) -> p h t", h=4))
                    TT = TTp[sl].get()
                    S.tt("dve", TT, PT, m4(C_ID), ALU.add)
                    return dict(qd=qd, sl=sl, H=H, at=at, P=P, PT=PT, TT=TT)

                def g_mm1(t_, lev):
                    P, PT = t_["P"], t_["PT"]
                    bP = PA.get()
                    for r in range(4):
                        S.mm(bP[:, r * 128:(r + 1) * 128], PT[:, r, :], P[:, r, :])
                    t_["bP"] = bP
                    if lev < 5:
                        bPT = PA.get()
                        for r in range(4):
                            S.mm(bPT[:, r * 128:(r + 1) * 128], P[:, r, :], PT[:, r, :])
                        t_["bPT"] = bPT

                def g_ev1_mm2(t_, lev):
                    sl = t_["sl"]
                    Pn = Pp[sl].get()
                    S.copy("act", Pn, t_["bP"].rr("p (h t) -> p h t", h=4))
                    if lev < 5:
                        PTn = PTp[sl].get()
                        S.copy("dve", PTn, t_["bPT"].rr("p (h t) -> p h t", h=4))
                        t_["PT"] = PTn
                    t_["P"] = Pn
                    bT = PA.get()
                    for r in range(4):
                        S.mm(bT[:, r * 128:(r + 1) * 128], Pn[:, r, :], t_["TT"][:, r, :])
                    t_["bT"] = bT

                def g_ev2(t_):
                    TTn = TTp[t_["sl"]].get()
                    S.tt("dve", TTn, t_["TT"], t_["bT"].rr("p (h t) -> p h t", h=4), ALU.add)
                    t_["TT"] = TTn

                def g_stage2(t_):
                    qd, H, at, TT = t_["qd"], t_["H"], t_["at"], t_["TT"]
                    hs = slice(qd * 4, qd * 4 + 4)
                    TTu = TTup.get()
                    S.copy("act", TTu, TT)
                    pbK = PB.get()
                    for r, hd in enumerate(H):
                        S.tr(pbK[:, r * 128:(r + 1) * 128], k[:, hd, :], CB(C_ID))
                    for r, hd in enumerate(H):
                        S.tr(pbK[:, 512 + r * 128:512 + (r + 1) * 128], v[:, hd, :], CB(C_ID))
                    ktm = ktmp.get()
                    vtm = vtmp.get()
                    S.copy("act", ktm, pbK[:, 0:512].rr("p (h t) -> p h t", h=4))
                    S.copy("dve", vtm, pbK[:, 512:1024].rr("p (h t) -> p h t", h=4))
                    kbe = kbep.get()
                    kwe = kwep.get()
                    vbe = vbep.get()
                    S.tt("dve", kbe, ktm, bc4(w[:, 112 + qd * 4:116 + qd * 4]), ALU.mult)
                    S.tt("pool", kwe, ktm, bc4(w[:, 96 + qd * 4:100 + qd * 4]), ALU.mult)
                    S.tt("dve", vbe, vtm, bc4(w[:, qd * 4:qd * 4 + 4]), ALU.mult)
                    bW = PA.get()
                    for r in range(4):
                        S.mm(bW[:, r * 128:(r + 1) * 128], kbe[:, r, :], TTu[:, r, :])
                    nwm = nwmp.get()
                    S.act(nwm.rr("p h t -> p (h t)"), bW, AF.Copy, scale=-1.0)
                    bU = PA.get()
                    for r, hd in enumerate(H):
                        S.mm(bU[:, r * 128:(r + 1) * 128], TTu[:, r, :], vbe[:, r, :], start=True, stop=False)
                        S.mm(bU[:, r * 128:(r + 1) * 128], nwm[:, r, :], Sb[:, hd, :], start=False, stop=True)
                    ub = ubp.get()
                    S.copy("dve", ub, bU.rr("p (h t) -> p h t", h=4))
                    bO1 = PA.get()
                    for r in range(4):
                        S.mm(bO1[:, r * 128:(r + 1) * 128], at[:, r, :], ub[:, r, :])
                    bO2 = PA.get()
                    for r, hd in enumerate(H):
                        S.mm(bO2[:, r * 128:(r + 1) * 128], q[:, hd, :], Sb[:, hd, :])
                    y1 = y1p.get()
                    S.tt("dve", y1, bO2.rr("p (h t) -> p h t", h=4), bc4(w[:, 64 + qd * 4:68 + qd * 4]), ALU.mult)
                    ycols = yo[:, qd * 512:(qd + 1) * 512]
                    if d == 0:
                        S.tt("dve", ycols, y1.rr("p h t -> p (h t)"), bO1, ALU.add)
                    else:
                        S.tt("dve", y1.rr("p h t -> p (h t)"), y1.rr("p h t -> p (h t)"), bO1, ALU.add)
                        S.tt("pool", ycols, y1.rr("p h t -> p (h t)"), yf[:, qd * 512:(qd + 1) * 512], ALU.add)
                    bS = PA.get()
                    for r in range(4):
                        S.mm(bS[:, r * 128:(r + 1) * 128], kwe[:, r, :], ub[:, r, :])
                    S.tt("dve", St[:, hs, :], St[:, hs, :], bc4(w[:, 80 + qd * 4:84 + qd * 4]), ALU.mult)
                    S.tt("dve", St[:, hs, :], St[:, hs, :], bS.rr("p (h t) -> p h t", h=4), ALU.add)
                    S.copy("act", Sb[:, hs, :], St[:, hs, :])

                for pair in ((0,), (1,), (2,), (3,)):
                    sts = [g_stage1(qd, sl) for sl, qd in enumerate(pair)]
                    for lev in range(6):
                        for t_ in sts:
                            g_mm1(t_, lev)
                        for t_ in sts:
                            g_ev1_mm2(t_, lev)
                        for t_ in sts:
                            g_ev2(t_)
                    for t_ in sts:
                        g_stage2(t_)
                if d == 0:
                    S.dma("act", YF[tok, 0:2048], yo)
                else:
                    z_ = zdp.get()
                    S.dma("sp", z_, ZT[tok, 4096:6144])
                    mb = mbf.get()
                    _group_rms(ctx, yo, 16, 128, scr, st16, gn, mb)
                    S.act(z_, z_, AF.Silu)
                    S.tt("dve", mb, mb, z_, ALU.mult)
                    _transpose_store(ctx, mb, 2048, c, mtp)


def _bc(v, n=128):
    v = np.asarray(v, np.float32).reshape(-1)
    return np.ascontiguousarray(np.broadcast_to(v[None, :], (n, v.size)))


def _conv_pack(w, b):
    C = w.shape[1]
    o = np.concatenate([w.T, b[:, None]], axis=1).astype(np.float32)
    return np.ascontiguousarray(o.reshape(C // 128, 128, 6))


def _na_bias(rpb):
    p = np.arange(128)[:, None]
    f = np.arange(128)[None, :]
    kr, kc = p // 64, p % 64
    qr, qc = f // 64, f % 64
    c0 = np.clip(qc - 8, 0, 48)
    col_ok = (kc >= c0) & (kc < c0 + 16)
    dc = np.clip(kc - qc + 15, 0, 30)
    out = np.full((16, 128, 7, 128), NEG, np.float32)
    for di, dl in enumerate(range(-3, 4)):
        dr = 2 * dl + kr - qr
        ok = col_ok & (np.abs(dr) <= 7)
        dri = np.clip(dr + 7, 0, 14)
        vals = rpb[:, dri, dc]
        out[:, :, di, :] = np.where(ok[None], vals, NEG)
    return np.ascontiguousarray(out.reshape(16, 128, 7 * 128))


def _na_mask(NT, T):
    j0 = 1
    rows = T // 64
    jend = 1 + T // 128
    p = np.arange(128)[:, None]
    f = np.arange(128)[None, :]
    out = np.full((NT, 128, 7, 128), NEG, np.float32)
    for qi in range(j0, jend):
        q_row = 2 * (qi - j0) + f // 64
        r0 = np.clip(q_row - 4, 0, rows - 8)
        for di, dl in enumerate(range(-3, 4)):
            ki = qi + dl
            if ki < j0 or ki >= jend:
                continue
            k_row = 2 * (ki - j0) + p // 64
            ok = (k_row >= r0) & (k_row < r0 + 8)
            out[qi, :, di, :] = np.where(ok, 0.0, NEG)
    return np.ascontiguousarray(out.reshape(NT, 128, 7 * 128))


def prep_core_inputs(inp, seqs, NT, DEPTH):
    Lp = NT * 128
    n_even = (DEPTH + 1) // 2
    n_odd = DEPTH // 2
    f = lambda a: np.ascontiguousarray(np.asarray(a, np.float32))
    shared = {"consts": make_consts()}
    shared["normw"] = f(np.stack([np.asarray(inp["norm_w"][l]).reshape(16, 128).T for l in range(DEPTH)]))
    if n_even:
        shared["ev_w_in"] = f(inp["ev_w_in"][:n_even])
        shared["ev_w_out"] = f(inp["ev_w_out"][:n_even])
        shared["ev_cab"] = f(np.stack([_conv_pack(np.asarray(inp["ev_conv_a_w"][i]), np.asarray(inp["ev_conv_a_b"][i])) for i in range(n_even)]))
        shared["ev_cbb"] = f(np.stack([_conv_pack(np.asarray(inp["ev_conv_b_w"][i]), np.asarray(inp["ev_conv_b_b"][i])) for i in range(n_even)]))
        shared["ev_gbias"] = f(np.stack([_bc(np.concatenate([np.asarray(inp["ev_ig_b"][i]).ravel(), np.asarray(inp["ev_fg_b"][i]).ravel()])) for i in range(n_even)]))
        shared["ev_hnorm"] = f(np.stack([_bc(inp["ev_hnorm_a"][i]) for i in range(n_even)]))
        shared["ev_dtb"] = f(np.stack([_bc(inp["ev_dt_bias"][i]) for i in range(n_even)]))
        shared["ev_alog"] = f(np.stack([_bc(inp["ev_a_log"][i]) for i in range(n_even)]))
        shared["ev_dskip"] = f(np.stack([_bc(inp["ev_d_skip"][i]) for i in range(n_even)]))
        shared["ev_gnorm"] = f(np.stack([_bc(inp["ev_gnorm_b"][i]) for i in range(n_even)]))
    if n_odd:
        shared["od_w_in"] = f(inp["od_w_in"][:n_odd])
        shared["od_w_out"] = f(inp["od_w_out"][:n_odd])
        shared["od_qkn"] = f(np.stack([np.stack([np.asarray(inp["od_qn_w"][i]).reshape(128, 1), np.asarray(inp["od_kn_w"][i]).reshape(128, 1)], axis=0) for i in range(n_odd)]))
        shared["od_bias"] = f(np.stack([_na_bias(np.asarray(inp["od_rpb"][i])) for i in range(n_odd)]))
        shared["od_cdb"] = f(np.stack([_conv_pack(np.asarray(inp["od_conv_d_w"][i]), np.asarray(inp["od_conv_d_b"][i])) for i in range(n_odd)]))
        shared["od_dtb"] = f(np.stack([_bc(inp["od_dt_bias"][i]) for i in range(n_odd)]))
        shared["od_alog"] = f(np.stack([_bc(inp["od_a_log"][i]) for i in range(n_odd)]))
        shared["od_gnorm"] = f(np.stack([_bc(np.tile(np.asarray(inp["od_gnorm_d"][i]), 16)) for i in range(n_odd)]))
        shared["od_mbias"] = np.concatenate([np.full((16, 1), NEG, np.float32), np.zeros((16, 1), np.float32)])
    meta = np.asarray(inp["meta"], np.float32)
    maps = []
    mask_cache = {}
    for x in seqs:
        T = x.shape[0]
        h0 = np.zeros((Lp, D), np.float32)
        h0[128 - N_META:128] = meta
        h0[128:128 + T] = x
        tok = np.arange(Lp).reshape(NT, 128).T
        real = (tok >= 128 - N_META) & (tok < 128 + T)
        m = dict(shared)
        m["h0"] = h0
        m["tmask"] = np.ascontiguousarray(np.where(real, 1.0, 0.0).astype(np.float32))
        m["negm"] = np.ascontiguousarray(np.where(real, 0.0, NEG).astype(np.float32))
        if n_odd:
            if T not in mask_cache:
                mask_cache[T] = _na_mask(NT, T)
            m["od_mask"] = mask_cache[T]
        maps.append(m)
    return maps


_PROG_CACHE = {}


def run_model(inp, seqs, NT, DEPTH, dbg=None):
    key = (NT, DEPTH, tuple(dbg) if dbg else None)
    if key not in _PROG_CACHE:
        _PROG_CACHE[key] = build_program(NT, DEPTH, dbg)[0]
    nc = _PROG_CACHE[key]
    maps = prep_core_inputs(inp, seqs, NT, DEPTH)
    res = run_bass_kernel_spmd(nc, maps, core_ids=list(range(len(maps))))
    return res.results


def kernel(**inputs):
    xp = np.asarray(inputs["x_prompt"], np.float32)
    xs = np.asarray(inputs["x_sample"], np.float32)
    Ts = xs.shape[1]
    NT = (Ts + 128) // 128
    seqs = [xs[0], xs[1], xp[0], xp[1], xs[0], xs[1], xp[0], xp[1]]
    res = run_model(inputs, seqs, NT, 4)
    Lp = NT * 128
    ys = np.stack([res[0]["y"][128:128 + Ts], res[1]["y"][128:128 + Ts]]).astype(np.float32)
    Tp = xp.shape[1]
    yp = np.stack([res[2]["y"][128:128 + Tp], res[3]["y"][128:128 + Tp]]).astype(np.float32)
    return (yp, ys)
```

```python
import numpy as np
from contextlib import ExitStack
import concourse.bass as bass
import concourse.mybir as mybir
from concourse.bass_utils import run_bass_kernel_spmd

F32 = mybir.dt.float32
BF16 = mybir.dt.bfloat16
AF = mybir.ActivationFunctionType
ALU = mybir.AluOpType

D = 2048
N_META = 16
EPS = 1e-6
NEG = -30000.0
EVEN_SIZES = (1024, 1024, 2048, 2048, 2048, 32, 2048, 2048, 1024, 1024, 64)
ODD_SIZES = (2048, 2048, 2048, 2048, 2048, 2048, 2048, 2048, 64)
E_IN = sum(EVEN_SIZES)
O_IN = sum(ODD_SIZES)


def _offs(sizes):
    o, acc = [], 0
    for s in sizes:
        o.append(acc)
        acc += s
    return o


EOFF = _offs(EVEN_SIZES)
OOFF = _offs(ODD_SIZES)


class Track:
    __slots__ = ("writers", "readers")

    def __init__(self):
        self.writers = {}
        self.readers = {}


class View:
    __slots__ = ("ap", "tr")

    def __init__(self, ap, tr):
        self.ap = ap
        self.tr = tr

    def __getitem__(self, idx):
        return View(self.ap[idx], self.tr)

    def bc(self, shape):
        return View(self.ap.to_broadcast(shape), self.tr)

    def rr(self, s, **kw):
        return View(self.ap.rearrange(s, **kw), self.tr)


class Tile(View):
    def __init__(self, ap):
        View.__init__(self, ap, Track())


class Sched:
    SEM_ROT = 30000

    def __init__(self, nc, n_dma_sems=12):
        self.nc = nc
        self.eng = {"pe": nc.tensor, "act": nc.scalar, "dve": nc.vector,
                    "pool": nc.gpsimd, "sp": nc.sync}
        self.sem = {}
        self.cnt = {}
        self.semid = 0
        for e in self.eng:
            self._new_sem(e)
        self.known = {e: {} for e in self.eng}
        self.dma_sems = {}
        for q in ("sp", "act", "pool"):
            lst = []
            for i in range(n_dma_sems):
                s = nc.alloc_semaphore(name=f"dq_{q}_{i}")
                lst.append([s, 0, f"dq_{q}_{i}"])
            self.dma_sems[q] = lst
        self.dma_rr = {q: 0 for q in self.dma_sems}
        self.ninst = 0

    def _new_sem(self, e):
        self.semid += 1
        key = f"s_{e}_{self.semid}"
        self.sem[e] = (self.nc.alloc_semaphore(name=key), key)
        self.cnt[e] = 0

    def _wait(self, e, key, sem, val):
        k = self.known[e]
        if k.get(key, 0) >= val:
            return
        self.eng[e].wait_ge(sem, val)
        k[key] = val
        self.ninst += 1

    def _deps(self, e, reads, writes):
        mykey = self.sem[e][1]
        for r in reads:
            for key, (sem, val) in r.tr.writers.items():
                if key == mykey and e == "pe":
                    continue
                self._wait(e, key, sem, val)
        for w in writes:
            for key, (sem, val) in w.tr.writers.items():
                if key == mykey:
                    continue
                self._wait(e, key, sem, val)
            for key, (sem, val) in w.tr.readers.items():
                if key == mykey:
                    continue
                self._wait(e, key, sem, val)

    def _record(self, key, sem, val, reads, writes):
        for r in reads:
            r.tr.readers[key] = (sem, val)
        for w in writes:
            w.tr.writers = {key: (sem, val)}
            w.tr.readers = {}

    def op(self, e, fn, reads, writes):
        if self.cnt[e] >= self.SEM_ROT:
            self._new_sem(e)
        self._deps(e, reads, writes)
        inst = fn(self.eng[e])
        sem, key = self.sem[e]
        self.cnt[e] += 1
        inst.then_inc(sem, 1)
        self._record(key, sem, self.cnt[e], reads, writes)
        self.ninst += 1
        return inst

    def dma(self, q, out, in_, **kw):
        lst = self.dma_sems[q]
        i = self.dma_rr[q]
        self.dma_rr[q] = (i + 1) % len(lst)
        ent = lst[i]
        sem, val, key = ent
        if val > 0:
            self._wait(q, key, sem, val)
        self._deps(q, [in_], [out])
        inst = self.eng[q].dma_start(out=out.ap, in_=in_.ap, **kw)
        ent[1] = val + 16
        inst.then_inc(sem, 16)
        self._record(key, sem, ent[1], [in_], [out])
        self.ninst += 1
        return inst

    def drain(self, e, views):
        self._deps(e, views, views)

    def barrier(self, views):
        for e in self.eng:
            self._deps(e, views, views)

    def mm(self, out, lhsT, rhs, start=True, stop=True):
        rd = [lhsT, rhs] + ([] if start else [out])
        return self.op("pe", lambda g: g.matmul(out.ap, lhsT.ap, rhs.ap, start=start, stop=stop), rd, [out])

    def tr(self, out, in_, ident):
        return self.op("pe", lambda g: g.transpose(out.ap, in_.ap, ident.ap), [in_, ident], [out])

    def act(self, out, in_, func, bias=None, scale=None, accum=None):
        rd = [in_]
        kw = {}
        if bias is not None:
            if isinstance(bias, View):
                rd.append(bias)
                kw["bias"] = bias.ap
            else:
                kw["bias"] = bias
        if scale is not None:
            if isinstance(scale, View):
                rd.append(scale)
                kw["scale"] = scale.ap
            else:
                kw["scale"] = scale
        wr = [out]
        if accum is not None:
            wr.append(accum)
            kw["accum_out"] = accum.ap
        return self.op("act", lambda g: g.activation(out.ap, in_.ap, func, **kw), rd, wr)

    def ts(self, e, out, in0, s1, s2, op0, op1=None):
        rd = [in0]
        a1, a2 = s1, s2
        if isinstance(s1, View):
            rd.append(s1)
            a1 = s1.ap
        if isinstance(s2, View):
            rd.append(s2)
            a2 = s2.ap
        if op1 is None:
            return self.op(e, lambda g: g.tensor_scalar(out.ap, in0.ap, a1, a2, op0), rd, [out])
        return self.op(e, lambda g: g.tensor_scalar(out.ap, in0.ap, a1, a2, op0, op1), rd, [out])

    def stt(self, e, out, in0, s, in1, op0, op1):
        e = "dve"
        rd = [in0, in1]
        a = s
        if isinstance(s, View):
            rd.append(s)
            a = s.ap
        return self.op(e, lambda g: g.scalar_tensor_tensor(out.ap, in0.ap, a, in1.ap, op0, op1), rd, [out])

    def tt(self, e, out, in0, in1, op):
        return self.op(e, lambda g: g.tensor_tensor(out.ap, in0.ap, in1.ap, op), [in0, in1], [out])

    def copy(self, e, out, in_):
        if e == "act":
            return self.op(e, lambda g: g.copy(out.ap, in_.ap), [in_], [out])
        return self.op(e, lambda g: g.tensor_copy(out.ap, in_.ap), [in_], [out])

    def memset(self, e, out, val):
        return self.op(e, lambda g: g.memset(out.ap, val), [], [out])

    def recip(self, out, in_):
        return self.op("dve", lambda g: g.reciprocal(out.ap, in_.ap), [in_], [out])


class Pool:
    def __init__(self, tiles):
        self.tiles = tiles
        self.i = 0

    def get(self):
        t = self.tiles[self.i]
        self.i = (self.i + 1) % len(self.tiles)
        return t


C_ID, C_UFW, C_UBW, C_SUFW, C_SUBW, C_NEGFW, C_NEGBW, C_ONES, C_SFW, C_SBW = range(10)
N_CONST = 10


def make_consts():
    p = np.arange(128)[:, None]
    f = np.arange(128)[None, :]
    c = np.zeros((N_CONST, 128, 128), np.float32)
    c[C_ID] = (p == f)
    c[C_UFW] = (p <= f)
    c[C_UBW] = (p >= f)
    c[C_SUFW] = (p > f)
    c[C_SUBW] = (p < f)
    c[C_NEGFW] = np.where(p <= f, 0.0, NEG)
    c[C_NEGBW] = np.where(p >= f, 0.0, NEG)
    c[C_ONES] = 1.0
    c[C_SFW] = (p < f)
    c[C_SBW] = (p > f)
    return c


def build_program(NT, DEPTH, dbg=None):
    Lp = NT * 128
    nc = bass.Bass("TRN2", target_bir_lowering=False)
    S = Sched(nc)
    n_even = (DEPTH + 1) // 2
    n_odd = DEPTH // 2

    def din(name, shape, dt=F32):
        return Tile(nc.dram_tensor(name, list(shape), dt, kind="ExternalInput").ap())

    def dscratch(name, shape, dt=F32):
        return Tile(nc.dram_tensor(name, list(shape), dt).ap())

    H0 = din("h0", [Lp, D])
    TMASK = din("tmask", [128, NT])
    NEGM = din("negm", [128, NT])
    CONSTS = din("consts", [N_CONST, 128, 128])
    NORMW = din("normw", [DEPTH, 128, 16])
    EV = {}
    OD = {}
    if n_even:
        EV["w_in"] = din("ev_w_in", [n_even, D, E_IN])
        EV["w_out"] = din("ev_w_out", [n_even, 4096, D])
        EV["cab"] = din("ev_cab", [n_even, 16, 128, 6])
        EV["cbb"] = din("ev_cbb", [n_even, 32, 128, 6])
        EV["gbias"] = din("ev_gbias", [n_even, 128, 32])
        EV["hnorm"] = din("ev_hnorm", [n_even, 128, 2048])
        EV["dtb"] = din("ev_dtb", [n_even, 128, 64])
        EV["alog"] = din("ev_alog", [n_even, 128, 64])
        EV["dskip"] = din("ev_dskip", [n_even, 128, 32])
        EV["gnorm"] = din("ev_gnorm", [n_even, 128, 2048])
    if n_odd:
        OD["w_in"] = din("od_w_in", [n_odd, D, O_IN])
        OD["w_out"] = din("od_w_out", [n_odd, 4096, D])
        OD["qkn"] = din("od_qkn", [n_odd, 2, 128, 1])
        OD["bias"] = din("od_bias", [n_odd, 16, 128, 7 * 128])
        OD["mask"] = din("od_mask", [NT, 128, 7 * 128])
        OD["mbias"] = din("od_mbias", [32, 1])
        OD["cdb"] = din("od_cdb", [n_odd, 48, 128, 6])
        OD["dtb"] = din("od_dtb", [n_odd, 128, 32])
        OD["alog"] = din("od_alog", [n_odd, 128, 32])
        OD["gnorm"] = din("od_gnorm", [n_odd, 128, 2048])
    Y = Tile(nc.dram_tensor("y", [Lp, D], F32, kind="ExternalOutput").ap())
    DBG = {}
    if dbg:
        for name, shape, dt in dbg:
            DBG[name] = Tile(nc.dram_tensor("dbg_" + name, list(shape), dt, kind="ExternalOutput").ap())

    HB = [dscratch("hb0", [Lp, D]), dscratch("hb1", [Lp, D])]
    class RowSplit:
        def __init__(self, name, nblk, dt):
            self.blk = [dscratch(f"{name}{b}", [2048, Lp], dt) for b in range(nblk)]

        def __getitem__(self, idx):
            r, c = idx
            b = r.start // 2048
            assert (r.stop - 1) // 2048 == b
            return self.blk[b][r.start - b * 2048:r.stop - b * 2048, c]

    class ColSplit:
        def __init__(self, name, cut, width, dt):
            self.cut = cut
            self.a = dscratch(name + "a", [Lp, cut], dt)
            self.b = dscratch(name + "b", [Lp, width - cut], dt)

        def __getitem__(self, idx):
            r, c = idx
            if c.start >= self.cut:
                return self.b[r, c.start - self.cut:c.stop - self.cut]
            assert c.stop <= self.cut
            return self.a[r, c]

    ZF = RowSplit("zf", 5, BF16)
    ZB = RowSplit("zb", 5, BF16)
    ZT = ColSplit("zt", 6144, 8320, F32)
    YF = dscratch("yf", [Lp, 4096])
    MT = dscratch("mt", [4096, Lp], BF16)

    es = ExitStack()
    uid = [0]

    def sb(name, shape, dt=F32):
        return Tile(es.enter_context(nc.sbuf_tensor("g_" + name, list(shape), dt)).ap())

    cf = sb("cf", [128, N_CONST, 128], F32)
    cb = sb("cb", [128, N_CONST, 128], BF16)
    tmask = sb("tmask", [128, NT], F32)
    negm = sb("negm", [128, NT], F32)
    S.dma("sp", cf, CONSTS.rr("c p f -> p c f"))
    S.dma("sp", tmask, TMASK)
    S.dma("sp", negm, NEGM)
    S.copy("dve", cb, cf)

    def CF(i):
        return cf[:, i, :]

    def CB(i):
        return cb[:, i, :]

    psA = [Tile(nc.alloc_psum_tensor(f"psA{i}", [128, 512], F32).ap()) for i in range(5)]
    psB = [Tile(nc.alloc_psum_tensor(f"psB{i}", [128, 1024], BF16).ap()) for i in range(2)]
    psC = Tile(nc.alloc_psum_tensor("psC", [128, 512], F32).ap())
    PA = Pool(psA)
    PB = Pool(psB)

    evac_rr = [0]

    def evac(out, in_):
        evac_rr[0] ^= 1
        S.copy("act" if evac_rr[0] else "dve", out, in_)

    def phaseA(layer, Hin, Win, fm_groups, tm_groups):
        with ExitStack() as st:
            ltiles = []

            def lsb(name, shape, dt=F32):
                uid[0] += 1
                t_ = Tile(st.enter_context(nc.sbuf_tensor(f"{name}_{uid[0]}", list(shape), dt)).ap())
                ltiles.append(t_)
                return t_
            st.callback(lambda: S.barrier(ltiles + psA + psB + [psC]))
            SBT = 16
            hnT = lsb("hnT", [128, 16, SBT * 128], BF16)
            ht = Pool([lsb(f"ht{i}", [128, D]) for i in range(2)])
            hnb = Pool([lsb(f"hnb{i}", [128, D], BF16) for i in range(2)])
            junk = lsb("junk", [128, D], BF16)
            st4 = Pool([lsb(f"st4_{i}", [128, 4]) for i in range(2)])
            wst = Pool([lsb(f"wst{i}", [128, 16, 256]) for i in range(2)])
            wbf = Pool([lsb(f"wbf{i}", [128, 16, 256], BF16) for i in range(2)])
            stf = Pool([lsb(f"stf{i}", [128, SBT * 128], BF16) for i in range(2)])
            stt_ = Pool([lsb(f"stt{i}", [128, 256]) for i in range(4)])
            for t_ in stt_.tiles:
                S.memset("pool", t_, 0.0)
            nw = lsb("nw", [128, 16])
            S.dma("sp", nw, NORMW[layer])
            Wv = Win.rr("(kc p) n -> p kc n", p=128)
            for sb0 in range(0, NT, SBT):
                tiles = list(range(sb0, min(sb0 + SBT, NT)))
                ntok = len(tiles) * 128
                for j in tiles:
                    h = ht.get()
                    S.dma("sp", h, Hin[j * 128:(j + 1) * 128, :])
                    s4 = st4.get()
                    S.act(junk, h, AF.Square, accum=s4[:, 0:1])
                    S.act(s4[:, 1:2], s4[:, 0:1], AF.Sqrt, bias=EPS, scale=1.0 / D)
                    S.recip(s4[:, 2:3], s4[:, 1:2])
                    S.tt("dve", s4[:, 3:4], s4[:, 2:3], tmask[:, j:j + 1], ALU.mult)
                    hb = hnb.get()
                    S.act(hb, h, AF.Copy, scale=s4[:, 3:4])
                    jj = j - sb0
                    for half in range(2):
                        pb = PB.get()
                        for k in range(8):
                            kc = half * 8 + k
                            S.tr(pb[:, k * 128:(k + 1) * 128], hb[:, kc * 128:(kc + 1) * 128], CB(C_ID))
                        evac(hnT[:, half * 8:(half + 1) * 8, jj * 128:(jj + 1) * 128],
                             pb.rr("p (k t) -> p k t", k=8))
                for (c0, r0) in fm_groups:
                    w = wst.get()
                    S.dma("sp", w[:, :, 0:128], Wv[:, :, c0:c0 + 128])
                    wb = wbf.get()
                    S.tt("pool", wb[:, :, 0:128], w[:, :, 0:128], nw.rr("p (k o) -> p k o", o=1).bc([128, 16, 128]), ALU.mult)
                    stg = stf.get()
                    for q0 in range(0, ntok, 512):
                        n = min(512, ntok - q0)
                        ps = PA.get()
                        for kc in range(16):
                            S.mm(ps[:, 0:n], wb[:, kc, 0:128], hnT[:, kc, q0:q0 + n], start=(kc == 0), stop=(kc == 15))
                        evac(stg[:, q0:q0 + n], ps[:, 0:n])
                    S.dma("act", ZF[r0:r0 + 128, sb0 * 128:sb0 * 128 + ntok], stg[:, 0:ntok])
                for (pieces, wd, z0) in tm_groups:
                    w = wst.get()
                    for (c0, pw, off) in pieces:
                        S.dma("sp", w[:, :, off:off + pw], Wv[:, :, c0:c0 + pw])
                    used = max(off + pw for (c0, pw, off) in pieces)
                    wb = wbf.get()
                    S.tt("pool", wb[:, :, 0:used], w[:, :, 0:used], nw.rr("p (k o) -> p k o", o=1).bc([128, 16, used]), ALU.mult)
                    for j in tiles:
                        jj = j - sb0
                        ps = PA.get()
                        for kc in range(16):
                            S.mm(ps[:, 0:used], hnT[:, kc, jj * 128:(jj + 1) * 128], wb[:, kc, 0:used], start=(kc == 0), stop=(kc == 15))
                        sg = stt_.get()
                        evac(sg[:, 0:used], ps[:, 0:used])
                        S.dma("act", ZT[j * 128:(j + 1) * 128, z0:z0 + wd], sg[:, 0:wd])

    def phaseA2(jobs):
        with ExitStack() as st:
            ltiles = []

            def lsb(name, shape, dt=F32):
                uid[0] += 1
                t_ = Tile(st.enter_context(nc.sbuf_tensor(f"{name}_{uid[0]}", list(shape), dt)).ap())
                ltiles.append(t_)
                return t_
            st.callback(lambda: S.barrier(ltiles + psA + psB + [psC]))
            xin = Pool([lsb(f"xin{i}", [128, Lp + 4], BF16) for i in range(3)])
            ob = Pool([lsb(f"ob{i}", [128, Lp], BF16) for i in range(3)])
            cw = Pool([lsb(f"cw{i}", [128, 8]) for i in range(3)])
            dgp = Pool([lsb(f"dg{i}", [128, 5, 128], BF16) for i in range(3)])
            tmpf = Pool([lsb(f"tf{i}", [128, 512]) for i in range(3)])
            sq = Pool([lsb(f"sq{i}", [128, 512], BF16) for i in range(3)])
            rs = Pool([lsb(f"rs{i}", [128, 512]) for i in range(3)])
            for x in xin.tiles:
                S.memset("pool", x[:, 0:2], 0.0)
                S.memset("pool", x[:, Lp + 2:Lp + 4], 0.0)
            for ji, job in enumerate(jobs):
                r0 = job["row"]
                x = xin.get()
                S.dma("sp", x[:, 2:Lp + 2], ZF[r0:r0 + 128, 0:Lp])
                c = cw.get()
                o = ob.get()
                conv = job["kind"] == "conv"
                nm = job["norm"]
                if conv:
                    S.dma("sp", c[:, 0:6], job["cw"])
                    dg = dgp.get()
                    for k in range(5):
                        S.ts("pool", dg[:, k, :], CB(C_ID), c[:, k:k + 1], None, ALU.mult)
                else:
                    S.dma("sp", c[:, 0:1], job["wcol"])
                for q0 in range(0, Lp, 512):
                    n = min(512, Lp - q0)
                    if conv:
                        ps = PA.get()
                        for k in range(5):
                            S.mm(ps[:, 0:n], dg[:, k, :], x[:, q0 + k:q0 + k + n], start=(k == 0), stop=(k == 4))
                        if nm is None:
                            S.act(o[:, q0:q0 + n], ps[:, 0:n], AF.Silu, bias=c[:, 5:6])
                            continue
                        t = tmpf.get()
                        S.act(t[:, 0:n], ps[:, 0:n], AF.Silu, bias=c[:, 5:6])
                        src = t[:, 0:n]
                    else:
                        src = x[:, 2 + q0:2 + q0 + n]
                    s2 = sq.get()
                    S.act(s2[:, 0:n], src, AF.Square)
                    ps2 = PA.get()
                    S.mm(ps2[:, 0:n], CB(C_ONES), s2[:, 0:n])
                    r = rs.get()
                    if nm == "l2q":
                        S.act(r[:, 0:n], ps2[:, 0:n], AF.Sqrt, bias=128.0 * EPS, scale=128.0)
                    elif nm == "l2k":
                        S.act(r[:, 0:n], ps2[:, 0:n], AF.Sqrt, bias=EPS, scale=1.0)
                    elif nm == "rmsq":
                        S.act(r[:, 0:n], ps2[:, 0:n], AF.Sqrt, bias=128.0 * EPS, scale=1.0)
                    else:
                        S.act(r[:, 0:n], ps2[:, 0:n], AF.Sqrt, bias=EPS, scale=1.0 / 128.0)
                    S.recip(r[:, 0:n], r[:, 0:n])
                    if conv:
                        S.tt("dve", o[:, q0:q0 + n], src, r[:, 0:n], ALU.mult)
                    else:
                        t = tmpf.get()
                        S.act(t[:, 0:n], src, AF.Copy, scale=c[:, 0:1])
                        S.tt("dve", o[:, q0:q0 + n], t[:, 0:n], r[:, 0:n], ALU.mult)
                S.dma("act", ZB[r0:r0 + 128, 0:Lp], o)

    def dirconst(d):
        if d == 0:
            return C_UFW, C_SUFW, C_NEGFW
        return C_UBW, C_SUBW, C_NEGBW

    def phaseC(Wout, Hin, Hout):
        with ExitStack() as st:
            ltiles = []

            def lsb(name, shape, dt=F32):
                uid[0] += 1
                t_ = Tile(st.enter_context(nc.sbuf_tensor(f"{name}_{uid[0]}", list(shape), dt)).ap())
                ltiles.append(t_)
                return t_
            st.callback(lambda: S.barrier(ltiles + psA + psB + [psC]))
            wst = Pool([lsb(f"cwst{i}", [128, 8, 512]) for i in range(2)])
            wb = lsb("cwb", [128, 32, 512], BF16)
            mt = Pool([lsb(f"cmt{i}", [128, 32, 128], BF16) for i in range(3)])
            hh = Pool([lsb(f"chh{i}", [128, 512]) for i in range(3)])
            Wv = Wout.rr("(kc p) n -> p kc n", p=128)
            MTv = MT.rr("(kc p) t -> p kc t", p=128)
            for cg in range(4):
                c0 = cg * 512
                for k4 in range(4):
                    w = wst.get()
                    S.dma("sp", w, Wv[:, k4 * 8:(k4 + 1) * 8, c0:c0 + 512])
                    S.copy("pool", wb[:, k4 * 8:(k4 + 1) * 8, :], w)
                for j in range(NT):
                    m = mt.get()
                    S.dma("sp", m, MTv[:, :, j * 128:(j + 1) * 128])
                    h = hh.get()
                    S.dma("sp", h, Hin[j * 128:(j + 1) * 128, c0:c0 + 512])
                    ps = PA.get()
                    for kc in range(32):
                        S.mm(ps, m[:, kc, :], wb[:, kc, :], start=(kc == 0), stop=(kc == 31))
                    S.tt("dve", h, h, ps, ALU.add)
                    S.dma("act", Hout[j * 128:(j + 1) * 128, c0:c0 + 512], h)

    ctx = dict(nc=nc, S=S, NT=NT, Lp=Lp, CF=CF, CB=CB, PA=PA, PB=PB, psC=psC, tmask=tmask, negm=negm,
               ZF=ZF, ZB=ZB, ZT=ZT, YF=YF, MT=MT, evac=evac, dirconst=dirconst, EV=EV, OD=OD, DBG=DBG)

    Hcur = H0
    for layer in range(DEPTH):
        i = layer // 2
        last = (layer == DEPTH - 1)
        Hout = Y if last else HB[layer % 2]
        if layer % 2 == 0:
            fm, tm = even_groups()
            phaseA(layer, Hcur, EV["w_in"][i], fm, tm)
            phaseA2(even_jobs(EV, i))
            mlstm_phase(ctx, i)
            ssd_phase(ctx, i)
            phaseC(EV["w_out"][i], Hcur, Hout)
        else:
            fm, tm = odd_groups()
            phaseA(layer, Hcur, OD["w_in"][i], fm, tm)
            phaseA2(odd_jobs(OD, i))
            na_phase(ctx, i)
            gdn_phase(ctx, i)
            phaseC(OD["w_out"][i], Hcur, Hout)
        Hcur = Hout
    scratch = dict(yf=YF, mt=MT)
    for name in DBG:
        S.dma("sp", DBG[name], scratch[name])
    for e in ("sp", "act", "pool"):
        S.drain(e, [Y] + list(DBG.values()))
    es.close()
    return nc, S


EV_ZF = dict(q=0, k=1024, x=2048, b=4096, c=5120)
EV_ZT = dict(v=0, o=2048, z=4096, zb=6144, g=8192, dt=8192 + 32)
OD_ZF = dict(qc=0, kc=2048, qd=4096, kd=6144, vd=8192)
OD_ZT = dict(vc=0, zc=2048, zd=4096, g=6144)


def even_groups():
    fm = []
    for name, src, n in (("q", 0, 1024), ("k", 1, 1024), ("x", 7, 2048), ("b", 8, 1024), ("c", 9, 1024)):
        for g in range(n // 128):
            fm.append((EOFF[src] + g * 128, EV_ZF[name] + g * 128))
    tm = []
    for name, src, n in (("v", 2, 2048), ("o", 3, 2048), ("z", 4, 2048), ("zb", 6, 2048)):
        for g in range(n // 256):
            tm.append(([(EOFF[src] + g * 256, 256, 0)], 256, EV_ZT[name] + g * 256))
    tm.append(([(EOFF[5], 32, 0), (EOFF[10], 64, 32)], 128, EV_ZT["g"]))
    return fm, tm


def odd_groups():
    fm = []
    for name, src in (("qc", 0), ("kc", 1), ("qd", 4), ("kd", 5), ("vd", 6)):
        for g in range(16):
            fm.append((OOFF[src] + g * 128, OD_ZF[name] + g * 128))
    tm = []
    for name, src in (("vc", 2), ("zc", 3), ("zd", 7)):
        for g in range(8):
            tm.append(([(OOFF[src] + g * 256, 256, 0)], 256, OD_ZT[name] + g * 256))
    tm.append(([(OOFF[8], 64, 0)], 128, OD_ZT["g"]))
    return fm, tm


def even_jobs(EV, i):
    jobs = []
    for g in range(8):
        jobs.append(dict(row=EV_ZF["q"] + g * 128, kind="conv", cw=EV["cab"][i, g], norm=None, post=1.0))
    for g in range(8):
        jobs.append(dict(row=EV_ZF["k"] + g * 128, kind="conv", cw=EV["cab"][i, 8 + g], norm=None, post=1.0))
    for g in range(32):
        jobs.append(dict(row=EV_ZF["x"] + g * 128, kind="conv", cw=EV["cbb"][i, g], norm=None, post=1.0))
    return jobs


def odd_jobs(OD, i):
    jobs = []
    for g in range(16):
        jobs.append(dict(row=OD_ZF["qc"] + g * 128, kind="rms", norm="rmsq", wcol=OD["qkn"][i, 0], post=1.0))
    for g in range(16):
        jobs.append(dict(row=OD_ZF["kc"] + g * 128, kind="rms", norm="rmsk", wcol=OD["qkn"][i, 1], post=1.0))
    for g in range(16):
        jobs.append(dict(row=OD_ZF["qd"] + g * 128, kind="conv", cw=OD["cdb"][i, g], norm="l2q", post=1.0))
    for g in range(16):
        jobs.append(dict(row=OD_ZF["kd"] + g * 128, kind="conv", cw=OD["cdb"][i, 16 + g], norm="l2k", post=1.0))
    for g in range(16):
        jobs.append(dict(row=OD_ZF["vd"] + g * 128, kind="conv", cw=OD["cdb"][i, 32 + g], norm=None, post=1.0))
    return jobs


def _softplus(S, out, in_, tmp):
    S.act(tmp, in_, AF.Exp)
    S.act(out, tmp, AF.Ln, bias=1.0, scale=1.0)


def _transpose_store(ctx, src_bf, MTrow0, c, mtp):
    S, PB, CB, MT, evac = ctx["S"], ctx["PB"], ctx["CB"], ctx["MT"], ctx["evac"]
    m = mtp.get()
    for half in range(2):
        pb = PB.get()
        for k in range(8):
            kc = half * 8 + k
            S.tr(pb[:, k * 128:(k + 1) * 128], src_bf[:, kc * 128:(kc + 1) * 128], CB(C_ID))
        evac(m[:, half * 8:(half + 1) * 8, :], pb.rr("p (k t) -> p k t", k=8))
    S.dma("act", MT[MTrow0:MTrow0 + 2048, c * 128:(c + 1) * 128].rr("(k p) t -> p k t", p=128), m)


def _group_rms(ctx, y, ngrp, gsz, scr, st8, wtile, out_bf):
    S = ctx["S"]
    S.act(scr, y, AF.Square)
    S.op("dve", lambda g: g.tensor_reduce(st8.ap[:, 0:ngrp], scr.ap.rearrange("p (g c) -> p g c", g=ngrp),
                                          mybir.AxisListType.X, ALU.add), [scr], [st8])
    S.act(st8[:, 0:ngrp], st8[:, 0:ngrp], AF.Sqrt, bias=EPS, scale=1.0 / gsz)
    S.recip(st8[:, 0:ngrp], st8[:, 0:ngrp])
    S.tt("dve", scr.rr("p (g c) -> p g c", g=ngrp), y.rr("p (g c) -> p g c", g=ngrp),
         st8[:, 0:ngrp].rr("p (g o) -> p g o", o=1).bc([128, ngrp, gsz]), ALU.mult)
    S.tt("pool", out_bf, scr, wtile, ALU.mult)


def mlstm_phase(ctx, i):
    nc, S, NT = ctx["nc"], ctx["S"], ctx["NT"]
    CF, CB, PA, PB, psC = ctx["CF"], ctx["CB"], ctx["PA"], ctx["PB"], ctx["psC"]
    tmask, negm, ZB, ZT, YF, EV = ctx["tmask"], ctx["negm"], ctx["ZB"], ctx["ZT"], ctx["YF"], ctx["EV"]
    with ExitStack() as st:
        ltiles = []

        def lsb(name, shape, dt=F32):
            t_ = Tile(st.enter_context(nc.sbuf_tensor(f"a{i}_" + name, list(shape), dt)).ap())
            ltiles.append(t_)
            return t_
        st.callback(lambda: S.barrier(ltiles + PA.tiles + PB.tiles + [psC]))
        qk = Pool([lsb(f"qk{j}", [128, 16, 128], BF16) for j in range(2)])
        vf = Pool([lsb(f"vf{j}", [128, 2048]) for j in range(2)])
        vb = Pool([lsb(f"vb{j}", [128, 8, 257], BF16) for j in range(2)])
        gt = Pool([lsb(f"g{j}", [128, 128]) for j in range(2)])
        gw = Pool([lsb(f"gw{j}", [128, 96]) for j in range(2)])
        lbp = Pool([lsb(f"lb{j}", [128, 128]) for j in range(3)])
        dtp = Pool([lsb(f"dt{j}", [128, 128]) for j in range(3)])
        spp = Pool([lsb(f"sp{j}", [128, 128], BF16) for j in range(3)])
        kwp = Pool([lsb(f"kw{j}", [128, 128], BF16) for j in range(3)])
        ysp = Pool([lsb(f"ys{j}", [128, 260]) for j in range(3)])
        yst = Pool([lsb(f"yst{j}", [128, 2048]) for j in range(2)])
        yfl = Pool([lsb(f"yfl{j}", [128, 2048]) for j in range(2)])
        oz = Pool([lsb(f"oz{j}", [128, 4096]) for j in range(2)])
        scr = lsb("scr", [128, 2048])
        st8 = lsb("st8", [128, 8])
        mbf = Pool([lsb(f"mbf{j}", [128, 2048], BF16) for j in range(2)])
        mtp = Pool([lsb(f"mtp{j}", [128, 16, 128], BF16) for j in range(2)])
        X = lsb("X", [128, 8, 257])
        Xb = lsb("Xb", [128, 8, 257], BF16)
        gbias = lsb("gbias", [128, 32])
        hnorm = lsb("hnorm", [128, 2048])
        S.dma("sp", gbias, EV["gbias"][i])
        S.dma("sp", hnorm, EV["hnorm"][i])
        for v in vb.tiles:
            S.memset("pool", v[:, :, 256:257], 1.0)
        for d in (0, 1):
            cU, cSU, cNEG = ctx["dirconst"](d)
            S.memset("pool", X, 0.0)
            S.memset("pool", Xb, 0.0)
            order = range(NT) if d == 0 else range(NT - 1, -1, -1)
            for c in order:
                tok = slice(c * 128, (c + 1) * 128)
                q = qk.get()
                S.dma("sp", q, ZB[0:2048, tok].rr("(h p) t -> p h t", p=128))
                v32 = vf.get()
                S.dma("sp", v32, ZT[tok, 0:2048])
                v = vb.get()
                S.copy("pool", v[:, :, 0:256], v32.rr("p (h c) -> p h c", h=8))
                g = gt.get()
                S.dma("sp", g, ZT[tok, 8192:8320])
                w = gw.get()
                S.tt("dve", w[:, 0:8], g[:, d * 8:d * 8 + 8], gbias[:, d * 8:d * 8 + 8], ALU.add)
                S.ts("dve", w[:, 0:8], w[:, 0:8], negm[:, c:c + 1], -0.5 * float(np.log(128.0)), ALU.add, ALU.add)
                S.tt("dve", w[:, 8:16], g[:, 16 + d * 8:24 + d * 8], gbias[:, 16 + d * 8:24 + d * 8], ALU.add)
                S.act(w[:, 16:24], w[:, 8:16], AF.Exp, scale=-1.0)
                S.act(w[:, 16:24], w[:, 16:24], AF.Ln, bias=1.0, scale=1.0)
                S.ts("dve", w[:, 8:16], w[:, 16:24], tmask[:, c:c + 1], -1.0, ALU.mult, ALU.mult)
                S.mm(psC[:, 0:8], CF(cU), w[:, 8:16])
                S.mm(psC[:, 8:16], CF(C_ONES), w[:, 8:16])
                S.copy("dve", w[:, 24:40], psC[:, 0:16])
                S.act(w[:, 40:56], w[:, 24:40], AF.Exp)
                S.tt("dve", w[:, 56:64], w[:, 32:40], w[:, 24:32], ALU.subtract)
                S.tt("dve", w[:, 56:64], w[:, 56:64], w[:, 0:8], ALU.add)
                S.act(w[:, 64:72], w[:, 56:64], AF.Exp)
                if d == 0:
                    yo = yst.get()
                else:
                    yo = yst.get()
                    yf = yfl.get()
                    S.dma("sp", yf, YF[tok, 0:2048])
                def stage1(h):
                    qT = q[:, h, :]
                    kT = q[:, 8 + h, :]
                    lb = lbp.get()
                    S.act(lb, CF(cSU), AF.Copy, scale=w[:, 8 + h:9 + h])
                    psD = PA.get()
                    S.mm(psD[:, 0:128], lb, CF(cU), start=True, stop=False)
                    S.mm(psD[:, 0:128], CF(C_ID), CF(cNEG), start=False, stop=True)
                    S.mm(psD[:, 128:256], kT, qT)
                    dtm = dtp.get()
                    S.act(dtm, psD[:, 0:128], AF.Exp, bias=w[:, h:h + 1], scale=1.0)
                    sp = spp.get()
                    S.tt("dve", sp, psD[:, 128:256], dtm, ALU.mult)
                    pt = PB.get()
                    S.tr(pt[:, 0:128], kT, CB(C_ID))
                    kw = kwp.get()
                    S.act(kw, pt[:, 0:128], AF.Copy, scale=w[:, 64 + h:65 + h])
                    return sp, kw

                def stage2(h, sp, kw):
                    qT = q[:, h, :]
                    ps1 = PA.get()
                    S.mm(ps1[:, 0:257], sp, v[:, h, :])
                    ps2 = PA.get()
                    S.mm(ps2[:, 0:257], qT, Xb[:, h, :])
                    ys = ysp.get()
                    S.act(ys[:, 0:257], ps2[:, 0:257], AF.Copy, scale=w[:, 40 + h:41 + h])
                    S.tt("dve", ys[:, 0:257], ys[:, 0:257], ps1[:, 0:257], ALU.add)
                    S.act(ys[:, 257:258], ys[:, 256:257], AF.Abs)
                    S.ts("dve", ys[:, 258:259], ys[:, 257:258], 1.0, None, ALU.max)
                    S.recip(ys[:, 259:260], ys[:, 258:259])
                    if d == 0:
                        S.ts("dve", yo[:, h * 256:(h + 1) * 256], ys[:, 0:256], ys[:, 259:260], None, ALU.mult)
                    else:
                        S.stt("dve", yo[:, h * 256:(h + 1) * 256], ys[:, 0:256], ys[:, 259:260],
                              yf[:, h * 256:(h + 1) * 256], ALU.mult, ALU.add)
                    ps3 = PA.get()
                    S.mm(ps3[:, 0:257], kw, v[:, h, :])
                    S.stt("dve", X[:, h, :], X[:, h, :], w[:, 48 + h:49 + h], ps3[:, 0:257], ALU.mult, ALU.add)
                    S.copy("act", Xb[:, h, :], X[:, h, :])

                pend = None
                for h in range(9):
                    cur = stage1(h) if h < 8 else None
                    if pend is not None:
                        stage2(h - 1, *pend)
                    pend = cur
                if d == 0:
                    S.dma("act", YF[tok, 0:2048], yo)
                else:
                    o_z = oz.get()
                    S.dma("sp", o_z, ZT[tok, 2048:6144])
                    mb = mbf.get()
                    _group_rms(ctx, yo, 8, 256, scr, st8, hnorm, mb)
                    S.act(o_z[:, 0:2048], o_z[:, 0:2048], AF.Sigmoid)
                    S.act(o_z[:, 2048:4096], o_z[:, 2048:4096], AF.Silu)
                    S.tt("pool", o_z[:, 0:2048], o_z[:, 0:2048], o_z[:, 2048:4096], ALU.mult)
                    S.tt("dve", mb, mb, o_z[:, 0:2048], ALU.mult)
                    _transpose_store(ctx, mb, 0, c, mtp)


def ssd_phase(ctx, i):
    nc, S, NT = ctx["nc"], ctx["S"], ctx["NT"]
    CF, CB, PA, PB, psC = ctx["CF"], ctx["CB"], ctx["PA"], ctx["PB"], ctx["psC"]
    tmask, ZB, ZT, YF, EV, evac = ctx["tmask"], ctx["ZB"], ctx["ZT"], ctx["YF"], ctx["EV"], ctx["evac"]
    with ExitStack() as st:
        ltiles = []

        def lsb(name, shape, dt=F32):
            t_ = Tile(st.enter_context(nc.sbuf_tensor(f"b{i}_" + name, list(shape), dt)).ap())
            ltiles.append(t_)
            return t_
        st.callback(lambda: S.barrier(ltiles + PA.tiles + PB.tiles + [psC]))
        cbt = Pool([lsb(f"cbt{j}", [128, 16, 128], BF16) for j in range(2)])
        xt = Pool([lsb(f"xt{j}", [128, 16, 128], BF16) for j in range(2)])
        dtr = Pool([lsb(f"dtr{j}", [128, 128]) for j in range(2)])
        gw = Pool([lsb(f"gw{j}", [128, 256]) for j in range(2)])
        xs = Pool([lsb(f"xs{j}", [128, 2048], BF16) for j in range(2)])
        xdt = Pool([lsb(f"xdt{j}", [128, 32, 64], BF16) for j in range(2)])
        xw = Pool([lsb(f"xw{j}", [128, 32, 64], BF16) for j in range(2)])
        btm = Pool([lsb(f"btm{j}", [128, 8, 128], BF16) for j in range(2)])
        cbs = Pool([lsb(f"cbs{j}", [128, 128]) for j in range(2)])
        lbp = Pool([lsb(f"lb{j}", [128, 128]) for j in range(8)])
        dtp = Pool([lsb(f"dt{j}", [128, 512]) for j in range(3)])
        spp = Pool([lsb(f"sp{j}", [128, 4, 128], BF16) for j in range(3)])
        y1p = Pool([lsb(f"y1{j}", [128, 256]) for j in range(2)])
        yst = Pool([lsb(f"yst{j}", [128, 2048]) for j in range(2)])
        yfl = Pool([lsb(f"yfl{j}", [128, 2048]) for j in range(2)])
        zb = Pool([lsb(f"zb{j}", [128, 2048]) for j in range(2)])
        scr = lsb("scr", [128, 2048])
        st8 = lsb("st8", [128, 8])
        mbf = Pool([lsb(f"mbf{j}", [128, 2048], BF16) for j in range(2)])
        mtp = Pool([lsb(f"mtp{j}", [128, 16, 128], BF16) for j in range(2)])
        St = lsb("St", [128, 8, 256])
        Sb = lsb("Sb", [128, 8, 256], BF16)
        dtb = lsb("dtb", [128, 64])
        nA = lsb("nA", [128, 64])
        dskip = lsb("dskip", [128, 32])
        gnorm = lsb("gnorm", [128, 2048])
        S.dma("sp", dtb, EV["dtb"][i])
        S.dma("sp", nA, EV["alog"][i])
        S.dma("sp", dskip, EV["dskip"][i])
        S.dma("sp", gnorm, EV["gnorm"][i])
        S.act(nA, nA, AF.Exp)
        S.ts("dve", nA, nA, -1.0, None, ALU.mult)
        for d in (0, 1):
            cU, cSU, cNEG = ctx["dirconst"](d)
            S.memset("pool", St, 0.0)
            S.memset("pool", Sb, 0.0)
            order = range(NT) if d == 0 else range(NT - 1, -1, -1)
            for c in order:
                tok = slice(c * 128, (c + 1) * 128)
                cb_ = cbt.get()
                S.dma("sp", cb_, ZB[4096:6144, tok].rr("(g p) t -> p g t", p=128))
                x_ = xt.get()
                S.dma("sp", x_, ZB[2048:4096, tok].rr("(g p) t -> p g t", p=128))
                dr = dtr.get()
                S.dma("sp", dr, ZT[tok, 8192:8320])
                w = gw.get()
                S.tt("dve", w[:, 32:64], dr[:, 32 + d * 32:64 + d * 32], dtb[:, d * 32:(d + 1) * 32], ALU.add)
                S.act(w[:, 32:64], w[:, 32:64], AF.Exp)
                S.act(w[:, 32:64], w[:, 32:64], AF.Ln, bias=1.0, scale=1.0)
                S.ts("dve", w[:, 0:32], w[:, 32:64], tmask[:, c:c + 1], None, ALU.mult)
                S.tt("dve", w[:, 32:64], w[:, 0:32], nA[:, d * 32:(d + 1) * 32], ALU.mult)
                S.mm(psC[:, 0:32], CF(cU), w[:, 32:64])
                S.mm(psC[:, 32:64], CF(C_ONES), w[:, 32:64])
                S.copy("dve", w[:, 64:128], psC[:, 0:64])
                S.act(w[:, 128:192], w[:, 64:128], AF.Exp)
                S.tt("dve", w[:, 192:224], w[:, 96:128], w[:, 64:96], ALU.subtract)
                S.act(w[:, 192:224], w[:, 192:224], AF.Exp)
                xs_ = xs.get()
                xd = xdt.get()
                xw_ = xw.get()
                for half in range(2):
                    pb = PB.get()
                    for k in range(8):
                        S.tr(pb[:, k * 128:(k + 1) * 128], x_[:, half * 8 + k, :], CB(C_ID))
                    evac(xs_[:, half * 1024:(half + 1) * 1024], pb)
                S.tt("dve", xd, xs_.rr("p (h c) -> p h c", h=32),
                     w[:, 0:32].rr("p (h o) -> p h o", o=1).bc([128, 32, 64]), ALU.mult)
                S.tt("pool", xw_, xd, w[:, 192:224].rr("p (h o) -> p h o", o=1).bc([128, 32, 64]), ALU.mult)
                bt_ = btm.get()
                pb = PB.get()
                for g in range(8):
                    S.tr(pb[:, g * 128:(g + 1) * 128], cb_[:, g, :], CB(C_ID))
                evac(bt_, pb.rr("p (g t) -> p g t", g=8))
                yo = yst.get()
                if d == 1:
                    yf = yfl.get()
                    S.dma("sp", yf, YF[tok, 2048:4096])
                def stage1(g):
                    psCB = PA.get()
                    S.mm(psCB[:, 0:128], cb_[:, g, :], cb_[:, 8 + g, :])
                    cs = cbs.get()
                    S.tt("dve", cs, psCB[:, 0:128], CF(cU), ALU.mult)
                    psD = PA.get()
                    for r in range(4):
                        hd = g * 4 + r
                        lb = lbp.get()
                        S.act(lb, CF(cSU), AF.Copy, scale=w[:, 32 + hd:33 + hd])
                        S.mm(psD[:, r * 128:(r + 1) * 128], lb, CF(cU))
                    dtm = dtp.get()
                    S.act(dtm, psD, AF.Exp)
                    sp = spp.get()
                    S.tt("dve", sp, dtm.rr("p (r t) -> p r t", r=4),
                         cs.rr("p (o t) -> p o t", o=1).bc([128, 4, 128]), ALU.mult)
                    return sp

                def stage2(g, sp):
                    psY = PA.get()
                    for r in range(4):
                        S.mm(psY[:, r * 64:(r + 1) * 64], sp[:, r, :], xd[:, g * 4 + r, :])
                    psY2 = PA.get()
                    S.mm(psY2[:, 0:256], cb_[:, 8 + g, :], Sb[:, g, :])
                    y1 = y1p.get()
                    S.tt("dve", y1.rr("p (r c) -> p r c", r=4), psY2[:, 0:256].rr("p (r c) -> p r c", r=4),
                         w[:, 128 + g * 4:132 + g * 4].rr("p (r o) -> p r o", o=1).bc([128, 4, 64]), ALU.mult)
                    if d == 0:
                        S.tt("dve", yo[:, g * 256:(g + 1) * 256], y1, psY[:, 0:256], ALU.add)
                    else:
                        S.tt("dve", y1, y1, psY[:, 0:256], ALU.add)
                        S.tt("dve", yo[:, g * 256:(g + 1) * 256], y1, yf[:, g * 256:(g + 1) * 256], ALU.add)
                    psS = PA.get()
                    S.mm(psS[:, 0:256], bt_[:, g, :], xw_[:, g * 4:(g + 1) * 4, :].rr("p r c -> p (r c)"))
                    S.tt("dve", St[:, g, :].rr("p (r c) -> p r c", r=4), St[:, g, :].rr("p (r c) -> p r c", r=4),
                         w[:, 160 + g * 4:164 + g * 4].rr("p (r o) -> p r o", o=1).bc([128, 4, 64]), ALU.mult)
                    S.tt("dve", St[:, g, :], St[:, g, :], psS[:, 0:256], ALU.add)
                    S.copy("act", Sb[:, g, :], St[:, g, :])

                pend = None
                for g in range(9):
                    cur = stage1(g) if g < 8 else None
                    if pend is not None:
                        stage2(g - 1, pend)
                    pend = cur
                if d == 0:
                    S.dma("act", YF[tok, 2048:4096], yo)
                else:
                    z_ = zb.get()
                    S.dma("sp", z_, ZT[tok, 6144:8192])
                    S.tt("pool", scr.rr("p (h c) -> p h c", h=32), xs_.rr("p (h c) -> p h c", h=32),
                         dskip.rr("p (h o) -> p h o", o=1).bc([128, 32, 64]), ALU.mult)
                    S.tt("dve", yo, yo, scr, ALU.add)
                    S.act(z_, z_, AF.Silu)
                    S.tt("dve", yo, yo, z_, ALU.mult)
                    mb = mbf.get()
                    _group_rms(ctx, yo, 8, 256, scr, st8, gnorm, mb)
                    _transpose_store(ctx, mb, 2048, c, mtp)


def na_phase(ctx, i):
    nc, S, NT = ctx["nc"], ctx["S"], ctx["NT"]
    CF, CB, PA, PB, psC = ctx["CF"], ctx["CB"], ctx["PA"], ctx["PB"], ctx["psC"]
    ZB, ZT, OD = ctx["ZB"], ctx["ZT"], ctx["OD"]
    with ExitStack() as st:
        ltiles = []

        def lsb(name, shape, dt=F32):
            t_ = Tile(st.enter_context(nc.sbuf_tensor(f"c{i}_" + name, list(shape), dt)).ap())
            ltiles.append(t_)
            return t_
        st.callback(lambda: S.barrier(ltiles + PA.tiles + PB.tiles + [psC]))
        biasb = lsb("biasb", [128, 16, 896], BF16)
        stg = Pool([lsb(f"stg{j}", [128, 896]) for j in range(2)])
        maskb = Pool([lsb(f"maskb{j}", [128, 896], BF16) for j in range(2)])
        kring = Pool([lsb(f"kr{j}", [128, 16, 128], BF16) for j in range(8)])
        vring = Pool([lsb(f"vr{j}", [128, 16, 129], BF16) for j in range(8)])
        vst = Pool([lsb(f"vst{j}", [128, 2048]) for j in range(2)])
        qp = Pool([lsb(f"q{j}", [128, 16, 128], BF16) for j in range(2)])
        ptp = Pool([lsb(f"pt{j}", [128, 512], BF16) for j in range(4)])
        pmp = Pool([lsb(f"pm{j}", [32, 128], BF16) for j in range(2)])
        rdp = Pool([lsb(f"rd{j}", [128, 2]) for j in range(4)])
        op_ = Pool([lsb(f"o{j}", [128, 2048]) for j in range(2)])
        zp = Pool([lsb(f"z{j}", [128, 2048]) for j in range(2)])
        mbf = Pool([lsb(f"mbf{j}", [128, 2048], BF16) for j in range(2)])
        mtp = Pool([lsb(f"mtp{j}", [128, 16, 128], BF16) for j in range(2)])
        kmeta = lsb("kmeta", [128, 16, 32], BF16)
        vmst = lsb("vmst", [32, 2048])
        vmeta = lsb("vmeta", [32, 16, 129], BF16)
        mbias = lsb("mbias", [32, 1])
        S.dma("sp", mbias, OD["mbias"])
        for h in range(16):
            sg = stg.get()
            S.dma("sp", sg, OD["bias"][i, h])
            S.copy("pool", biasb[:, h, :], sg)
        for v in vring.tiles:
            S.memset("pool", v[:, :, 128:129], 1.0)
        S.memset("pool", vmeta[:, :, 128:129], 1.0)
        S.dma("sp", kmeta, ZB[2048:4096, 96:128].rr("(h p) t -> p h t", p=128))
        S.dma("sp", vmst, ZT[96:128, 0:2048])
        S.copy("pool", vmeta[:, :, 0:128], vmst.rr("p (h c) -> p h c", h=16))
        loaded = {}

        def get_kv(ki):
            if ki not in loaded:
                k = kring.get()
                v = vring.get()
                S.dma("sp", k, ZB[2048:4096, ki * 128:(ki + 1) * 128].rr("(h p) t -> p h t", p=128))
                vs = vst.get()
                S.dma("sp", vs, ZT[ki * 128:(ki + 1) * 128, 0:2048])
                S.copy("pool", v[:, :, 0:128], vs.rr("p (h c) -> p h c", h=16))
                loaded[ki] = (k, v)
            return loaded[ki]

        for qi in range(NT):
            tok = slice(qi * 128, (qi + 1) * 128)
            q = qp.get()
            S.dma("sp", q, ZB[0:2048, tok].rr("(h p) t -> p h t", p=128))
            valid = [(di, qi + di - 3) for di in range(7) if 1 <= qi + di - 3 < NT]
            kv = {ki: get_kv(ki) for (_, ki) in valid}
            sg = stg.get()
            S.dma("sp", sg, OD["mask"][qi])
            mk = maskb.get()
            S.copy("pool", mk, sg)
            z_ = zp.get()
            S.dma("sp", z_, ZT[tok, 2048:4096])
            o_ = op_.get()
            for h in range(16):
                banks = []
                for b0 in range(0, len(valid), 4):
                    grp = valid[b0:b0 + 4]
                    ps = PA.get()
                    for sl, (di, ki) in enumerate(grp):
                        dst = ps[:, sl * 128:(sl + 1) * 128]
                        S.mm(dst, kv[ki][0][:, h, :], q[:, h, :], start=True, stop=False)
                        S.mm(dst, CB(C_ID), biasb[:, h, di * 128:(di + 1) * 128], start=False, stop=False)
                        S.mm(dst, CB(C_ID), mk[:, di * 128:(di + 1) * 128], start=False, stop=True)
                    pt = ptp.get()
                    n = len(grp) * 128
                    S.act(pt[:, 0:n], ps[:, 0:n], AF.Exp)
                    banks.append((pt, grp))
                S.mm(psC[0:32, 0:128], kmeta[:, h, :], q[:, h, :])
                pm = pmp.get()
                S.act(pm, psC[0:32, 0:128], AF.Exp, bias=mbias[:, 0:1], scale=1.0)
                po = PA.get()
                first = True
                for (pt, grp) in banks:
                    for sl, (di, ki) in enumerate(grp):
                        S.mm(po[:, 0:129], pt[:, sl * 128:(sl + 1) * 128], kv[ki][1][:, h, :], start=first, stop=False)
                        first = False
                S.mm(po[:, 0:129], pm, vmeta[:, h, :], start=first, stop=True)
                rd = rdp.get()
                S.recip(rd[:, 0:1], po[:, 128:129])
                S.act(o_[:, h * 128:(h + 1) * 128], po[:, 0:128], AF.Copy, scale=rd[:, 0:1])
            S.act(z_, z_, AF.Silu)
            mb = mbf.get()
            S.tt("dve", mb, o_, z_, ALU.mult)
            _transpose_store(ctx, mb, 0, qi, mtp)


def gdn_phase(ctx, i):
    nc, S, NT = ctx["nc"], ctx["S"], ctx["NT"]
    CF, CB, PA, PB, psC = ctx["CF"], ctx["CB"], ctx["PA"], ctx["PB"], ctx["psC"]
    tmask, ZB, ZT, YF, OD = ctx["tmask"], ctx["ZB"], ctx["ZT"], ctx["YF"], ctx["OD"]
    with ExitStack() as st:
        ltiles = []

        def lsb(name, shape, dt=F32):
            t_ = Tile(st.enter_context(nc.sbuf_tensor(f"d{i}_" + name, list(shape), dt)).ap())
            ltiles.append(t_)
            return t_
        st.callback(lambda: S.barrier(ltiles + PA.tiles + PB.tiles + [psC]))
        q3 = Pool([lsb(f"q{j}", [128, 16, 128], BF16) for j in range(2)])
        k3 = Pool([lsb(f"k{j}", [128, 16, 128], BF16) for j in range(2)])
        v3 = Pool([lsb(f"v{j}", [128, 16, 128], BF16) for j in range(2)])
        gt = Pool([lsb(f"g{j}", [128, 128]) for j in range(2)])
        gw = Pool([lsb(f"gw{j}", [128, 160]) for j in range(2)])
        lbp = Pool([lsb(f"lb{j}", [128, 128]) for j in range(8)])
        etp = Pool([lsb(f"et{j}", [128, 4, 128]) for j in range(2)])
        ep = Pool([lsb(f"e{j}", [128, 4, 128]) for j in range(2)])
        atp = Pool([lsb(f"at{j}", [128, 4, 128], BF16) for j in range(2)])
        CHDT = F32
        KDT = BF16
        CI = CF(C_ID) if CHDT == F32 else CB(C_ID)
        Pp = [Pool([lsb(f"P{sl}{j}", [128, 4, 128], CHDT) for j in range(2)]) for sl in range(2)]
        PTp = [Pool([lsb(f"PT{sl}{j}", [128, 4, 128], CHDT) for j in range(2)]) for sl in range(2)]
        TTp = [Pool([lsb(f"TT{sl}{j}", [128, 4, 128], CHDT) for j in range(2)]) for sl in range(2)]
        TTup = Pool([lsb(f"TTu{j}", [128, 4, 128], KDT) for j in range(2)])
        ktmp = Pool([lsb(f"ktm{j}", [128, 4, 128], BF16) for j in range(2)])
        vtmp = Pool([lsb(f"vtm{j}", [128, 4, 128], BF16) for j in range(2)])
        kbep = Pool([lsb(f"kbe{j}", [128, 4, 128], KDT) for j in range(2)])
        kwep = Pool([lsb(f"kwe{j}", [128, 4, 128], BF16) for j in range(2)])
        vbep = Pool([lsb(f"vbe{j}", [128, 4, 128], KDT) for j in range(2)])
        nwmp = Pool([lsb(f"nwm{j}", [128, 4, 128], BF16) for j in range(2)])
        ubp = Pool([lsb(f"ub{j}", [128, 4, 128], BF16) for j in range(2)])
        y1p = Pool([lsb(f"y1{j}", [128, 4, 128]) for j in range(2)])
        yst = Pool([lsb(f"yst{j}", [128, 2048]) for j in range(2)])
        yfl = Pool([lsb("yfl0", [128, 2048])])
        zdp = Pool([lsb("zd0", [128, 2048])])
        scr = lsb("scr", [128, 2048])
        st16 = lsb("st16", [128, 16])
        mbf = Pool([lsb(f"mbf{j}", [128, 2048], BF16) for j in range(2)])
        mtp = Pool([lsb(f"mtp{j}", [128, 16, 128], BF16) for j in range(2)])
        St = lsb("St", [128, 16, 128])
        Sb = lsb("Sb", [128, 16, 128], BF16)
        dtb = lsb("dtb", [128, 32])
        nA = lsb("nA", [128, 32])
        gn = lsb("gn", [128, 2048])
        S.dma("sp", dtb, OD["dtb"][i])
        S.dma("sp", nA, OD["alog"][i])
        S.dma("sp", gn, OD["gnorm"][i])
        S.act(nA, nA, AF.Exp)
        S.ts("dve", nA, nA, -1.0, None, ALU.mult)

        def bc4(v):
            return v.rr("p (h o) -> p h o", o=1).bc([128, 4, 128])

        def m4(cidx):
            return CF(cidx).rr("p (o t) -> p o t", o=1).bc([128, 4, 128])

        for d in (0, 1):
            cU, cSU, cNEG = ctx["dirconst"](d)
            cST = C_SBW if d == 0 else C_SFW
            S.memset("pool", St, 0.0)
            S.memset("pool", Sb, 0.0)
            order = range(NT) if d == 0 else range(NT - 1, -1, -1)
            for c in order:
                tok = slice(c * 128, (c + 1) * 128)
                q = q3.get()
                k = k3.get()
                v = v3.get()
                S.dma("sp", q, ZB[4096:6144, tok].rr("(h p) t -> p h t", p=128))
                S.dma("sp", k, ZB[6144:8192, tok].rr("(h p) t -> p h t", p=128))
                S.dma("sp", v, ZB[8192:10240, tok].rr("(h p) t -> p h t", p=128))
                g = gt.get()
                S.dma("sp", g, ZT[tok, 6144:6272])
                w = gw.get()
                S.act(w[:, 0:16], g[:, d * 16:(d + 1) * 16], AF.Sigmoid)
                S.ts("dve", w[:, 0:16], w[:, 0:16], tmask[:, c:c + 1], None, ALU.mult)
                S.tt("dve", w[:, 16:32], g[:, 32 + d * 16:48 + d * 16], dtb[:, d * 16:(d + 1) * 16], ALU.add)
                S.act(w[:, 16:32], w[:, 16:32], AF.Exp)
                S.act(w[:, 16:32], w[:, 16:32], AF.Ln, bias=1.0, scale=1.0)
                S.tt("dve", w[:, 16:32], w[:, 16:32], nA[:, d * 16:(d + 1) * 16], ALU.mult)
                S.ts("dve", w[:, 16:32], w[:, 16:32], tmask[:, c:c + 1], None, ALU.mult)
                S.mm(psC[:, 0:16], CF(cU), w[:, 16:32])
                S.mm(psC[:, 16:32], CF(C_ONES), w[:, 16:32])
                S.copy("dve", w[:, 32:64], psC[:, 0:32])
                S.act(w[:, 64:96], w[:, 32:64], AF.Exp)
                S.tt("dve", w[:, 96:112], w[:, 48:64], w[:, 32:48], ALU.subtract)
                S.act(w[:, 96:112], w[:, 96:112], AF.Exp)
                S.tt("dve", w[:, 112:128], w[:, 0:16], w[:, 64:80], ALU.mult)
                S.ts("dve", w[:, 128:144], w[:, 0:16], -1.0, None, ALU.mult)
                yo = yst.get()
                if d == 1:
                    yf = yfl.get()
                    S.dma("sp", yf, YF[tok, 0:2048])
                def g_stage1(qd, sl):
                    H = [qd * 4 + r for r in range(4)]
                    lbs = []
                    for r, hd in enumerate(H):
                        lb = lbp.get()
                        S.act(lb, CF(cSU), AF.Copy, scale=w[:, 16 + hd:17 + hd])
                        lbs.append(lb)
                    psET = PA.get()
                    for r in range(4):
                        S.mm(psET[:, r * 128:(r + 1) * 128], lbs[r], CF(cU))
                    et = etp.get()
                    S.act(et.rr("p h t -> p (h t)"), psET, AF.Exp)
                    psE = PA.get()
                    for r in range(4):
                        S.mm(psE[:, r * 128:(r + 1) * 128], CF(cU), lbs[r])
                    e_ = ep.get()
                    S.act(e_.rr("p h t -> p (h t)"), psE, AF.Exp)
                    psQK = PA.get()
                    for r, hd in enumerate(H):
                        S.mm(psQK[:, r * 128:(r + 1) * 128], k[:, hd, :], q[:, hd, :])
                    S.tt("dve", et, et, m4(cU), ALU.mult)
                    at = atp.get()
                    S.tt("dve", at, psQK.rr("p (h t) -> p h t", h=4), et, ALU.mult)
                    psKK = PA.get()
                    for r, hd in enumerate(H):
                        S.mm(psKK[:, r * 128:(r + 1) * 128], k[:, hd, :], k[:, hd, :])
                    S.tt("dve", e_, e_, m4(cST), ALU.mult)
                    S.tt("dve", e_, psKK.rr("p (h t) -> p h t", h=4), e_, ALU.mult)
                    P = Pp[sl].get()
                    S.tt("dve", P, e_, bc4(w[:, 128 + qd * 4:132 + qd * 4]), ALU.mult)
                    pbN = PA.get()
                    for r in range(4):
                        S.mm(pbN[:, r * 128:(r + 1) * 128], P[:, r, :], CI)
                    PT = PTp[sl].get()
                    S.copy("act", PT, pbN.rr("p (h t) -> p h t", h=4))
                    TT = TTp[sl].get()
                    S.tt("dve", TT, PT, m4(C_ID), ALU.add)
                    return dict(qd=qd, sl=sl, H=H, at=at, P=P, PT=PT, TT=TT)

                def g_mm1(t_, lev):
                    P, PT = t_["P"], t_["PT"]
                    bP = PA.get()
                    for r in range(4):
                        S.mm(bP[:, r * 128:(r + 1) * 128], PT[:, r, :], P[:, r, :])
                    t_["bP"] = bP
                    if lev < 5:
                        bPT = PA.get()
                        for r in range(4):
                            S.mm(bPT[:, r * 128:(r + 1) * 128], P[:, r, :], PT[:, r, :])
                        t_["bPT"] = bPT

                def g_ev1_mm2(t_, lev):
                    sl = t_["sl"]
                    Pn = Pp[sl].get()
                    S.copy("act", Pn, t_["bP"].rr("p (h t) -> p h t", h=4))
                    if lev < 5:
                        PTn = PTp[sl].get()
                        S.copy("dve", PTn, t_["bPT"].rr("p (h t) -> p h t", h=4))
                        t_["PT"] = PTn
                    t_["P"] = Pn
                    bT = PA.get()
                    for r in range(4):
                        S.mm(bT[:, r * 128:(r + 1) * 128], Pn[:, r, :], t_["TT"][:, r, :])
                    t_["bT"] = bT

                def g_ev2(t_):
                    TTn = TTp[t_["sl"]].get()
                    S.tt("dve", TTn, t_["TT"], t_["bT"].rr("p (h t) -> p h t", h=4), ALU.add)
                    t_["TT"] = TTn

                def g_stage2(t_):
                    qd, H, at, TT = t_["qd"], t_["H"], t_["at"], t_["TT"]
                    hs = slice(qd * 4, qd * 4 + 4)
                    TTu = TTup.get()
                    S.copy("act", TTu, TT)
                    pbK = PB.get()
                    for r, hd in enumerate(H):
                        S.tr(pbK[:, r * 128:(r + 1) * 128], k[:, hd, :], CB(C_ID))
                    for r, hd in enumerate(H):
                        S.tr(pbK[:, 512 + r * 128:512 + (r + 1) * 128], v[:, hd, :], CB(C_ID))
                    ktm = ktmp.get()
                    vtm = vtmp.get()
                    S.copy("act", ktm, pbK[:, 0:512].rr("p (h t) -> p h t", h=4))
                    S.copy("dve", vtm, pbK[:, 512:1024].rr("p (h t) -> p h t", h=4))
                    kbe = kbep.get()
                    kwe = kwep.get()
                    vbe = vbep.get()
                    S.tt("dve", kbe, ktm, bc4(w[:, 112 + qd * 4:116 + qd * 4]), ALU.mult)
                    S.tt("dve", kwe, ktm, bc4(w[:, 96 + qd * 4:100 + qd * 4]), ALU.mult)
                    S.tt("dve", vbe, vtm, bc4(w[:, qd * 4:qd * 4 + 4]), ALU.mult)
                    bW = PA.get()
                    for r in range(4):
                        S.mm(bW[:, r * 128:(r + 1) * 128], kbe[:, r, :], TTu[:, r, :])
                    nwm = nwmp.get()
                    S.act(nwm.rr("p h t -> p (h t)"), bW, AF.Copy, scale=-1.0)
                    bU = PA.get()
                    for r, hd in enumerate(H):
                        S.mm(bU[:, r * 128:(r + 1) * 128], TTu[:, r, :], vbe[:, r, :], start=True, stop=False)
                        S.mm(bU[:, r * 128:(r + 1) * 128], nwm[:, r, :], Sb[:, hd, :], start=False, stop=True)
                    ub = ubp.get()
                    S.copy("dve", ub, bU.rr("p (h t) -> p h t", h=4))
                    bO1 = PA.get()
                    for r in range(4):
                        S.mm(bO1[:, r * 128:(r + 1) * 128], at[:, r, :], ub[:, r, :])
                    bO2 = PA.get()
                    for r, hd in enumerate(H):
                        S.mm(bO2[:, r * 128:(r + 1) * 128], q[:, hd, :], Sb[:, hd, :])
                    y1 = y1p.get()
                    S.tt("dve", y1, bO2.rr("p (h t) -> p h t", h=4), bc4(w[:, 64 + qd * 4:68 + qd * 4]), ALU.mult)
                    ycols = yo[:, qd * 512:(qd + 1) * 512]
                    if d == 0:
                        S.tt("dve", ycols, y1.rr("p h t -> p (h t)"), bO1, ALU.add)
                    else:
                        S.tt("dve", y1.rr("p h t -> p (h t)"), y1.rr("p h t -> p (h t)"), bO1, ALU.add)
                        S.tt("dve", ycols, y1.rr("p h t -> p (h t)"), yf[:, qd * 512:(qd + 1) * 512], ALU.add)
                    bS = PA.get()
                    for r in range(4):
                        S.mm(bS[:, r * 128:(r + 1) * 128], kwe[:, r, :], ub[:, r, :])
                    S.tt("dve", St[:, hs, :], St[:, hs, :], bc4(w[:, 80 + qd * 4:84 + qd * 4]), ALU.mult)
                    S.tt("dve", St[:, hs, :], St[:, hs, :], bS.rr("p (h t) -> p h t", h=4), ALU.add)
                    S.copy("act", Sb[:, hs, :], St[:, hs, :])

                for pair in ((0,), (1,), (2,), (3,)):
                    sts = [g_stage1(qd, sl) for sl, qd in enumerate(pair)]
                    for lev in range(6):
                        for t_ in sts:
                            g_mm1(t_, lev)
                        for t_ in sts:
                            g_ev1_mm2(t_, lev)
                        for t_ in sts:
                            g_ev2(t_)
                    for t_ in sts:
                        g_stage2(t_)
                if d == 0:
                    S.dma("act", YF[tok, 0:2048], yo)
                else:
                    z_ = zdp.get()
                    S.dma("sp", z_, ZT[tok, 4096:6144])
                    mb = mbf.get()
                    _group_rms(ctx, yo, 16, 128, scr, st16, gn, mb)
                    S.act(z_, z_, AF.Silu)
                    S.tt("dve", mb, mb, z_, ALU.mult)
                    _transpose_store(ctx, mb, 2048, c, mtp)


def _bc(v, n=128):
    v = np.asarray(v, np.float32).reshape(-1)
    return np.ascontiguousarray(np.broadcast_to(v[None, :], (n, v.size)))


def _conv_pack(w, b):
    C = w.shape[1]
    o = np.concatenate([w.T, b[:, None]], axis=1).astype(np.float32)
    return np.ascontiguousarray(o.reshape(C // 128, 128, 6))


def _na_bias(rpb):
    p = np.arange(128)[:, None]
    f = np.arange(128)[None, :]
    kr, kc = p // 64, p % 64
    qr, qc = f // 64, f % 64
    c0 = np.clip(qc - 8, 0, 48)
    col_ok = (kc >= c0) & (kc < c0 + 16)
    dc = np.clip(kc - qc + 15, 0, 30)
    out = np.full((16, 128, 7, 128), NEG, np.float32)
    for di, dl in enumerate(range(-3, 4)):
        dr = 2 * dl + kr - qr
        ok = col_ok & (np.abs(dr) <= 7)
        dri = np.clip(dr + 7, 0, 14)
        vals = rpb[:, dri, dc]
        out[:, :, di, :] = np.where(ok[None], vals, NEG)
    return np.ascontiguousarray(out.reshape(16, 128, 7 * 128))


def _na_mask(NT, T):
    j0 = 1
    rows = T // 64
    jend = 1 + T // 128
    p = np.arange(128)[:, None]
    f = np.arange(128)[None, :]
    out = np.full((NT, 128, 7, 128), NEG, np.float32)
    for qi in range(j0, jend):
        q_row = 2 * (qi - j0) + f // 64
        r0 = np.clip(q_row - 4, 0, rows - 8)
        for di, dl in enumerate(range(-3, 4)):
            ki = qi + dl
            if ki < j0 or ki >= jend:
                continue
            k_row = 2 * (ki - j0) + p // 64
            ok = (k_row >= r0) & (k_row < r0 + 8)
            out[qi, :, di, :] = np.where(ok, 0.0, NEG)
    return np.ascontiguousarray(out.reshape(NT, 128, 7 * 128))


def prep_core_inputs(inp, seqs, NT, DEPTH):
    Lp = NT * 128
    n_even = (DEPTH + 1) // 2
    n_odd = DEPTH // 2
    f = lambda a: np.ascontiguousarray(np.asarray(a, np.float32))
    shared = {"consts": make_consts()}
    shared["normw"] = f(np.stack([np.asarray(inp["norm_w"][l]).reshape(16, 128).T for l in range(DEPTH)]))
    if n_even:
        shared["ev_w_in"] = f(inp["ev_w_in"][:n_even])
        shared["ev_w_out"] = f(inp["ev_w_out"][:n_even])
        shared["ev_cab"] = f(np.stack([_conv_pack(np.asarray(inp["ev_conv_a_w"][i]), np.asarray(inp["ev_conv_a_b"][i])) for i in range(n_even)]))
        shared["ev_cbb"] = f(np.stack([_conv_pack(np.asarray(inp["ev_conv_b_w"][i]), np.asarray(inp["ev_conv_b_b"][i])) for i in range(n_even)]))
        shared["ev_gbias"] = f(np.stack([_bc(np.concatenate([np.asarray(inp["ev_ig_b"][i]).ravel(), np.asarray(inp["ev_fg_b"][i]).ravel()])) for i in range(n_even)]))
        shared["ev_hnorm"] = f(np.stack([_bc(inp["ev_hnorm_a"][i]) for i in range(n_even)]))
        shared["ev_dtb"] = f(np.stack([_bc(inp["ev_dt_bias"][i]) for i in range(n_even)]))
        shared["ev_alog"] = f(np.stack([_bc(inp["ev_a_log"][i]) for i in range(n_even)]))
        shared["ev_dskip"] = f(np.stack([_bc(inp["ev_d_skip"][i]) for i in range(n_even)]))
        shared["ev_gnorm"] = f(np.stack([_bc(inp["ev_gnorm_b"][i]) for i in range(n_even)]))
    if n_odd:
        shared["od_w_in"] = f(inp["od_w_in"][:n_odd])
        shared["od_w_out"] = f(inp["od_w_out"][:n_odd])
        shared["od_qkn"] = f(np.stack([np.stack([np.asarray(inp["od_qn_w"][i]).reshape(128, 1), np.asarray(inp["od_kn_w"][i]).reshape(128, 1)], axis=0) for i in range(n_odd)]))
        shared["od_bias"] = f(np.stack([_na_bias(np.asarray(inp["od_rpb"][i])) for i in range(n_odd)]))
        shared["od_cdb"] = f(np.stack([_conv_pack(np.asarray(inp["od_conv_d_w"][i]), np.asarray(inp["od_conv_d_b"][i])) for i in range(n_odd)]))
        shared["od_dtb"] = f(np.stack([_bc(inp["od_dt_bias"][i]) for i in range(n_odd)]))
        shared["od_alog"] = f(np.stack([_bc(inp["od_a_log"][i]) for i in range(n_odd)]))
        shared["od_gnorm"] = f(np.stack([_bc(np.tile(np.asarray(inp["od_gnorm_d"][i]), 16)) for i in range(n_odd)]))
        shared["od_mbias"] = np.concatenate([np.full((16, 1), NEG, np.float32), np.zeros((16, 1), np.float32)])
    meta = np.asarray(inp["meta"], np.float32)
    maps = []
    mask_cache = {}
    for x in seqs:
        T = x.shape[0]
        h0 = np.zeros((Lp, D), np.float32)
        h0[128 - N_META:128] = meta
        h0[128:128 + T] = x
        tok = np.arange(Lp).reshape(NT, 128).T
        real = (tok >= 128 - N_META) & (tok < 128 + T)
        m = dict(shared)
        m["h0"] = h0
        m["tmask"] = np.ascontiguousarray(np.where(real, 1.0, 0.0).astype(np.float32))
        m["negm"] = np.ascontiguousarray(np.where(real, 0.0, NEG).astype(np.float32))
        if n_odd:
            if T not in mask_cache:
                mask_cache[T] = _na_mask(NT, T)
            m["od_mask"] = mask_cache[T]
        maps.append(m)
    return maps


_PROG_CACHE = {}


def run_model(inp, seqs, NT, DEPTH, dbg=None):
    key = (NT, DEPTH, tuple(dbg) if dbg else None)
    if key not in _PROG_CACHE:
        _PROG_CACHE[key] = build_program(NT, DEPTH, dbg)[0]
    nc = _PROG_CACHE[key]
    maps = prep_core_inputs(inp, seqs, NT, DEPTH)
    res = run_bass_kernel_spmd(nc, maps, core_ids=list(range(len(maps))))
    return res.results


def kernel(**inputs):
    xp = np.asarray(inputs["x_prompt"], np.float32)
    xs = np.asarray(inputs["x_sample"], np.float32)
    Ts = xs.shape[1]
    NT = (Ts + 128) // 128
    seqs = [xs[0], xs[1], xp[0], xp[1], xs[0], xs[1], xp[0], xp[1]]
    res = run_model(inputs, seqs, NT, 4)
    Lp = NT * 128
    ys = np.stack([res[0]["y"][128:128 + Ts], res[1]["y"][128:128 + Ts]]).astype(np.float32)
    Tp = xp.shape[1]
    yp = np.stack([res[2]["y"][128:128 + Tp], res[3]["y"][128:128 + Tp]]).astype(np.float32)
    return (yp, ys)
```
